# Optimizing a Trainium2 kernel written in Bass

```python
import math
import jax, jax.numpy as jnp
from jax import lax
import numpy as np

D_MODEL = 2048
BATCH = 1
SEQ = 8192
DEPTH = 4

HEAD_DIM = 128
N_HEADS_A = 8
N_HEADS_B = 8
Q_BLOCK = 128
MOBA_BLOCK = 256
MOBA_TOPK = 3
MOBA_Q_CHUNK = 64
N_BUCKETS = 32
MAX_DISTANCE = 128
MLA_HEADS = 16
MLA_Q_RANK = 512
MLA_KV_RANK = 512
MLA_NOPE = 128
MLA_ROPE = 64
MLA_V = 128
ROPE_THETA = 10000.0
D_FF = -(-8 * D_MODEL // (3 * 256)) * 256
DEEPNORM_ALPHA = (2 * DEPTH) ** 0.25
DEEPNORM_BETA = (8 * DEPTH) ** -0.25
N_EVEN = (DEPTH + 1) // 2
N_ODD = DEPTH // 2

DA = N_HEADS_A * HEAD_DIM
DB = N_HEADS_B * HEAD_DIM
AB_SPLITS = (DA, 2 * DA, 3 * DA, 3 * DA + N_HEADS_A,
             3 * DA + N_HEADS_A + DB, 3 * DA + N_HEADS_A + 2 * DB)
AB_IN = 3 * DA + N_HEADS_A + 3 * DB
AB_OUT = DA + DB
MLA_IN = MLA_Q_RANK + MLA_KV_RANK + MLA_ROPE

kernel_name = "fox_moba_mla_deepnorm_hybrid"


def layer_norm(x, g, b, eps=1e-5):
    xf = x.astype(jnp.float32)
    mu = jnp.mean(xf, axis=-1, keepdims=True)
    var = jnp.mean(jnp.square(xf - mu), axis=-1, keepdims=True)
    return ((xf - mu) * lax.rsqrt(var + eps)).astype(x.dtype) * g + b


def rms_norm(x, g, eps=1e-6):
    xf = x.astype(jnp.float32)
    return (xf * lax.rsqrt(jnp.mean(jnp.square(xf), axis=-1, keepdims=True) + eps)).astype(x.dtype) * g


def t5_bucket(rel):
    n = jnp.maximum(rel, 0)
    max_exact = N_BUCKETS // 2
    nf = jnp.maximum(n, 1).astype(jnp.float32)
    large = max_exact + (jnp.log(nf / max_exact) / math.log(MAX_DISTANCE / max_exact)
                         * (N_BUCKETS - max_exact)).astype(jnp.int32)
    large = jnp.minimum(large, N_BUCKETS - 1)
    return jnp.where(n < max_exact, n, large)


def rope_tables(S, dtype):
    inv = ROPE_THETA ** (-jnp.arange(0, MLA_ROPE, 2, dtype=jnp.float32) / MLA_ROPE)
    ang = jnp.arange(S, dtype=jnp.float32)[:, None] * inv[None, :]
    return jnp.cos(ang).astype(dtype), jnp.sin(ang).astype(dtype)


def apply_rope(x, cos, sin):
    x1, x2 = jnp.split(x, 2, axis=-1)
    c = cos[None, :, None, :]
    s = sin[None, :, None, :]
    return jnp.concatenate([x1 * c - x2 * s, x1 * s + x2 * c], axis=-1)


def causal_attention(q, k, v, log_decay=None):
    B, S, H, Dq = q.shape
    Dv = v.shape[-1]
    scale = Dq ** -0.5
    kpos = jnp.arange(S)
    cT = None if log_decay is None else jnp.cumsum(log_decay, axis=1).transpose(0, 2, 1)

    def block(i):
        start = i * Q_BLOCK
        qb = lax.dynamic_slice_in_dim(q, start, Q_BLOCK, axis=1)
        qpos = start + jnp.arange(Q_BLOCK)
        s = jnp.einsum('bqhd,bkhd->bhqk', qb, k, preferred_element_type=jnp.float32) * scale
        if cT is not None:
            cq = lax.dynamic_slice_in_dim(cT, start, Q_BLOCK, axis=2)
            s = s + (cq[..., :, None] - cT[..., None, :])
        s = jnp.where(qpos[:, None] >= kpos[None, :], s, -jnp.inf)
        p = jax.nn.softmax(s, axis=-1).astype(v.dtype)
        return jnp.einsum('bhqk,bkhd->bqhd', p, v)

    out = lax.map(block, jnp.arange(S // Q_BLOCK))
    return out.transpose(1, 0, 2, 3, 4).reshape(B, S, H, Dv)


def moba_attention(q, k, v, rel_bias):
    B, S, H, D = q.shape
    nb = -(-S // MOBA_BLOCK)
    pad = nb * MOBA_BLOCK - S
    kp = jnp.pad(k, ((0, 0), (0, pad), (0, 0), (0, 0)))
    vp = jnp.pad(v, ((0, 0), (0, pad), (0, 0), (0, 0)))
    kbt = kp.reshape(B, nb, MOBA_BLOCK, H, D).transpose(0, 3, 1, 2, 4)
    vbt = vp.reshape(B, nb, MOBA_BLOCK, H, D).transpose(0, 3, 1, 2, 4)
    k_mean = jnp.mean(kbt.astype(jnp.float32), axis=3)
    topk = min(MOBA_TOPK, nb)
    scale = D ** -0.5
    offs = jnp.arange(MOBA_BLOCK)
    bias_t = rel_bias.T.astype(jnp.float32)
    bi = jnp.arange(B)[:, None, None, None]
    hi = jnp.arange(H)[None, :, None, None]
    hi5 = jnp.arange(H)[None, :, None, None, None]

    def chunk(i):
        start = i * MOBA_Q_CHUNK
        qc = lax.dynamic_slice_in_dim(q, start, MOBA_Q_CHUNK, axis=1)
        qpos = start + jnp.arange(MOBA_Q_CHUNK)
        own = start // MOBA_BLOCK
        g = jnp.einsum('bqhd,bhnd->bhqn', qc.astype(jnp.float32), k_mean)
        g = jnp.where(jnp.arange(nb) < own, g, -jnp.inf)
        _, sel = lax.top_k(g, topk)
        valid = sel < own
        k_sel = kbt[bi, hi, sel]
        v_sel = vbt[bi, hi, sel]
        s_sel = jnp.einsum('bqhd,bhqnkd->bhqnk', qc, k_sel,
                           preferred_element_type=jnp.float32) * scale
        kpos_sel = sel[..., None] * MOBA_BLOCK + offs
        rel_sel = qpos[None, None, :, None, None] - kpos_sel
        s_sel = s_sel + bias_t[hi5, t5_bucket(rel_sel)]
        s_sel = jnp.where(valid[..., None], s_sel, -jnp.inf)
        k_own = lax.dynamic_slice_in_dim(kp, own * MOBA_BLOCK, MOBA_BLOCK, axis=1)
        v_own = lax.dynamic_slice_in_dim(vp, own * MOBA_BLOCK, MOBA_BLOCK, axis=1)
        rel_own = qpos[:, None] - (own * MOBA_BLOCK + offs)[None, :]
        s_own = jnp.einsum('bqhd,bkhd->bhqk', qc, k_own,
                           preferred_element_type=jnp.float32) * scale
        s_own = s_own + bias_t[:, t5_bucket(rel_own)][None]
        s_own = jnp.where(rel_own >= 0, s_own, -jnp.inf)
        s_all = jnp.concatenate(
            [s_sel.reshape(B, H, MOBA_Q_CHUNK, topk * MOBA_BLOCK), s_own], axis=-1)
        p = jax.nn.softmax(s_all, axis=-1).astype(v.dtype)
        p_sel = p[..., :topk * MOBA_BLOCK].reshape(B, H, MOBA_Q_CHUNK, topk, MOBA_BLOCK)
        p_own = p[..., topk * MOBA_BLOCK:]
        return (jnp.einsum('bhqnk,bhqnkd->bqhd', p_sel, v_sel)
                + jnp.einsum('bhqk,bkhd->bqhd', p_own, v_own))

    out = lax.map(chunk, jnp.arange(S // MOBA_Q_CHUNK))
    return out.transpose(1, 0, 2, 3, 4).reshape(B, S, H, D)


def fox_moba_mixer(x, w_in, b_f, w_out, rel_bias):
    B, S, _ = x.shape
    h = x @ w_in
    qa, ka, va, fa, qb, kb, vb = jnp.split(h, AB_SPLITS, axis=-1)
    shp_a = (B, S, N_HEADS_A, HEAD_DIM)
    shp_b = (B, S, N_HEADS_B, HEAD_DIM)
    log_f = jax.nn.log_sigmoid((fa + b_f).astype(jnp.float32))
    ya = causal_attention(qa.reshape(shp_a), ka.reshape(shp_a), va.reshape(shp_a), log_f)
    yb = moba_attention(qb.reshape(shp_b), kb.reshape(shp_b), vb.reshape(shp_b), rel_bias)
    y = jnp.concatenate([ya.reshape(B, S, DA), yb.reshape(B, S, DB)], axis=-1)
    return y @ w_out


def mla_mixer(x, w_in, q_norm_g, kv_norm_g, w_uq, w_ukv, w_out, cos, sin):
    B, S, _ = x.shape
    h = x @ w_in
    cq, ckv, k_rope = jnp.split(h, (MLA_Q_RANK, MLA_Q_RANK + MLA_KV_RANK), axis=-1)
    cq = rms_norm(cq, q_norm_g)
    ckv = rms_norm(ckv, kv_norm_g)
    q = (cq @ w_uq).reshape(B, S, MLA_HEADS, MLA_NOPE + MLA_ROPE)
    kv = (ckv @ w_ukv).reshape(B, S, MLA_HEADS, MLA_NOPE + MLA_V)
    q_nope, q_rope = jnp.split(q, (MLA_NOPE,), axis=-1)
    k_nope, v = jnp.split(kv, (MLA_NOPE,), axis=-1)
    q_rope = apply_rope(q_rope, cos, sin)
    k_rope = apply_rope(k_rope[:, :, None, :], cos, sin)
    q = jnp.concatenate([q_nope, q_rope], axis=-1)
    k = jnp.concatenate(
        [k_nope, jnp.broadcast_to(k_rope, (B, S, MLA_HEADS, MLA_ROPE))], axis=-1)
    y = causal_attention(q, k, v)
    return y.reshape(B, S, MLA_HEADS * MLA_V) @ w_out


def swiglu(x, w_gate, w_up, w_down):
    return (jax.nn.silu(x @ w_gate) * (x @ w_up)) @ w_down


def setup_inputs(seed: int = 0) -> dict:
    key = jax.random.key(seed)
    ks = jax.random.split(key, 16)
    f32 = jnp.float32

    def w(k, shape, fan_in, gain=1.0):
        return jax.random.normal(k, shape, f32) * (gain * fan_in ** -0.5)

    x = jax.random.normal(ks[0], (BATCH, SEQ, D_MODEL), f32)
    ab_w_in = w(ks[1], (N_EVEN, D_MODEL, AB_IN), D_MODEL)
    ab_forget_bias = 3.0 + 0.1 * jax.random.normal(ks[2], (N_EVEN, N_HEADS_A), f32)
    ab_w_out = w(ks[3], (N_EVEN, AB_OUT, D_MODEL), AB_OUT, DEEPNORM_BETA)
    rel_bias = 0.2 * jax.random.normal(ks[4], (N_BUCKETS, N_HEADS_B), f32)
    mla_w_in = w(ks[5], (N_ODD, D_MODEL, MLA_IN), D_MODEL)
    mla_q_norm = 1.0 + 0.02 * jax.random.normal(ks[6], (N_ODD, MLA_Q_RANK), f32)
    mla_kv_norm = 1.0 + 0.02 * jax.random.normal(ks[7], (N_ODD, MLA_KV_RANK), f32)
    mla_w_uq = w(ks[8], (N_ODD, MLA_Q_RANK, MLA_HEADS * (MLA_NOPE + MLA_ROPE)), MLA_Q_RANK)
    mla_w_ukv = w(ks[9], (N_ODD, MLA_KV_RANK, MLA_HEADS * (MLA_NOPE + MLA_V)), MLA_KV_RANK)
    mla_w_out = w(ks[10], (N_ODD, MLA_HEADS * MLA_V, D_MODEL), MLA_HEADS * MLA_V, DEEPNORM_BETA)
    ffn_w_gate = w(ks[11], (DEPTH, D_MODEL, D_FF), D_MODEL)
    ffn_w_up = w(ks[12], (DEPTH, D_MODEL, D_FF), D_MODEL)
    ffn_w_down = w(ks[13], (DEPTH, D_FF, D_MODEL), D_FF, DEEPNORM_BETA)
    ln_g = 1.0 + 0.02 * jax.random.normal(ks[14], (DEPTH, 2, D_MODEL), f32)
    ln_b = 0.02 * jax.random.normal(ks[15], (DEPTH, 2, D_MODEL), f32)
    return {"x": x, "ab_w_in": ab_w_in, "ab_forget_bias": ab_forget_bias,
            "ab_w_out": ab_w_out, "rel_bias": rel_bias, "mla_w_in": mla_w_in,
            "mla_q_norm": mla_q_norm, "mla_kv_norm": mla_kv_norm, "mla_w_uq": mla_w_uq,
            "mla_w_ukv": mla_w_ukv, "mla_w_out": mla_w_out, "ffn_w_gate": ffn_w_gate,
            "ffn_w_up": ffn_w_up, "ffn_w_down": ffn_w_down, "ln_g": ln_g, "ln_b": ln_b}


def reference(x, ab_w_in, ab_forget_bias, ab_w_out, rel_bias, mla_w_in, mla_q_norm,
              mla_kv_norm, mla_w_uq, mla_w_ukv, mla_w_out, ffn_w_gate, ffn_w_up,
              ffn_w_down, ln_g, ln_b):
    S = x.shape[1]
    cos, sin = rope_tables(S, x.dtype)
    for layer in range(DEPTH):
        j = layer // 2
        if layer % 2 == 0:
            y = fox_moba_mixer(x, ab_w_in[j], ab_forget_bias[j], ab_w_out[j], rel_bias)
        else:
            y = mla_mixer(x, mla_w_in[j], mla_q_norm[j], mla_kv_norm[j], mla_w_uq[j],
                          mla_w_ukv[j], mla_w_out[j], cos, sin)
        x = layer_norm(DEEPNORM_ALPHA * x + y, ln_g[layer, 0], ln_b[layer, 0])
        y = swiglu(x, ffn_w_gate[layer], ffn_w_up[layer], ffn_w_down[layer])
        x = layer_norm(DEEPNORM_ALPHA * x + y, ln_g[layer, 1], ln_b[layer, 1])
    return x
```

```python
import math
import numpy as np
import ml_dtypes
import concourse.bass as bass
import concourse.mybir as mybir
from concourse.bass_utils import run_bass_kernel_spmd

F32 = mybir.dt.float32
BF16 = mybir.dt.bfloat16
AF = mybir.ActivationFunctionType
ALU = mybir.AluOpType
AX = mybir.AxisListType

NCORES = 8
D = 2048
S = 8192
DEPTH = 4
TOK = S // NCORES
DFF = 5632
NFC = DFF // 128
KC = D // 128
ALPHA = (2 * DEPTH) ** 0.25
HD = 128
MLA_IN = 1088
NEG = -30000.0

EPOCH = 20000
SAME_ENGINE_SYNC = True
COMPUTE = ("pe", "act", "dve", "pool")


class Res:
    __slots__ = ("w", "rs", "name")

    def __init__(self, name=""):
        self.w = None
        self.rs = {}
        self.name = name


class Prog:
    def __init__(self, nc, same_engine_sync=None):
        if same_engine_sync is None:
            same_engine_sync = SAME_ENGINE_SYNC
        self.nc = nc
        self.ops = {e: [] for e in ("pe", "act", "dve", "pool", "sp")}
        self.same_engine_sync = same_engine_sync
        self.dma_sems = []
        self._ctx = []

    def enter(self, cm):
        v = cm.__enter__()
        self._ctx.append(cm)
        return v

    def sbuf(self, name, shape, dtype):
        return self.enter(self.nc.sbuf_tensor("sb_" + name, list(shape), dtype))

    def psum(self, name, shape, dtype=F32):
        return self.enter(self.nc.psum_tensor("ps_" + name, list(shape), dtype))

    def new_dma_sem(self, name, exclusive=False):
        if not exclusive:
            return None
        h = self.enter(self.nc.semaphore(name))
        self.dma_sems.append([h, 0])
        return len(self.dma_sems) - 1

    NRING = 24

    def _ring_sem(self, queue):
        if not hasattr(self, "_rings"):
            self._rings = {}
        if queue not in self._rings:
            n = self.NRING if queue == "sp" else 8
            idx = []
            for i in range(n):
                h = self.enter(self.nc.semaphore(f"ring_{queue}{i}"))
                self.dma_sems.append([h, 0])
                idx.append(len(self.dma_sems) - 1)
            self._rings[queue] = [idx, [None] * n, 0]
        rg = self._rings[queue]
        i = rg[2] % len(rg[0])
        rg[2] += 1
        return rg, i

    def _deps(self, eng, reads, writes):
        deps = set()
        for r in reads:
            if r.w is not None:
                deps.add(r.w)
        for w in writes:
            if w.w is not None:
                deps.add(w.w)
            deps.update(w.rs.values())
        out = []
        for d in deps:
            if d[0] == "E":
                if d[1] == eng and (eng == "pe" or not self.same_engine_sync):
                    continue
                self.ops[d[1]][d[2]][2] = True
            out.append(d)
        return out

    def op(self, eng, fn, reads=(), writes=()):
        deps = self._deps(eng, reads, writes)
        idx = len(self.ops[eng])
        ev = ("E", eng, idx)
        self.ops[eng].append([fn, deps, False, None])
        for r in reads:
            r.rs[eng] = ev
        for w in writes:
            w.w = ev
            w.rs = {}
        return ev

    def dma(self, queue, fn, sem, reads=(), writes=()):
        deps = self._deps(queue, reads, writes)
        if sem is None:
            rg, ri = self._ring_sem(queue)
            sem = rg[0][ri]
            if rg[1][ri] is not None:
                deps.append(rg[1][ri])
            rg[1][ri] = ("S", sem, self.dma_sems[sem][1] + 16)
        self.dma_sems[sem][1] += 16
        ev = ("S", sem, self.dma_sems[sem][1])
        self.ops[queue].append([fn, deps, False, sem])
        for r in reads:
            r.rs[("S", sem)] = ev
        for w in writes:
            w.w = ev
            w.rs = {}
        return ev

    def emit(self, final_events):
        nc = self.nc
        sigval = {}
        esems = {}
        for e in COMPUTE + ("sp",):
            c = 0
            vals = []
            for o in self.ops[e]:
                if o[2] and o[3] is None:
                    c += 1
                vals.append(c)
            sigval[e] = vals
            n_ep = max((c + EPOCH - 1) // EPOCH, 1)
            esems[e] = [self.enter(nc.semaphore(f"s_{e}_{k}")) for k in range(n_ep)]

        def resolve(d):
            if d[0] == "S":
                return ("S", d[1]), self.dma_sems[d[1]][0], d[2]
            v = sigval[d[1]][d[2]]
            ep = (v - 1) // EPOCH
            return ("E", d[1], ep), esems[d[1]][ep], v - ep * EPOCH

        engmap = {"pe": "tensor", "act": "scalar", "dve": "vector", "pool": "gpsimd", "sp": "sync"}
        prog = self

        def body_for(e):
            def body(engine):
                waited = {}
                for i, (fn, deps, sig, dsem) in enumerate(prog.ops[e]):
                    for d in deps:
                        key, sem, val = resolve(d)
                        if waited.get(key, 0) < val:
                            engine.wait_ge(sem, val)
                            waited[key] = val
                    inst = fn(engine)
                    if dsem is not None:
                        inst.then_inc(prog.dma_sems[dsem][0], 16)
                    elif sig:
                        v = sigval[e][i]
                        ep = (v - 1) // EPOCH
                        inst.then_inc(esems[e][ep], 1)
                for d in final_events.get(e, ()):
                    key, sem, val = resolve(d)
                    engine.wait_ge(sem, val)
            return body

        with nc.Block() as block:
            for e in COMPUTE + ("sp",):
                if not self.ops[e] and not final_events.get(e):
                    continue
                getattr(block, engmap[e])(body_for(e))

    def close(self):
        while self._ctx:
            self._ctx.pop().__exit__(None, None, None)


class Slots:
    def __init__(self, p, name, n, nbytes):
        self.p = p
        self.n = n
        self.t = [p.sbuf(f"{name}{i}", [128, nbytes // 2], BF16) for i in range(n)]
        self.r = [Res(f"{name}{i}") for i in range(n)]
        self.sem = [p.new_dma_sem(f"{name}s{i}", exclusive=True) for i in range(n)]
        self.k = 0

    def load(self, shape, src_ap, queue="pool"):
        i = self.k % self.n
        self.k += 1
        a, b = shape
        view = self.t[i][:, 0:a * b].rearrange("p (a b) -> p a b", a=a)
        self.p.dma(queue, lambda q, v=view, s=src_ap: q.dma_start(out=v, in_=s), self.sem[i],
                   writes=[self.r[i]])
        return view, self.r[i]


def build_T(mode_next, has_mixer=True):
    nc = bass.Bass("TRN2", target_bir_lowering=False)
    dt = nc.dram_tensor
    yT_d = dt("yT", [D, TOK], BF16, kind="ExternalInput").ap()
    xres_d = dt("xres", [D, TOK], F32, kind="ExternalInput").ap()
    wout_d = dt("wout", [D, D], F32, kind="ExternalInput").ap()
    lnp_d = dt("lnp", [128, 4 * KC], F32, kind="ExternalInput").ap()
    wg_d = dt("wg", [D, DFF], F32, kind="ExternalInput").ap()
    wu_d = dt("wu", [D, DFF], F32, kind="ExternalInput").ap()
    wd_d = dt("wd", [DFF, D], F32, kind="ExternalInput").ap()
    xout_d = dt("xout", [D, TOK], F32, kind="ExternalOutput").ap()
    if mode_next == "even":
        xTb_d = dt("xTb", [D, TOK], BF16, kind="ExternalOutput").ap()
    if mode_next == "mla":
        win_d = dt("win", [D, MLA_IN], F32, kind="ExternalInput").ap()
        ng_d = dt("ng", [128, 8], F32, kind="ExternalInput").ap()
        cs_d = dt("cs", [64, 2 * TOK], F32, kind="ExternalInput").ap()
        lat_d = dt("lat", [MLA_IN, TOK], BF16, kind="ExternalOutput").ap()

    p = Prog(nc)
    z = p.sbuf("z", [128, KC, TOK], F32)
    xb = p.sbuf("xb", [128, KC, TOK], BF16)
    hg = [p.sbuf(f"hg{i}", [128, 4, TOK], BF16) for i in range(2)]
    ws = Slots(p, "ws", 4, 16384)
    lnp = p.sbuf("lnp", [128, 4 * KC], F32)
    ones = p.sbuf("ones", [128, 128], F32)
    mean = p.sbuf("mean", [128, TOK], F32)
    rstd = p.sbuf("rstd", [128, TOK], F32)
    tmpf = [p.sbuf(f"tmpf{i}", [128, 512], F32) for i in range(4)]
    P = [p.psum(f"P{i}", [128, 512], F32) for i in range(8)]
    rP = [Res(f"P{i}") for i in range(8)]
    rz = [[Res(f"z{m}_{h}") for h in range(2)] for m in range(KC)]
    rxb = [[Res(f"xb{m}_{h}") for h in range(2)] for m in range(KC)]
    rhg = [[[Res(f"hg{i}_{f}_{h}") for h in range(2)] for f in range(4)] for i in range(2)]
    rtmp = [Res(f"tmp{i}") for i in range(4)]
    rlnp, rones, rmean, rrstd = Res("lnp"), Res("ones"), [Res("mean0"), Res("mean1")], [Res("rstd0"), Res("rstd1")]
    s_in = p.new_dma_sem("s_in")
    s_in2 = p.new_dma_sem("s_in2")
    s_out = p.new_dma_sem("s_out")
    H = [slice(0, 512), slice(512, 1024)]

    p.dma("sp", lambda q: q.dma_start(out=lnp[:], in_=lnp_d[:, :]), s_in2, writes=[rlnp])
    p.op("dve", lambda v: v.memset(ones[:], 1.0 / D), writes=[rones])
    yv = yT_d.rearrange("(kc p) t -> p kc t", p=128)
    xv = xres_d.rearrange("(kc p) t -> p kc t", p=128)
    for m in range(KC):
        if has_mixer:
            p.dma("sp", lambda q, m=m: q.dma_start(out=xb[:, m, :], in_=yv[:, m, :]), s_in,
                  writes=[rxb[m][0], rxb[m][1]])
        p.dma("sp", lambda q, m=m: q.dma_start(out=z[:, m, :], in_=xv[:, m, :]), s_in2,
              writes=[rz[m][0], rz[m][1]])

    bank_rr = [0]

    def proj_accum(nk, lhs_fn, rhs_fn, rhs_res, w_res, bank):
        for k in range(nk):
            p.op("pe", lambda t, k=k, bank=bank: t.matmul(P[bank][:], lhsT=lhs_fn(k), rhs=rhs_fn(k),
                                                           start=(k == 0), stop=(k == nk - 1)),
                 reads=[w_res, rhs_res(k)], writes=[rP[bank]])

    if has_mixer:
        wv = wout_d.rearrange("(kc p) n -> p kc n", p=128)
        for m4 in range(4):
            slab, rs = ws.load((KC, 512), wv[:, :, m4 * 512:(m4 + 1) * 512])
            for mi in range(4):
                m = m4 * 4 + mi
                for h in range(2):
                    bank = bank_rr[0] % 4
                    bank_rr[0] += 1
                    proj_accum(KC, lambda k, slab=slab, mi=mi: slab[:, k, mi * 128:(mi + 1) * 128],
                               lambda k, h=h: xb[:, k, H[h]], lambda k, h=h: rxb[k][h], rs, bank)
                    p.op("dve", lambda v, m=m, h=h, bank=bank: v.scalar_tensor_tensor(
                        out=z[:, m, H[h]], in0=z[:, m, H[h]], scalar=ALPHA, in1=P[bank][:],
                        op0=ALU.mult, op1=ALU.add), reads=[rP[bank], rz[m][h]], writes=[rz[m][h]])

    def layer_norm(gi, bi):
        for h in range(2):
            for k in range(KC):
                p.op("pe", lambda t, k=k, h=h: t.matmul(P[4 + h][:], lhsT=ones[:], rhs=z[:, k, H[h]],
                                                         start=(k == 0), stop=(k == KC - 1)),
                     reads=[rones, rz[k][h]], writes=[rP[4 + h]])
            for k in range(KC):
                ti = k % 2
                p.op("act", lambda a, k=k, h=h, ti=ti: a.activation(out=tmpf[ti][:], in_=z[:, k, H[h]], func=AF.Square),
                     reads=[rz[k][h]], writes=[rtmp[ti]])
                p.op("pe", lambda t, k=k, h=h, ti=ti: t.matmul(P[6 + h][:], lhsT=ones[:], rhs=tmpf[ti][:],
                                                                start=(k == 0), stop=(k == KC - 1)),
                     reads=[rones, rtmp[ti]], writes=[rP[6 + h]])
            p.op("dve", lambda v, h=h: v.tensor_copy(out=mean[:, H[h]], in_=P[4 + h][:]),
                 reads=[rP[4 + h]], writes=[rmean[h]])
            p.op("dve", lambda v, h=h: v.tensor_tensor(out=tmpf[2][:], in0=mean[:, H[h]], in1=mean[:, H[h]], op=ALU.mult),
                 reads=[rmean[h]], writes=[rtmp[2]])
            p.op("dve", lambda v, h=h: v.tensor_tensor(out=tmpf[2][:], in0=P[6 + h][:], in1=tmpf[2][:], op=ALU.subtract),
                 reads=[rP[6 + h], rtmp[2]], writes=[rtmp[2]])
            p.op("dve", lambda v, h=h: v.tensor_scalar(out=tmpf[2][:], in0=tmpf[2][:], scalar1=1e-5, scalar2=None, op0=ALU.add),
                 reads=[rtmp[2]], writes=[rtmp[2]])
            p.op("act", lambda a, h=h: a.activation(out=tmpf[3][:], in_=tmpf[2][:], func=AF.Sqrt),
                 reads=[rtmp[2]], writes=[rtmp[3]])
            p.op("dve", lambda v, h=h: v.reciprocal(out=rstd[:, H[h]], in_=tmpf[3][:]),
                 reads=[rtmp[3]], writes=[rrstd[h]])
        for h in range(2):
            for k in range(KC):
                p.op("dve", lambda g, k=k, h=h: g.tensor_tensor(out=z[:, k, H[h]], in0=z[:, k, H[h]], in1=mean[:, H[h]], op=ALU.subtract),
                     reads=[rz[k][h], rmean[h]], writes=[rz[k][h]])
                p.op("dve", lambda v, k=k, h=h: v.tensor_tensor(out=z[:, k, H[h]], in0=z[:, k, H[h]], in1=rstd[:, H[h]], op=ALU.mult),
                     reads=[rz[k][h], rrstd[h]], writes=[rz[k][h]])
                p.op("act", lambda a, k=k, h=h: a.activation(out=z[:, k, H[h]], in_=z[:, k, H[h]], func=AF.Identity,
                                                              scale=lnp[:, gi * KC + k:gi * KC + k + 1],
                                                              bias=lnp[:, bi * KC + k:bi * KC + k + 1]),
                     reads=[rz[k][h], rlnp], writes=[rz[k][h]])
                p.op("act", lambda v, k=k, h=h: v.activation(out=xb[:, k, H[h]], in_=z[:, k, H[h]], func=AF.Copy),
                     reads=[rz[k][h]], writes=[rxb[k][h]])

    layer_norm(0, 1)

    gv = wg_d.rearrange("(kc p) n -> p kc n", p=128)
    uv = wu_d.rearrange("(kc p) n -> p kc n", p=128)
    dv = wd_d.rearrange("(fc p) n -> p fc n", p=128)
    NG = NFC // 4
    for g in range(NG):
        gslab, rg = ws.load((KC, 512), gv[:, :, g * 512:(g + 1) * 512])
        uslab, ru = ws.load((KC, 512), uv[:, :, g * 512:(g + 1) * 512])
        hb = hg[g % 2]
        rh = rhg[g % 2]
        for fi in range(4):
            for h in range(2):
                bg = bank_rr[0] % 4
                bu = (bank_rr[0] + 1) % 4
                bank_rr[0] += 2
                proj_accum(KC, lambda k, fi=fi: gslab[:, k, fi * 128:(fi + 1) * 128],
                           lambda k, h=h: xb[:, k, H[h]], lambda k, h=h: rxb[k][h], rg, bg)
                proj_accum(KC, lambda k, fi=fi: uslab[:, k, fi * 128:(fi + 1) * 128],
                           lambda k, h=h: xb[:, k, H[h]], lambda k, h=h: rxb[k][h], ru, bu)
                ti = (fi * 2 + h) % 2
                p.op("act", lambda a, bg=bg, ti=ti: a.activation(out=tmpf[ti][:], in_=P[bg][:], func=AF.Silu),
                     reads=[rP[bg]], writes=[rtmp[ti]])
                p.op("dve", lambda v, bu=bu, ti=ti, fi=fi, h=h, hb=hb: v.tensor_tensor(
                    out=hb[:, fi, H[h]], in0=P[bu][:], in1=tmpf[ti][:], op=ALU.mult),
                    reads=[rP[bu], rtmp[ti]], writes=[rh[fi][h]])
        for half in range(2):
            dslab, rd = ws.load((4, 1024), dv[:, g * 4:(g + 1) * 4, half * 1024:(half + 1) * 1024])
            for mi in range(8):
                m = half * 8 + mi
                for h in range(2):
                    bank = 4 + (bank_rr[0] % 4)
                    bank_rr[0] += 1
                    proj_accum(4, lambda k, mi=mi, dslab=dslab: dslab[:, k, mi * 128:(mi + 1) * 128],
                               lambda k, h=h, hb=hb: hb[:, k, H[h]], lambda k, h=h, rh=rh: rh[k][h], rd, bank)
                    if g == 0:
                        p.op("dve", lambda v, m=m, h=h, bank=bank: v.scalar_tensor_tensor(
                            out=z[:, m, H[h]], in0=z[:, m, H[h]], scalar=ALPHA, in1=P[bank][:],
                            op0=ALU.mult, op1=ALU.add), reads=[rP[bank], rz[m][h]], writes=[rz[m][h]])
                    else:
                        p.op("dve", lambda v, m=m, h=h, bank=bank: v.tensor_tensor(
                            out=z[:, m, H[h]], in0=z[:, m, H[h]], in1=P[bank][:], op=ALU.add),
                            reads=[rP[bank], rz[m][h]], writes=[rz[m][h]])

    layer_norm(2, 3)

    finals = []
    xo = xout_d.rearrange("(kc p) t -> p kc t", p=128)
    for m in range(KC):
        finals.append(p.dma("sp", lambda q, m=m: q.dma_start(out=xo[:, m, :], in_=z[:, m, :]), s_out,
                            reads=[rz[m][0], rz[m][1]]))
    if mode_next == "even":
        xt = xTb_d.rearrange("(kc p) t -> p kc t", p=128)
        for m in range(KC):
            finals.append(p.dma("sp", lambda q, m=m: q.dma_start(out=xt[:, m, :], in_=xb[:, m, :]), s_out,
                                reads=[rxb[m][0], rxb[m][1]]))
    if mode_next == "mla":
        finals += mla_latents(p, nc, ws, xb, rxb, P, rP, bank_rr, tmpf, rtmp, H,
                              win_d, ng_d, cs_d, lat_d, s_in2, s_out,
                              hg, rhg, mean, rstd, rmean, rrstd)
    p.emit({"sp": [finals[-1]] if False else finals})
    p.close()
    return nc


def mla_latents(p, nc, ws, xb, rxb, P, rP, bank_rr, tmpf, rtmp, H,
                win_d, ng_d, cs_d, lat_d, s_in, s_out, hg, rhg, mean, rstd, rmean, rrstd):
    finals = []
    ng = p.sbuf("ng", [128, 8], F32)
    rng_ = Res("ng")
    p.dma("sp", lambda q: q.dma_start(out=ng[:], in_=ng_d[:, :]), s_in, writes=[rng_])
    all_mean = [rmean[0], rmean[1]]
    all_rstd = [rrstd[0], rrstd[1]]
    p.dma("sp", lambda q: q.dma_start(out=mean[0:64, :], in_=cs_d[:, 0:TOK]), s_in, writes=all_mean)
    p.dma("sp", lambda q: q.dma_start(out=rstd[0:64, :], in_=cs_d[:, TOK:2 * TOK]), s_in, writes=all_rstd)
    lat = hg[0].bitcast(F32)
    latb = hg[1]
    rl0 = [r for f in rhg[0] for r in f]
    rl1 = [r for f in rhg[1] for r in f]
    ones512 = p.sbuf("ones512", [128, 128], F32)
    r512 = Res("ones512")
    p.op("dve", lambda v: v.memset(ones512[:], 1.0 / 512), writes=[r512])
    wv = win_d.rearrange("(kc p) n -> p kc n", p=128)
    for li in range(2):
        slab, rs = ws.load((KC, 512), wv[:, :, li * 512:(li + 1) * 512])
        for h in range(2):
            for c in range(4):
                bank = bank_rr[0] % 4
                bank_rr[0] += 1
                for k in range(KC):
                    p.op("pe", lambda t, k=k, c=c, h=h, bank=bank, slab=slab: t.matmul(
                        P[bank][:], lhsT=slab[:, k, c * 128:(c + 1) * 128], rhs=xb[:, k, H[h]],
                        start=(k == 0), stop=(k == KC - 1)), reads=[rs, rxb[k][h]], writes=[rP[bank]])
                p.op("act", lambda a, c=c, bank=bank: a.activation(out=lat[:, c, :], in_=P[bank][:], func=AF.Copy),
                     reads=[rP[bank]], writes=rl0)
            for c in range(4):
                ti = c % 2
                p.op("act", lambda a, c=c, ti=ti: a.activation(out=tmpf[ti][:], in_=lat[:, c, :], func=AF.Square),
                     reads=rl0, writes=[rtmp[ti]])
                p.op("pe", lambda t, c=c, h=h, ti=ti: t.matmul(P[6 + h][:], lhsT=ones512[:], rhs=tmpf[ti][:],
                                                                start=(c == 0), stop=(c == 3)),
                     reads=[r512, rtmp[ti]], writes=[rP[6 + h]])
            p.op("dve", lambda v, h=h: v.tensor_scalar(out=tmpf[2][:], in0=P[6 + h][:], scalar1=1e-6, scalar2=None, op0=ALU.add),
                 reads=[rP[6 + h]], writes=[rtmp[2]])
            p.op("act", lambda a: a.activation(out=tmpf[3][:], in_=tmpf[2][:], func=AF.Sqrt),
                 reads=[rtmp[2]], writes=[rtmp[3]])
            p.op("dve", lambda v: v.reciprocal(out=tmpf[2][:], in_=tmpf[3][:]), reads=[rtmp[3]], writes=[rtmp[2]])
            for c in range(4):
                p.op("dve", lambda v, c=c: v.tensor_tensor(out=lat[:, c, :], in0=lat[:, c, :], in1=tmpf[2][:], op=ALU.mult),
                     reads=rl0 + [rtmp[2]], writes=rl0)
                p.op("act", lambda a, c=c, li=li: a.activation(out=latb[:, c, 0:512], in_=lat[:, c, :], func=AF.Identity,
                                                               scale=ng[:, li * 4 + c:li * 4 + c + 1]),
                     reads=rl0 + [rng_], writes=rl1)
            lv = lat_d[li * 512:(li + 1) * 512, H[h]].rearrange("(c p) t -> p c t", p=128)
            finals.append(p.dma("sp", lambda q, lv=lv: q.dma_start(out=lv, in_=latb[:, :, 0:512]), s_out, reads=rl1))
    slab, rs = ws.load((KC, 64), wv[:, :, 1024:1088])
    wrot = p.sbuf("wrot", [128, KC, 64], BF16)
    rwrot = Res("wrot")
    p.op("dve", lambda v: v.tensor_scalar(out=wrot[:, :, 0:32], in0=slab[:, :, 32:64], scalar1=-1.0, scalar2=None, op0=ALU.mult),
         reads=[rs], writes=[rwrot])
    p.op("dve", lambda v: v.tensor_copy(out=wrot[:, :, 32:64], in_=slab[:, :, 0:32]), reads=[rs], writes=[rwrot])
    kr = p.sbuf("kr", [64, TOK], BF16)
    rkr = [Res("kr0"), Res("kr1")]
    for h in range(2):
        b0 = bank_rr[0] % 4
        b1 = (bank_rr[0] + 1) % 4
        bank_rr[0] += 2
        for k in range(KC):
            p.op("pe", lambda t, k=k, h=h, b0=b0: t.matmul(P[b0][0:64, :], lhsT=slab[:, k, :], rhs=xb[:, k, H[h]],
                                                            start=(k == 0), stop=(k == KC - 1)),
                 reads=[rs, rxb[k][h]], writes=[rP[b0]])
        for k in range(KC):
            p.op("pe", lambda t, k=k, h=h, b1=b1: t.matmul(P[b1][0:64, :], lhsT=wrot[:, k, :], rhs=xb[:, k, H[h]],
                                                            start=(k == 0), stop=(k == KC - 1)),
                 reads=[rwrot, rxb[k][h]], writes=[rP[b1]])
        p.op("dve", lambda v, h=h, b0=b0: v.tensor_tensor(out=tmpf[0][0:64, :], in0=P[b0][0:64, :], in1=mean[0:64, H[h]], op=ALU.mult),
             reads=[rP[b0]] + all_mean, writes=[rtmp[0]])
        p.op("dve", lambda v, h=h, b1=b1: v.tensor_tensor(out=tmpf[1][0:64, :], in0=P[b1][0:64, :],
                                                          in1=rstd[0:64, H[h]], op=ALU.mult),
             reads=[rP[b1]] + all_rstd, writes=[rtmp[1]])
        p.op("dve", lambda v, h=h: v.tensor_tensor(out=kr[:, H[h]], in0=tmpf[0][0:64, :], in1=tmpf[1][0:64, :], op=ALU.add),
             reads=[rtmp[0], rtmp[1]], writes=[rkr[h]])
    finals.append(p.dma("sp", lambda q: q.dma_start(out=lat_d[1024:1088, :], in_=kr[:]), s_out, reads=rkr))
    return finals


NQG = S // 512
NKB = S // 128
YOFF = [0, 129, 258, 0]
YBNK = [0, 0, 0, 1]


class AttnCtx:
    def __init__(self, p, P, rP, identb, rident, trib, rtri, yT_d, s_out):
        self.p, self.P, self.rP = p, P, rP
        self.identb, self.rident, self.trib, self.rtri = identb, rident, trib, rtri
        self.pt = [p.sbuf(f"pt{i}", [128, 512], BF16) for i in range(3)]
        self.rpt = [Res(f"pt{i}") for i in range(3)]
        self.yn = [p.sbuf(f"yn{i}", [128, 128], BF16) for i in range(2)]
        self.ryn = [Res(f"yn{i}") for i in range(2)]
        self.ytb = [p.sbuf(f"ytb{i}", [128, 512], BF16) for i in range(2)]
        self.rytb = [Res(f"ytb{i}") for i in range(2)]
        self.rinv = [p.sbuf(f"rinv{i}", [128, 1], F32) for i in range(2)]
        self.rrinv = [Res(f"rinv{i}") for i in range(2)]
        self.tp = p.enter(p.nc.psum_tensor("ps_tp", [128, 1024], BF16))
        self.rtp = Res("tp")
        self.yT_d, self.s_out = yT_d, s_out
        self.finals = []
        self.kpt = 0
        self.kyn = 0
        self.kg = 0

    def finish_block(self, src_ap, sum_ap, src_res, s, row0, g, last):
        p = self.p
        i = self.kyn % 2
        self.kyn += 1
        gb = self.kg % 2
        p.op("dve", lambda v, i=i: v.reciprocal(out=self.rinv[i][:], in_=sum_ap), reads=src_res, writes=[self.rrinv[i]])
        p.op("dve", lambda v, i=i: v.tensor_scalar(out=self.yn[i][:], in0=src_ap, scalar1=self.rinv[i][:, 0:1], scalar2=None,
                                                   op0=ALU.mult),
             reads=list(src_res) + [self.rrinv[i]], writes=[self.ryn[i]])
        p.op("pe", lambda t, i=i: t.transpose(out=self.tp[:, 0:128], in_=self.yn[i][:], identity=self.identb[:]),
             reads=[self.ryn[i], self.rident], writes=[self.rtp])
        p.op("dve", lambda v, s=s, gb=gb: v.tensor_copy(out=self.ytb[gb][:, s * 128:(s + 1) * 128], in_=self.tp[:, 0:128]),
             reads=[self.rtp], writes=[self.rytb[gb]])
        if last:
            self.finals.append(p.dma("sp", lambda q, gb=gb: q.dma_start(
                out=self.yT_d[row0:row0 + 128, g * 512:(g + 1) * 512], in_=self.ytb[gb][:]), self.s_out,
                reads=[self.rytb[gb]]))
            self.kg += 1


def dense_causal_sweep(cx, qk_fn, bias_fn, V, rV, scale, row0):
    p, P, rP = cx.p, cx.P, cx.rP
    for g in range(NQG):
        for J in range(4 * g + 4):
            r = J - 4 * g
            c0 = max(r, 0) * 128
            sb = J % 2
            terms = qk_fn(J, g, c0)
            for ti, (lh, rh, rd) in enumerate(terms):
                p.op("pe", lambda t, lh=lh, rh=rh, ti=ti, sb=sb, c0=c0, nt=len(terms), r=r: t.matmul(
                    P[sb][:, c0:512], lhsT=lh, rhs=rh, start=(ti == 0), stop=(ti == nt - 1 and r < 0)),
                    reads=rd, writes=[rP[sb]])
            if r >= 0:
                p.op("pe", lambda t, sb=sb, c0=c0: t.matmul(P[sb][:, c0:c0 + 128], lhsT=cx.identb[:], rhs=cx.trib[:],
                                                            start=False, stop=True),
                     reads=[cx.rident, cx.rtri], writes=[rP[sb]])
            pi = cx.kpt % 3
            cx.kpt += 1
            b = bias_fn(g, J) if bias_fn else None
            if b is None:
                p.op("act", lambda a, pi=pi, sb=sb, c0=c0: a.activation(out=cx.pt[pi][:, c0:512], in_=P[sb][:, c0:512],
                                                                          func=AF.Exp, scale=scale),
                     reads=[rP[sb]], writes=[cx.rpt[pi]])
            else:
                bap, brd = b
                p.op("act", lambda a, pi=pi, sb=sb, c0=c0, bap=bap: a.activation(
                    out=cx.pt[pi][:, c0:512], in_=P[sb][:, c0:512], func=AF.Exp, scale=scale, bias=bap),
                    reads=[rP[sb]] + brd, writes=[cx.rpt[pi]])
            for s in range(max(r, 0), 4):
                bank = 3 + s
                p.op("pe", lambda t, pi=pi, s=s, bank=bank, J=J, g=g: t.matmul(
                    P[bank][:, 0:129], lhsT=cx.pt[pi][:, s * 128:(s + 1) * 128], rhs=V[:, J, 0:129],
                    start=(J == 0), stop=(J == 4 * g + s)),
                    reads=[cx.rpt[pi], rV[J // 4]], writes=[rP[bank]])
                if J == 4 * g + s:
                    cx.finish_block(P[bank][:, 0:128], P[bank][:, 128:129], [rP[bank]], s, row0, g, s == 3)


def build_AE(x_f32=False, stop=None, flags="qkvgABFrlhmx"):
    nc = bass.Bass("TRN2", target_bir_lowering=False)
    dt = nc.dram_tensor
    xT_d = dt("xT", [NCORES, D, TOK], F32 if x_f32 else BF16, kind="ExternalInput").ap()
    wh_d = dt("wh", [D, 770], F32, kind="ExternalInput").ap()
    cst_d = dt("cst", [128, 576], F32, kind="ExternalInput").ap()
    gb_d = dt("gb", [128, 258], F32, kind="ExternalInput").ap()
    yT_d = dt("yT", [256, S], BF16, kind="ExternalOutput").ap()

    p = Prog(nc)
    s_in = p.new_dma_sem("s_in")
    s_w = p.new_dma_sem("s_w")
    s_out = p.new_dma_sem("s_out")
    s_x = [p.new_dma_sem(f"s_x{i}") for i in range(2)]
    s_xe = [p.new_dma_sem(f"s_xe{i}", exclusive=True) for i in range(2)]
    cst = p.sbuf("cst", [128, 576], F32)
    rcst = Res("cst")
    p.dma("sp", lambda q: q.dma_start(out=cst[:], in_=cst_d[:, :]), s_in, writes=[rcst])
    gb = p.sbuf("gb", [128, 258], F32)
    rgb = Res("gb")
    p.dma("sp", lambda q: q.dma_start(out=gb[:], in_=gb_d[:, :]), s_in, writes=[rgb])
    U, LST, E127, IDF, TRIF = cst[:, 0:128], cst[0:64, 128:192], cst[:, 192:320], cst[:, 320:448], cst[:, 448:576]
    identb = p.sbuf("identb", [128, 128], BF16)
    trib = p.sbuf("trib", [128, 128], BF16)
    d0b = p.sbuf("d0b", [128, 128], BF16)
    d1b = p.sbuf("d1b", [128, 128], BF16)
    rident, rtri, rd0, rd1 = Res("identb"), Res("trib"), Res("d0b"), Res("d1b")
    tmpc = p.sbuf("tmpc", [128, 128], F32)
    rtmpc = Res("tmpc")
    p.op("dve", lambda v: v.tensor_copy(out=identb[:], in_=IDF), reads=[rcst], writes=[rident])
    p.op("dve", lambda v: v.tensor_copy(out=trib[:], in_=TRIF), reads=[rcst], writes=[rtri])
    SQ = math.sqrt(HD)
    p.op("dve", lambda v: v.tensor_scalar(out=tmpc[:], in0=gb[:, 0:128], scalar1=gb[:, 256:257], scalar2=SQ,
                                          op0=ALU.subtract, op1=ALU.mult), reads=[rgb], writes=[rtmpc])
    p.op("dve", lambda v: v.tensor_tensor(out=d0b[:], in0=tmpc[:], in1=TRIF, op=ALU.add), reads=[rtmpc, rcst], writes=[rd0])
    p.op("dve", lambda v: v.tensor_scalar(out=d1b[:], in0=gb[:, 128:256], scalar1=gb[:, 256:257], scalar2=SQ,
                                          op0=ALU.subtract, op1=ALU.mult), reads=[rgb], writes=[rd1])
    Wh = p.sbuf("Wh", [128, KC, 770], BF16)
    rWh = Res("Wh")
    whv = wh_d.rearrange("(kc p) n -> p kc n", p=128)
    for k4 in range(4):
        p.dma("pool", lambda q, k4=k4: q.dma_start(out=Wh[:, k4 * 4:(k4 + 1) * 4, :], in_=whv[:, k4 * 4:(k4 + 1) * 4, :]),
              s_w, writes=[rWh])
    QAT = p.sbuf("QAT", [128, S], BF16)
    KAT = p.sbuf("KAT", [128, S], BF16)
    QBT = p.sbuf("QBT", [128, S], BF16)
    KBT = p.sbuf("KBT", [128, S], BF16)
    VA = p.sbuf("VA", [128, NKB, 130], BF16)
    VB = p.sbuf("VB", [128, NKB, 130], BF16)
    rQA = [Res(f"QA{i}") for i in range(NQG)]
    rKA = [Res(f"KA{i}") for i in range(NQG)]
    rQB = [Res(f"QB{i}") for i in range(NQG)]
    rKB = [Res(f"KB{i}") for i in range(NQG)]
    rVA = [Res(f"VA{i}") for i in range(NQG)]
    rVB = [Res(f"VB{i}") for i in range(NQG)]
    Fl = p.sbuf("Fl", [128, NKB], F32)
    rF = Res("F")
    kmT = p.sbuf("kmT", [128, 32], F32)
    rkm = Res("kmT")
    sel = p.sbuf("sel", [128, NKB, 32], F32)
    rsel = [Res(f"sel{i}") for i in range(NKB)]
    gm = p.sbuf("gm", [128, 32], F32)
    rgm = Res("gm")
    m8 = p.sbuf("m8", [128, 8], F32)
    rm8 = Res("m8")
    qlo = [p.sbuf(f"qlo{i}", [128, 512], BF16) for i in range(2)]
    kmh = p.sbuf("kmh", [128, 32], BF16)
    kml = p.sbuf("kml", [128, 32], BF16)
    rkmh = Res("kmh")
    rqf = [Res(f"qf{i}") for i in range(2)]
    xc = [p.sbuf(f"xc{i}", [128, KC, 512], BF16) for i in range(2)]
    rxc = [Res(f"xc{i}") for i in range(2)]
    P = [p.psum(f"P{i}", [128, 512], F32) for i in range(7)]
    rP = [Res(f"P{i}") for i in range(7)]
    p.op("dve", lambda v: v.memset(kmT[:], 0.0), writes=[rkm])
    p.op("dve", lambda v: v.memset(gm[:], -1e30), writes=[rgm])
    for i, (Vt, rVt) in enumerate(((VA, rVA), (VB, rVB))):
        p.op("pool", lambda g_, Vt=Vt: g_.memset(Vt[:, :, 128:130], 1.0), writes=rVt)

    rr = [0]

    def nb():
        b = rr[0] % 7
        rr[0] += 1
        return b

    for tc in range(NQG):
        xi = tc % 2
        src = xT_d[tc // 2, :, (tc % 2) * 512:(tc % 2 + 1) * 512].rearrange("(kc p) t -> p kc t", p=128)
        if x_f32:
            for k4 in range(4):
                ev_ = p.dma("pool", lambda q, xi=xi, src=src, k4=k4: q.dma_start(out=xc[xi][:, k4 * 4:(k4 + 1) * 4, :],
                                                                            in_=src[:, k4 * 4:(k4 + 1) * 4, :]),
                            s_xe[xi], writes=[rxc[xi]])
        else:
            p.dma("sp", lambda q, xi=xi, src=src: q.dma_start(out=xc[xi][:], in_=src), s_x[xi], writes=[rxc[xi]])
        tsl = slice(tc * 512, (tc + 1) * 512)
        for (c0, dst, rdst, extra) in ((0, QAT, rQA, None), (128, KAT, rKA, None), (384, KBT, rKB, "km"), (256, QBT, rQB, "qf")):
            b = nb()
            for k in range(KC):
                p.op("pe", lambda t, k=k, b=b, c0=c0, xi=xi: t.matmul(P[b][:], lhsT=Wh[:, k, c0:c0 + 128], rhs=xc[xi][:, k, :],
                                                                      start=(k == 0), stop=(k == KC - 1)),
                     reads=[rWh, rxc[xi]], writes=[rP[b]])
            if extra is None:
                p.op("act", lambda a, b=b, dst=dst, tsl=tsl: a.activation(out=dst[:, tsl], in_=P[b][:], func=AF.Copy),
                     reads=[rP[b]], writes=[rdst[tc]])
            else:
                p.op("dve", lambda v, b=b, dst=dst, tsl=tsl: v.tensor_copy(out=dst[:, tsl], in_=P[b][:]),
                     reads=[rP[b]], writes=[rdst[tc]])
            if extra == "km" and "r" in flags:
                p.op("dve", lambda v, b=b, tc=tc: v.tensor_reduce(out=kmT[:, 2 * tc:2 * tc + 2],
                                                                  in_=P[b][:].rearrange("p (n k) -> p n k", n=2),
                                                                  axis=AX.X, op=ALU.add),
                     reads=[rP[b]], writes=[rkm])
            if extra == "qf" and "l" in flags:
                p.op("dve", lambda v, b=b, xi=xi, tsl=tsl: v.tensor_tensor(out=qlo[xi][:], in0=P[b][:], in1=QBT[:, tsl], op=ALU.subtract),
                     reads=[rP[b], rQB[tc]], writes=[rqf[xi]])
        for tb in range(4 if "v" in flags else 0):
            b = nb()
            blk = tc * 4 + tb
            for k in range(KC):
                p.op("pe", lambda t, k=k, b=b, tb=tb, xi=xi: t.matmul(P[b][:, 0:257], lhsT=xc[xi][:, k, tb * 128:(tb + 1) * 128],
                                                                      rhs=Wh[:, k, 512:769], start=(k == 0), stop=(k == KC - 1)),
                     reads=[rWh, rxc[xi]], writes=[rP[b]])
            if "A" in flags:
                p.op("dve", lambda v, b=b, blk=blk: v.tensor_copy(out=VA[:, blk, 0:128], in_=P[b][:, 0:128]),
                     reads=[rP[b]], writes=[rVA[tc]])
            if "B" in flags:
                p.op("dve", lambda v, b=b, blk=blk: v.tensor_copy(out=VB[:, blk, 0:128], in_=P[b][:, 129:257]),
                     reads=[rP[b]], writes=[rVB[tc]])
            if "F" in flags:
                p.op("dve", lambda v, b=b, blk=blk: v.tensor_copy(out=Fl[:, blk:blk + 1], in_=P[b][:, 128:129]),
                     reads=[rP[b]], writes=[rF])
        for s in range(4):
            qb = tc * 4 + s
            own = qb // 2
            if own == 0 or "g" not in flags:
                continue
            b = nb()
            if s in (0, 2) and "h" in flags:
                p.op("dve", lambda v: v.tensor_copy(out=kmh[:], in_=kmT[:]), reads=[rkm], writes=[rkmh])
                p.op("dve", lambda v: v.tensor_tensor(out=kml[:], in0=kmT[:], in1=kmh[:], op=ALU.subtract),
                     reads=[rkm, rkmh], writes=[rkmh])
            qsl = slice(tc * 512 + s * 128, tc * 512 + (s + 1) * 128)
            if "m" not in flags:
                continue
            p.op("pe", lambda t, b=b, qsl=qsl: t.matmul(P[b][:, 0:32], lhsT=QBT[:, qsl], rhs=kmh[:, :], start=True, stop=False),
                 reads=[rQB[tc], rkmh], writes=[rP[b]])
            p.op("pe", lambda t, b=b, qsl=qsl: t.matmul(P[b][:, 0:32], lhsT=QBT[:, qsl], rhs=kml[:, :], start=False, stop=False),
                 reads=[rQB[tc], rkmh], writes=[rP[b]])
            p.op("pe", lambda t, b=b, s=s, xi=xi: t.matmul(P[b][:, 0:32], lhsT=qlo[xi][:, s * 128:(s + 1) * 128], rhs=kmh[:, :],
                                                            start=False, stop=True),
                 reads=[rqf[xi], rkmh], writes=[rP[b]])
            if "x" not in flags:
                continue
            p.op("dve", lambda v, b=b, own=own: v.tensor_copy(out=gm[:, 0:own], in_=P[b][:, 0:own]), reads=[rP[b]], writes=[rgm])
            p.op("dve", lambda v: v.max(out=m8[:], in_=gm[:]), reads=[rgm], writes=[rm8])
            p.op("dve", lambda v, qb=qb, own=own: v.tensor_scalar(out=sel[:, qb, 0:own], in0=gm[:, 0:own], scalar1=m8[:, 2:3],
                                                                  scalar2=None, op0=ALU.is_ge),
                 reads=[rgm, rm8], writes=[rsel[qb]])

    if stop == "proj":
        ev = p.dma("sp", lambda q: q.dma_start(out=yT_d[0:128, :], in_=QAT[:, :]), None, reads=rQA + rKA + rQB + rKB + rVA + rVB + rsel + [rF])
        p.emit({"sp": [ev]})
        p.close()
        return nc
    nbf = p.sbuf("nbf", [128, 1], F32)
    rnbf = Res("nbf")
    p.op("dve", lambda v: v.tensor_scalar(out=nbf[:], in0=gb[:, 257:258], scalar1=-1.0, scalar2=None, op0=ALU.mult),
         reads=[rgb], writes=[rnbf])
    ef = p.sbuf("ef", [128, NKB], F32)
    lf = p.sbuf("lf", [128, NKB], F32)
    ref_, rlf = Res("ef"), Res("lf")
    p.op("act", lambda a: a.activation(out=ef[:], in_=Fl[:], func=AF.Exp, scale=-1.0, bias=nbf[:, 0:1]),
         reads=[rF, rnbf], writes=[ref_])
    p.op("act", lambda a: a.activation(out=lf[:], in_=ef[:], func=AF.Ln, bias=1.0, scale=1.0), reads=[ref_], writes=[rlf])
    p.op("dve", lambda v: v.tensor_scalar(out=lf[:], in0=lf[:], scalar1=-1.0, scalar2=None, op0=ALU.mult),
         reads=[rlf], writes=[rlf])
    onesf = p.sbuf("onesf", [128, 128], F32)
    ronesf = Res("onesf")
    p.op("dve", lambda v: v.memset(onesf[:], 1.0), writes=[ronesf])
    b0 = nb()
    p.op("pe", lambda t: t.matmul(P[b0][0:64, 0:1], lhsT=lf[:, :], rhs=onesf[:, 0:1], start=True, stop=True),
         reads=[rlf, ronesf], writes=[rP[b0]])
    totbc = p.sbuf("totbc", [64, 128], F32)
    rtot = Res("totbc")
    p.op("dve", lambda v: v.tensor_scalar(out=totbc[:], in0=onesf[0:64, :], scalar1=P[b0][0:64, 0:1], scalar2=None, op0=ALU.mult),
         reads=[rP[b0], ronesf], writes=[rtot])
    b1 = nb()
    p.op("pe", lambda t: t.matmul(P[b1][:, 0:NKB], lhsT=U, rhs=lf[:, :], start=True, stop=False),
         reads=[rcst, rlf], writes=[rP[b1]])
    p.op("pe", lambda t: t.matmul(P[b1][:, 0:NKB], lhsT=totbc[:, :], rhs=LST, start=False, stop=True),
         reads=[rcst, rtot], writes=[rP[b1]])
    csb = p.sbuf("csb", [128, NKB], F32)
    negc = p.sbuf("negc", [128, NKB], F32)
    rcsb, rnegc = Res("csb"), Res("negc")
    p.op("dve", lambda v: v.tensor_copy(out=csb[:], in_=P[b1][:, 0:NKB]), reads=[rP[b1]], writes=[rcsb])
    p.op("dve", lambda v: v.tensor_scalar(out=negc[:], in0=P[b1][:, 0:NKB], scalar1=-1.0, scalar2=None, op0=ALU.mult),
         reads=[rP[b1]], writes=[rnegc])
    cendc = p.sbuf("cendc", [128, NQG], F32)
    rcendc = Res("cendc")
    p.op("dve", lambda v: v.tensor_copy(out=cendc[:], in_=csb[:].rearrange("p (g f) -> p g f", f=4)[:, :, 3]),
         reads=[rcsb], writes=[rcendc])
    b2 = nb()
    p.op("pe", lambda t: t.matmul(P[b2][:, 0:NQG], lhsT=E127, rhs=cendc[:, :], start=True, stop=True),
         reads=[rcst, rcendc], writes=[rP[b2]])
    cend = p.sbuf("cend", [128, NQG], F32)
    rcend = Res("cend")
    p.op("dve", lambda v: v.tensor_copy(out=cend[:], in_=P[b2][:, 0:NQG]), reads=[rP[b2]], writes=[rcend])
    ball = p.sbuf("ball", [128, NQG, NKB], F32)
    rball = Res("ball")
    for g in range(NQG):
        p.op("dve", lambda v, g=g: v.tensor_scalar(out=ball[:, g, :], in0=negc[:, :], scalar1=cend[:, g:g + 1], scalar2=None,
                                                   op0=ALU.add), reads=[rnegc, rcend], writes=[rball])

    cx = AttnCtx(p, P, rP, identb, rident, trib, rtri, yT_d, s_out)
    SCALE = HD ** -0.5

    def qk_fox(J, g, c0):
        return [(KAT[:, J * 128:(J + 1) * 128], QAT[:, g * 512 + c0:(g + 1) * 512], [rKA[J // 4], rQA[g]])]

    def bias_fox(g, J):
        return (ball[:, g, J:J + 1], [rball])

    dense_causal_sweep(cx, qk_fox, bias_fox, VA, rVA, SCALE, 0)

    if stop == "fox":
        p.emit({"sp": cx.finals})
        p.close()
        return nc
    acc = p.sbuf("acc", [128, 4, 129], F32)
    racc = [Res(f"acc{s}") for s in range(4)]
    for g in range(NQG):
        for n in range(2 * g + 2):
            yb = [3 + 2 * (n % 2), 4 + 2 * (n % 2)]
            started = [False, False]
            smin_n = max(2 * n - 4 * g, 0)
            for J in (2 * n, 2 * n + 1):
                r = J - 4 * g
                c0 = max(r, 0) * 128
                sb = J % 2
                adds = []
                for s in range(max(r, 0), 4):
                    if J == 4 * g + s:
                        adds.append((s, d0b, rd0))
                    elif J == 4 * g + s - 1:
                        adds.append((s, d1b, rd1))
                p.op("pe", lambda t, J=J, g=g, c0=c0, sb=sb, na=len(adds): t.matmul(
                    P[sb][:, c0:512], lhsT=KBT[:, J * 128:(J + 1) * 128], rhs=QBT[:, g * 512 + c0:(g + 1) * 512],
                    start=True, stop=(na == 0)), reads=[rKB[J // 4], rQB[g]], writes=[rP[sb]])
                for ai, (s, dt_, rdt) in enumerate(adds):
                    p.op("pe", lambda t, s=s, dt_=dt_, sb=sb, ai=ai, na=len(adds): t.matmul(
                        P[sb][:, s * 128:(s + 1) * 128], lhsT=identb[:], rhs=dt_[:], start=False, stop=(ai == na - 1)),
                        reads=[rident, rdt], writes=[rP[sb]])
                pi = cx.kpt % 3
                cx.kpt += 1
                p.op("act", lambda a, pi=pi, sb=sb, c0=c0: a.activation(out=cx.pt[pi][:, c0:512], in_=P[sb][:, c0:512],
                                                                          func=AF.Exp, scale=SCALE),
                     reads=[rP[sb]], writes=[cx.rpt[pi]])
                for s in range(max(r, 0), 4):
                    bi = YBNK[s]
                    bank = yb[bi]
                    st = not started[bi]
                    started[bi] = True
                    p.op("pe", lambda t, pi=pi, s=s, bank=bank, J=J, st=st: t.matmul(
                        P[bank][:, YOFF[s]:YOFF[s] + 129], lhsT=cx.pt[pi][:, s * 128:(s + 1) * 128], rhs=VB[:, J, 0:129],
                        start=st, stop=True, skip_group_check=True),
                        reads=[cx.rpt[pi], rVB[J // 4]], writes=[rP[bank]])
            for s in range(smin_n, 4):
                qb = 4 * g + s
                own = qb // 2
                bank = yb[YBNK[s]]
                src = P[bank][:, YOFF[s]:YOFF[s] + 129]
                if n == own:
                    if n == 0:
                        p.op("dve", lambda v, s=s, src=src: v.tensor_copy(out=acc[:, s, :], in_=src),
                             reads=[rP[bank]], writes=[racc[s]])
                    else:
                        p.op("dve", lambda v, s=s, src=src: v.tensor_tensor(out=acc[:, s, :], in0=acc[:, s, :], in1=src, op=ALU.add),
                             reads=[rP[bank], racc[s]], writes=[racc[s]])
                    cx.finish_block(acc[:, s, 0:128], acc[:, s, 128:129], [racc[s]], s, 128, g, s == 3)
                elif n == 0:
                    p.op("dve", lambda v, s=s, src=src, qb=qb: v.tensor_scalar(out=acc[:, s, :], in0=src, scalar1=sel[:, qb, 0:1],
                                                                              scalar2=None, op0=ALU.mult),
                         reads=[rP[bank], rsel[qb]], writes=[racc[s]])
                else:
                    p.op("dve", lambda v, s=s, src=src, qb=qb, n=n: v.scalar_tensor_tensor(
                        out=acc[:, s, :], in0=src, scalar=sel[:, qb, n:n + 1], in1=acc[:, s, :], op0=ALU.mult, op1=ALU.add),
                        reads=[rP[bank], rsel[qb], racc[s]], writes=[racc[s]])

    p.emit({"sp": cx.finals})
    p.close()
    return nc


def _t5_bucket(rel):
    n = np.maximum(rel, 0)
    nf = np.maximum(n, 1).astype(np.float32)
    large = 16 + (np.log(nf / np.float32(16)) / np.float32(math.log(128 / 16)) * np.float32(16)).astype(np.int32)
    large = np.minimum(large, 31)
    return np.where(n < 16, n, large)


def consts_AE():
    j = np.arange(128)
    cst = np.zeros((128, 576), np.float32)
    cst[:, 0:128] = (j[:, None] <= j[None, :])
    b = np.arange(64)
    cst[0:64, 128:192] = (b[:, None] < b[None, :])
    cst[127, 192:320] = 1.0
    cst[:, 320:448] = np.eye(128)
    cst[:, 448:576] = np.where(j[:, None] <= j[None, :], 0.0, NEG)
    return cst


def gb_AE(rel_bias, b_f, head):
    j = np.arange(128)
    rel0 = j[None, :] - j[:, None]
    gbm = np.zeros((128, 258), np.float32)
    col = rel_bias[:, head]
    gbm[:, 0:128] = col[_t5_bucket(rel0)]
    gbm[:, 128:256] = col[_t5_bucket(128 + rel0)]
    gbm[:, 256] = col[31]
    gbm[:, 257] = b_f[head]
    return gbm


def build_AO():
    nc = bass.Bass("TRN2", target_bir_lowering=False)
    dt = nc.dram_tensor
    lat_d = dt("lat", [NCORES, MLA_IN, TOK], BF16, kind="ExternalInput").ap()
    wuq_d = dt("wuq", [512, 384], F32, kind="ExternalInput").ap()
    wukv_d = dt("wukv", [512, 512], F32, kind="ExternalInput").ap()
    cs_d = dt("cs", [64, 2 * S], F32, kind="ExternalInput").ap()
    cst_d = dt("cst", [128, 576], F32, kind="ExternalInput").ap()
    yT_d = dt("yT", [256, S], BF16, kind="ExternalOutput").ap()

    p = Prog(nc)
    cst = p.sbuf("cst", [128, 576], F32)
    rcst = Res("cst")
    p.dma("sp", lambda q: q.dma_start(out=cst[:], in_=cst_d[:, :]), None, writes=[rcst])
    identb = p.sbuf("identb", [128, 128], BF16)
    trib = p.sbuf("trib", [128, 128], BF16)
    rident, rtri = Res("identb"), Res("trib")
    p.op("dve", lambda v: v.tensor_copy(out=identb[:], in_=cst[:, 320:448]), reads=[rcst], writes=[rident])
    p.op("dve", lambda v: v.tensor_copy(out=trib[:], in_=cst[:, 448:576]), reads=[rcst], writes=[rtri])
    Wq = p.sbuf("Wq", [128, 4, 384], BF16)
    Wkv = p.sbuf("Wkv", [128, 4, 512], BF16)
    Wrot = p.sbuf("Wrot", [128, 2, 4, 64], BF16)
    rWq, rWkv, rWrot = Res("Wq"), Res("Wkv"), Res("Wrot")
    s_w = p.new_dma_sem("s_w", exclusive=True)
    p.dma("pool", lambda q: q.dma_start(out=Wq[:], in_=wuq_d.rearrange("(kc p) n -> p kc n", p=128)), s_w, writes=[rWq])
    s_w2 = p.new_dma_sem("s_w2", exclusive=True)
    p.dma("pool", lambda q: q.dma_start(out=Wkv[:], in_=wukv_d.rearrange("(kc p) n -> p kc n", p=128)), s_w2, writes=[rWkv])
    for h in range(2):
        r0 = h * 192 + 128
        p.op("dve", lambda v, h=h, r0=r0: v.tensor_scalar(out=Wrot[:, h, :, 0:32], in0=Wq[:, :, r0 + 32:r0 + 64], scalar1=-1.0,
                                                          scalar2=None, op0=ALU.mult), reads=[rWq], writes=[rWrot])
        p.op("dve", lambda v, h=h, r0=r0: v.tensor_copy(out=Wrot[:, h, :, 32:64], in_=Wq[:, :, r0:r0 + 32]), reads=[rWq], writes=[rWrot])
    QN = p.sbuf("QN", [128, S], BF16)
    KN = p.sbuf("KN", [128, S], BF16)
    QR = p.sbuf("QR", [64, S], BF16)
    KR = p.sbuf("KR", [64, S], BF16)
    V = p.sbuf("V", [128, NKB, 130], BF16)
    rQN = [Res(f"QN{i}") for i in range(NQG)]
    rKN = [Res(f"KN{i}") for i in range(NQG)]
    rQR = [Res(f"QR{i}") for i in range(NQG)]
    rKR = [Res(f"KR{i}") for i in range(NQG)]
    rV = [Res(f"V{i}") for i in range(NQG)]
    latc = [p.sbuf(f"latc{i}", [128, 8, 512], BF16) for i in range(2)]
    rlatc = [Res(f"latc{i}") for i in range(2)]
    csc = [p.sbuf(f"csc{i}", [64, 2, 512], F32) for i in range(2)]
    rcsc = [Res(f"csc{i}") for i in range(2)]
    t0 = p.sbuf("t0", [64, 512], F32)
    t1 = p.sbuf("t1", [64, 512], F32)
    rt0, rt1 = Res("t0"), Res("t1")
    P = [p.psum(f"P{i}", [128, 512], F32) for i in range(7)]
    rP = [Res(f"P{i}") for i in range(7)]
    p.op("pool", lambda g_: g_.memset(V[:, :, 128:130], 1.0), writes=rV)
    cx = AttnCtx(p, P, rP, identb, rident, trib, rtri, yT_d, None)
    SCALE = 192.0 ** -0.5
    rr = [0]

    def nb():
        b = rr[0] % 3
        rr[0] += 1
        return b

    kq = 0
    for h in range(2):
        for tc in range(NQG):
            xi = kq % 2
            kq += 1
            r_, c_ = tc // 2, (tc % 2) * 512
            tsl = slice(tc * 512, (tc + 1) * 512)
            p.dma("sp", lambda q, xi=xi, r_=r_, c_=c_: q.dma_start(
                out=latc[xi][:], in_=lat_d[r_, 0:1024, c_:c_ + 512].rearrange("(c p) t -> p c t", p=128)), None,
                writes=[rlatc[xi]])
            p.dma("sp", lambda q, xi=xi, tsl=tsl: q.dma_start(out=csc[xi][:, 0, :], in_=cs_d[:, tsl]), None, writes=[rcsc[xi]])
            p.dma("sp", lambda q, xi=xi, tc=tc: q.dma_start(out=csc[xi][:, 1, :], in_=cs_d[:, S + tc * 512:S + (tc + 1) * 512]), None,
                  writes=[rcsc[xi]])
            if h == 0:
                p.dma("sp", lambda q, r_=r_, c_=c_, tsl=tsl: q.dma_start(out=KR[:, tsl], in_=lat_d[r_, 1024:1088, c_:c_ + 512]), None,
                      writes=[rKR[tc]])
            for (W, rW, c0, kof, dst, rdst) in ((Wq, rWq, h * 192, 0, QN, rQN), (Wkv, rWkv, h * 256, 4, KN, rKN)):
                b = nb()
                for k in range(4):
                    p.op("pe", lambda t, k=k, b=b, W=W, c0=c0, kof=kof, xi=xi: t.matmul(
                        P[b][:], lhsT=W[:, k, c0:c0 + 128], rhs=latc[xi][:, kof + k, :], start=(k == 0), stop=(k == 3)),
                        reads=[rW, rlatc[xi]], writes=[rP[b]])
                p.op("act", lambda a, b=b, dst=dst, tsl=tsl: a.activation(out=dst[:, tsl], in_=P[b][:], func=AF.Copy),
                     reads=[rP[b]], writes=[rdst[tc]])
            b0, b1 = nb(), nb()
            for k in range(4):
                p.op("pe", lambda t, k=k, b0=b0, h=h, xi=xi: t.matmul(P[b0][0:64, :], lhsT=Wq[:, k, h * 192 + 128:h * 192 + 192],
                                                                       rhs=latc[xi][:, k, :], start=(k == 0), stop=(k == 3)),
                     reads=[rWq, rlatc[xi]], writes=[rP[b0]])
            for k in range(4):
                p.op("pe", lambda t, k=k, b1=b1, h=h, xi=xi: t.matmul(P[b1][0:64, :], lhsT=Wrot[:, h, k, :],
                                                                       rhs=latc[xi][:, k, :], start=(k == 0), stop=(k == 3)),
                     reads=[rWrot, rlatc[xi]], writes=[rP[b1]])
            p.op("dve", lambda v, b0=b0, xi=xi: v.tensor_tensor(out=t0[:], in0=P[b0][0:64, :], in1=csc[xi][:, 0, :], op=ALU.mult),
                 reads=[rP[b0], rcsc[xi]], writes=[rt0])
            p.op("dve", lambda v, b1=b1, xi=xi: v.tensor_tensor(out=t1[:], in0=P[b1][0:64, :], in1=csc[xi][:, 1, :], op=ALU.mult),
                 reads=[rP[b1], rcsc[xi]], writes=[rt1])
            p.op("dve", lambda v, tsl=tsl: v.tensor_tensor(out=QR[:, tsl], in0=t0[:], in1=t1[:], op=ALU.add),
                 reads=[rt0, rt1], writes=[rQR[tc]])
            for tb in range(4):
                b = nb()
                blk = tc * 4 + tb
                for k in range(4):
                    p.op("pe", lambda t, k=k, b=b, tb=tb, h=h, xi=xi: t.matmul(
                        P[b][:, 0:128], lhsT=latc[xi][:, 4 + k, tb * 128:(tb + 1) * 128], rhs=Wkv[:, k, h * 256 + 128:h * 256 + 256],
                        start=(k == 0), stop=(k == 3)), reads=[rWkv, rlatc[xi]], writes=[rP[b]])
                p.op("dve", lambda v, b=b, blk=blk: v.tensor_copy(out=V[:, blk, 0:128], in_=P[b][:, 0:128]),
                     reads=[rP[b]], writes=[rV[tc]])

        def qk_mla(J, g, c0):
            return [(KN[:, J * 128:(J + 1) * 128], QN[:, g * 512 + c0:(g + 1) * 512], [rKN[J // 4], rQN[g]]),
                    (KR[:, J * 128:(J + 1) * 128], QR[:, g * 512 + c0:(g + 1) * 512], [rKR[J // 4], rQR[g]])]

        dense_causal_sweep(cx, qk_mla, None, V, rV, SCALE, h * 128)

    p.emit({"sp": cx.finals})
    p.close()
    return nc


_NC_CACHE = {}


def _prog(name, fn):
    if name not in _NC_CACHE:
        _NC_CACHE[name] = fn()
    return _NC_CACHE[name]


def _lay128(v):
    return np.ascontiguousarray(v.reshape(-1, 128).T)


def _run(nc, in_maps):
    res = run_bass_kernel_spmd(nc, in_maps, core_ids=list(range(NCORES)))
    return res.results


def kernel(x, ab_w_in, ab_forget_bias, ab_w_out, rel_bias, mla_w_in, mla_q_norm, mla_kv_norm, mla_w_uq,
           mla_w_ukv, mla_w_out, ffn_w_gate, ffn_w_up, ffn_w_down, ln_g, ln_b):
    f32 = np.float32
    x = np.asarray(x, f32)
    xT = np.ascontiguousarray(x[0].T)
    xres = [np.ascontiguousarray(xT[:, c * TOK:(c + 1) * TOK]) for c in range(NCORES)]
    cstAE = consts_AE()
    inv = (np.float32(10000.0) ** (-np.arange(0, 64, 2, dtype=f32) / np.float32(64))).astype(f32)
    ang = np.arange(S, dtype=f32)[:, None] * inv[None, :]
    cos, sin = np.cos(ang).astype(f32), np.sin(ang).astype(f32)
    csT = np.ascontiguousarray(np.concatenate([np.concatenate([cos, cos], 1).T, np.concatenate([sin, sin], 1).T], axis=1))

    gathered = None
    for layer in range(DEPTH):
        j = layer // 2
        if layer % 2 == 0:
            if layer == 0:
                nc = _prog("AE32", lambda: build_AE(x_f32=True))
                gathered = np.ascontiguousarray(xT.reshape(D, NCORES, TOK).transpose(1, 0, 2))
            else:
                nc = _prog("AE16", lambda: build_AE(x_f32=False))
            w = np.asarray(ab_w_in[j], f32)
            maps = []
            for c in range(NCORES):
                cols = [w[:, c * 128:(c + 1) * 128], w[:, 1024 + c * 128:1024 + (c + 1) * 128],
                        w[:, 3080 + c * 128:3080 + (c + 1) * 128], w[:, 4104 + c * 128:4104 + (c + 1) * 128],
                        w[:, 2048 + c * 128:2048 + (c + 1) * 128], w[:, 3072 + c:3072 + c + 1],
                        w[:, 5128 + c * 128:5128 + (c + 1) * 128], np.zeros((D, 1), f32)]
                maps.append({"xT": gathered, "wh": np.ascontiguousarray(np.concatenate(cols, axis=1)), "cst": cstAE,
                             "gb": gb_AE(np.asarray(rel_bias, f32), np.asarray(ab_forget_bias[j], f32), c)})
            outs = _run(nc, maps)
            yT_full = np.concatenate([o["yT"][0:128] for o in outs] + [o["yT"][128:256] for o in outs], axis=0)
            w_out = np.asarray(ab_w_out[j], f32)
        else:
            nc = _prog("AO", build_AO)
            wq = np.asarray(mla_w_uq[j], f32)
            wkv = np.asarray(mla_w_ukv[j], f32)
            maps = []
            for c in range(NCORES):
                maps.append({"lat": gathered, "wuq": np.ascontiguousarray(wq[:, c * 384:(c + 1) * 384]),
                             "wukv": np.ascontiguousarray(wkv[:, c * 512:(c + 1) * 512]), "cs": csT, "cst": cstAE})
            outs = _run(nc, maps)
            yT_full = np.concatenate([o["yT"] for o in outs], axis=0)
            w_out = np.asarray(mla_w_out[j], f32)
        mode = "final" if layer == DEPTH - 1 else ("mla" if (layer + 1) % 2 == 1 else "even")
        nc = _prog("T" + mode, lambda: build_T(mode))
        lnp = np.ascontiguousarray(np.concatenate([_lay128(np.asarray(ln_g[layer, 0], f32)), _lay128(np.asarray(ln_b[layer, 0], f32)),
                                                   _lay128(np.asarray(ln_g[layer, 1], f32)), _lay128(np.asarray(ln_b[layer, 1], f32))], axis=1))
        maps = []
        for c in range(NCORES):
            m = {"yT": np.ascontiguousarray(yT_full[:, c * TOK:(c + 1) * TOK]), "xres": xres[c], "wout": w_out, "lnp": lnp,
                 "wg": np.asarray(ffn_w_gate[layer], f32), "wu": np.asarray(ffn_w_up[layer], f32),
                 "wd": np.asarray(ffn_w_down[layer], f32)}
            if mode == "mla":
                jn = (layer + 1) // 2
                m["win"] = np.asarray(mla_w_in[jn], f32)
                m["ng"] = np.ascontiguousarray(np.concatenate([_lay128(np.asarray(mla_q_norm[jn], f32)),
                                                               _lay128(np.asarray(mla_kv_norm[jn], f32))], axis=1))
                m["cs"] = np.ascontiguousarray(np.concatenate([csT[:, c * TOK:(c + 1) * TOK], csT[:, S + c * TOK:S + (c + 1) * TOK]], axis=1))
            maps.append(m)
        outs = _run(nc, maps)
        xres = [o["xout"] for o in outs]
        if mode == "mla":
            gathered = np.ascontiguousarray(np.stack([o["lat"] for o in outs], axis=0))
        elif mode == "even":
            gathered = np.ascontiguousarray(np.stack([o["xTb"] for o in outs], axis=0))
    out = np.concatenate(xres, axis=1)
    return np.ascontiguousarray(out.T)[None].astype(np.float32)
```

```python
import math
import numpy as np
import ml_dtypes
import concourse.bass as bass
import concourse.mybir as mybir
from concourse.bass_utils import run_bass_kernel_spmd

F32 = mybir.dt.float32
BF16 = mybir.dt.bfloat16
AF = mybir.ActivationFunctionType
ALU = mybir.AluOpType
AX = mybir.AxisListType

NCORES = 8
D = 2048
S = 8192
DEPTH = 4
TOK = S // NCORES
DFF = 5632
NFC = DFF // 128
KC = D // 128
ALPHA = (2 * DEPTH) ** 0.25
HD = 128
MLA_IN = 1088
NEG = -30000.0

EPOCH = 20000
SAME_ENGINE_SYNC = True
COMPUTE = ("pe", "act", "dve", "pool")


class Res:
    __slots__ = ("w", "rs", "name")

    def __init__(self, name=""):
        self.w = None
        self.rs = {}
        self.name = name


class Prog:
    def __init__(self, nc, same_engine_sync=None):
        if same_engine_sync is None:
            same_engine_sync = SAME_ENGINE_SYNC
        self.nc = nc
        self.ops = {e: [] for e in ("pe", "act", "dve", "pool", "sp")}
        self.same_engine_sync = same_engine_sync
        self.dma_sems = []
        self._ctx = []

    def enter(self, cm):
        v = cm.__enter__()
        self._ctx.append(cm)
        return v

    def sbuf(self, name, shape, dtype):
        return self.enter(self.nc.sbuf_tensor("sb_" + name, list(shape), dtype))

    def psum(self, name, shape, dtype=F32):
        return self.enter(self.nc.psum_tensor("ps_" + name, list(shape), dtype))

    def new_dma_sem(self, name, exclusive=False):
        if not exclusive:
            return None
        h = self.enter(self.nc.semaphore(name))
        self.dma_sems.append([h, 0])
        return len(self.dma_sems) - 1

    NRING = 24

    def _ring_sem(self, queue):
        if not hasattr(self, "_rings"):
            self._rings = {}
        if queue not in self._rings:
            n = self.NRING if queue == "sp" else 8
            idx = []
            for i in range(n):
                h = self.enter(self.nc.semaphore(f"ring_{queue}{i}"))
                self.dma_sems.append([h, 0])
                idx.append(len(self.dma_sems) - 1)
            self._rings[queue] = [idx, [None] * n, 0]
        rg = self._rings[queue]
        i = rg[2] % len(rg[0])
        rg[2] += 1
        return rg, i

    def _deps(self, eng, reads, writes):
        deps = set()
        for r in reads:
            if r.w is not None:
                deps.add(r.w)
        for w in writes:
            if w.w is not None:
                deps.add(w.w)
            deps.update(w.rs.values())
        out = []
        for d in deps:
            if d[0] == "E":
                if d[1] == eng and (eng == "pe" or not self.same_engine_sync):
                    continue
                self.ops[d[1]][d[2]][2] = True
            out.append(d)
        return out

    def op(self, eng, fn, reads=(), writes=()):
        deps = self._deps(eng, reads, writes)
        idx = len(self.ops[eng])
        ev = ("E", eng, idx)
        self.ops[eng].append([fn, deps, False, None])
        for r in reads:
            r.rs[eng] = ev
        for w in writes:
            w.w = ev
            w.rs = {}
        return ev

    def dma(self, queue, fn, sem, reads=(), writes=()):
        deps = self._deps(queue, reads, writes)
        if sem is None:
            rg, ri = self._ring_sem(queue)
            sem = rg[0][ri]
            if rg[1][ri] is not None:
                deps.append(rg[1][ri])
            rg[1][ri] = ("S", sem, self.dma_sems[sem][1] + 16)
        self.dma_sems[sem][1] += 16
        ev = ("S", sem, self.dma_sems[sem][1])
        self.ops[queue].append([fn, deps, False, sem])
        for r in reads:
            r.rs[("S", sem)] = ev
        for w in writes:
            w.w = ev
            w.rs = {}
        return ev

    def emit(self, final_events):
        nc = self.nc
        sigval = {}
        esems = {}
        for e in COMPUTE + ("sp",):
            c = 0
            vals = []
            for o in self.ops[e]:
                if o[2] and o[3] is None:
                    c += 1
                vals.append(c)
            sigval[e] = vals
            n_ep = max((c + EPOCH - 1) // EPOCH, 1)
            esems[e] = [self.enter(nc.semaphore(f"s_{e}_{k}")) for k in range(n_ep)]

        def resolve(d):
            if d[0] == "S":
                return ("S", d[1]), self.dma_sems[d[1]][0], d[2]
            v = sigval[d[1]][d[2]]
            ep = (v - 1) // EPOCH
            return ("E", d[1], ep), esems[d[1]][ep], v - ep * EPOCH

        engmap = {"pe": "tensor", "act": "scalar", "dve": "vector", "pool": "gpsimd", "sp": "sync"}
        prog = self

        def body_for(e):
            def body(engine):
                waited = {}
                for i, (fn, deps, sig, dsem) in enumerate(prog.ops[e]):
                    for d in deps:
                        key, sem, val = resolve(d)
                        if waited.get(key, 0) < val:
                            engine.wait_ge(sem, val)
                            waited[key] = val
                    inst = fn(engine)
                    if dsem is not None:
                        inst.then_inc(prog.dma_sems[dsem][0], 16)
                    elif sig:
                        v = sigval[e][i]
                        ep = (v - 1) // EPOCH
                        inst.then_inc(esems[e][ep], 1)
                for d in final_events.get(e, ()):
                    key, sem, val = resolve(d)
                    engine.wait_ge(sem, val)
            return body

        with nc.Block() as block:
            for e in COMPUTE + ("sp",):
                if not self.ops[e] and not final_events.get(e):
                    continue
                getattr(block, engmap[e])(body_for(e))

    def close(self):
        while self._ctx:
            self._ctx.pop().__exit__(None, None, None)


class Slots:
    def __init__(self, p, name, n, nbytes):
        self.p = p
        self.n = n
        self.t = [p.sbuf(f"{name}{i}", [128, nbytes // 2], BF16) for i in range(n)]
        self.r = [Res(f"{name}{i}") for i in range(n)]
        self.sem = [p.new_dma_sem(f"{name}s{i}", exclusive=True) for i in range(n)]
        self.k = 0

    def load(self, shape, src_ap, queue="pool"):
        i = self.k % self.n
        self.k += 1
        a, b = shape
        view = self.t[i][:, 0:a * b].rearrange("p (a b) -> p a b", a=a)
        self.p.dma(queue, lambda q, v=view, s=src_ap: q.dma_start(out=v, in_=s), self.sem[i],
                   writes=[self.r[i]])
        return view, self.r[i]


def build_T(mode_next, has_mixer=True):
    nc = bass.Bass("TRN2", target_bir_lowering=False)
    dt = nc.dram_tensor
    yT_d = dt("yT", [D, TOK], BF16, kind="ExternalInput").ap()
    xres_d = dt("xres", [D, TOK], F32, kind="ExternalInput").ap()
    wout_d = dt("wout", [D, D], F32, kind="ExternalInput").ap()
    lnp_d = dt("lnp", [128, 4 * KC], F32, kind="ExternalInput").ap()
    wg_d = dt("wg", [D, DFF], F32, kind="ExternalInput").ap()
    wu_d = dt("wu", [D, DFF], F32, kind="ExternalInput").ap()
    wd_d = dt("wd", [DFF, D], F32, kind="ExternalInput").ap()
    xout_d = dt("xout", [D, TOK], F32, kind="ExternalOutput").ap()
    if mode_next == "even":
        xTb_d = dt("xTb", [D, TOK], BF16, kind="ExternalOutput").ap()
    if mode_next == "mla":
        win_d = dt("win", [D, MLA_IN], F32, kind="ExternalInput").ap()
        ng_d = dt("ng", [128, 8], F32, kind="ExternalInput").ap()
        cs_d = dt("cs", [64, 2 * TOK], F32, kind="ExternalInput").ap()
        lat_d = dt("lat", [MLA_IN, TOK], BF16, kind="ExternalOutput").ap()

    p = Prog(nc)
    z = p.sbuf("z", [128, KC, TOK], F32)
    xb = p.sbuf("xb", [128, KC, TOK], BF16)
    hg = [p.sbuf(f"hg{i}", [128, 4, TOK], BF16) for i in range(2)]
    ws = Slots(p, "ws", 4, 16384)
    lnp = p.sbuf("lnp", [128, 4 * KC], F32)
    ones = p.sbuf("ones", [128, 128], F32)
    mean = p.sbuf("mean", [128, TOK], F32)
    rstd = p.sbuf("rstd", [128, TOK], F32)
    tmpf = [p.sbuf(f"tmpf{i}", [128, 512], F32) for i in range(4)]
    P = [p.psum(f"P{i}", [128, 512], F32) for i in range(8)]
    rP = [Res(f"P{i}") for i in range(8)]
    rz = [[Res(f"z{m}_{h}") for h in range(2)] for m in range(KC)]
    rxb = [[Res(f"xb{m}_{h}") for h in range(2)] for m in range(KC)]
    rhg = [[[Res(f"hg{i}_{f}_{h}") for h in range(2)] for f in range(4)] for i in range(2)]
    rtmp = [Res(f"tmp{i}") for i in range(4)]
    rlnp, rones, rmean, rrstd = Res("lnp"), Res("ones"), [Res("mean0"), Res("mean1")], [Res("rstd0"), Res("rstd1")]
    s_in = p.new_dma_sem("s_in")
    s_in2 = p.new_dma_sem("s_in2")
    s_out = p.new_dma_sem("s_out")
    H = [slice(0, 512), slice(512, 1024)]

    p.dma("sp", lambda q: q.dma_start(out=lnp[:], in_=lnp_d[:, :]), s_in2, writes=[rlnp])
    p.op("dve", lambda v: v.memset(ones[:], 1.0 / D), writes=[rones])
    yv = yT_d.rearrange("(kc p) t -> p kc t", p=128)
    xv = xres_d.rearrange("(kc p) t -> p kc t", p=128)
    for m in range(KC):
        if has_mixer:
            p.dma("sp", lambda q, m=m: q.dma_start(out=xb[:, m, :], in_=yv[:, m, :]), s_in,
                  writes=[rxb[m][0], rxb[m][1]])
        p.dma("sp", lambda q, m=m: q.dma_start(out=z[:, m, :], in_=xv[:, m, :]), s_in2,
              writes=[rz[m][0], rz[m][1]])

    bank_rr = [0]

    def proj_accum(nk, lhs_fn, rhs_fn, rhs_res, w_res, bank):
        for k in range(nk):
            p.op("pe", lambda t, k=k, bank=bank: t.matmul(P[bank][:], lhsT=lhs_fn(k), rhs=rhs_fn(k),
                                                           start=(k == 0), stop=(k == nk - 1)),
                 reads=[w_res, rhs_res(k)], writes=[rP[bank]])

    if has_mixer:
        wv = wout_d.rearrange("(kc p) n -> p kc n", p=128)
        for m4 in range(4):
            slab, rs = ws.load((KC, 512), wv[:, :, m4 * 512:(m4 + 1) * 512])
            for mi in range(4):
                m = m4 * 4 + mi
                for h in range(2):
                    bank = bank_rr[0] % 4
                    bank_rr[0] += 1
                    proj_accum(KC, lambda k, slab=slab, mi=mi: slab[:, k, mi * 128:(mi + 1) * 128],
                               lambda k, h=h: xb[:, k, H[h]], lambda k, h=h: rxb[k][h], rs, bank)
                    p.op("dve", lambda v, m=m, h=h, bank=bank: v.scalar_tensor_tensor(
                        out=z[:, m, H[h]], in0=z[:, m, H[h]], scalar=ALPHA, in1=P[bank][:],
                        op0=ALU.mult, op1=ALU.add), reads=[rP[bank], rz[m][h]], writes=[rz[m][h]])

    def layer_norm(gi, bi):
        for h in range(2):
            for k in range(KC):
                p.op("pe", lambda t, k=k, h=h: t.matmul(P[4 + h][:], lhsT=ones[:], rhs=z[:, k, H[h]],
                                                         start=(k == 0), stop=(k == KC - 1)),
                     reads=[rones, rz[k][h]], writes=[rP[4 + h]])
            for k in range(KC):
                ti = k % 2
                p.op("act", lambda a, k=k, h=h, ti=ti: a.activation(out=tmpf[ti][:], in_=z[:, k, H[h]], func=AF.Square),
                     reads=[rz[k][h]], writes=[rtmp[ti]])
                p.op("pe", lambda t, k=k, h=h, ti=ti: t.matmul(P[6 + h][:], lhsT=ones[:], rhs=tmpf[ti][:],
                                                                start=(k == 0), stop=(k == KC - 1)),
                     reads=[rones, rtmp[ti]], writes=[rP[6 + h]])
            p.op("dve", lambda v, h=h: v.tensor_copy(out=mean[:, H[h]], in_=P[4 + h][:]),
                 reads=[rP[4 + h]], writes=[rmean[h]])
            p.op("dve", lambda v, h=h: v.tensor_tensor(out=tmpf[2][:], in0=mean[:, H[h]], in1=mean[:, H[h]], op=ALU.mult),
                 reads=[rmean[h]], writes=[rtmp[2]])
            p.op("dve", lambda v, h=h: v.tensor_tensor(out=tmpf[2][:], in0=P[6 + h][:], in1=tmpf[2][:], op=ALU.subtract),
                 reads=[rP[6 + h], rtmp[2]], writes=[rtmp[2]])
            p.op("dve", lambda v, h=h: v.tensor_scalar(out=tmpf[2][:], in0=tmpf[2][:], scalar1=1e-5, scalar2=None, op0=ALU.add),
                 reads=[rtmp[2]], writes=[rtmp[2]])
            p.op("act", lambda a, h=h: a.activation(out=tmpf[3][:], in_=tmpf[2][:], func=AF.Sqrt),
                 reads=[rtmp[2]], writes=[rtmp[3]])
            p.op("dve", lambda v, h=h: v.reciprocal(out=rstd[:, H[h]], in_=tmpf[3][:]),
                 reads=[rtmp[3]], writes=[rrstd[h]])
        for h in range(2):
            for k in range(KC):
                p.op("dve", lambda g, k=k, h=h: g.tensor_tensor(out=z[:, k, H[h]], in0=z[:, k, H[h]], in1=mean[:, H[h]], op=ALU.subtract),
                     reads=[rz[k][h], rmean[h]], writes=[rz[k][h]])
                p.op("dve", lambda v, k=k, h=h: v.tensor_tensor(out=z[:, k, H[h]], in0=z[:, k, H[h]], in1=rstd[:, H[h]], op=ALU.mult),
                     reads=[rz[k][h], rrstd[h]], writes=[rz[k][h]])
                p.op("act", lambda a, k=k, h=h: a.activation(out=z[:, k, H[h]], in_=z[:, k, H[h]], func=AF.Identity,
                                                              scale=lnp[:, gi * KC + k:gi * KC + k + 1],
                                                              bias=lnp[:, bi * KC + k:bi * KC + k + 1]),
                     reads=[rz[k][h], rlnp], writes=[rz[k][h]])
                p.op("act", lambda v, k=k, h=h: v.activation(out=xb[:, k, H[h]], in_=z[:, k, H[h]], func=AF.Copy),
                     reads=[rz[k][h]], writes=[rxb[k][h]])

    layer_norm(0, 1)

    gv = wg_d.rearrange("(kc p) n -> p kc n", p=128)
    uv = wu_d.rearrange("(kc p) n -> p kc n", p=128)
    dv = wd_d.rearrange("(fc p) n -> p fc n", p=128)
    NG = NFC // 4
    for g in range(NG):
        gslab, rg = ws.load((KC, 512), gv[:, :, g * 512:(g + 1) * 512])
        uslab, ru = ws.load((KC, 512), uv[:, :, g * 512:(g + 1) * 512])
        hb = hg[g % 2]
        rh = rhg[g % 2]
        for fi in range(4):
            for h in range(2):
                bg = bank_rr[0] % 4
                bu = (bank_rr[0] + 1) % 4
                bank_rr[0] += 2
                proj_accum(KC, lambda k, fi=fi: gslab[:, k, fi * 128:(fi + 1) * 128],
                           lambda k, h=h: xb[:, k, H[h]], lambda k, h=h: rxb[k][h], rg, bg)
                proj_accum(KC, lambda k, fi=fi: uslab[:, k, fi * 128:(fi + 1) * 128],
                           lambda k, h=h: xb[:, k, H[h]], lambda k, h=h: rxb[k][h], ru, bu)
                ti = (fi * 2 + h) % 2
                p.op("act", lambda a, bg=bg, ti=ti: a.activation(out=tmpf[ti][:], in_=P[bg][:], func=AF.Silu),
                     reads=[rP[bg]], writes=[rtmp[ti]])
                p.op("dve", lambda v, bu=bu, ti=ti, fi=fi, h=h, hb=hb: v.tensor_tensor(
                    out=hb[:, fi, H[h]], in0=P[bu][:], in1=tmpf[ti][:], op=ALU.mult),
                    reads=[rP[bu], rtmp[ti]], writes=[rh[fi][h]])
        for half in range(2):
            dslab, rd = ws.load((4, 1024), dv[:, g * 4:(g + 1) * 4, half * 1024:(half + 1) * 1024])
            for mi in range(8):
                m = half * 8 + mi
                for h in range(2):
                    bank = 4 + (bank_rr[0] % 4)
                    bank_rr[0] += 1
                    proj_accum(4, lambda k, mi=mi, dslab=dslab: dslab[:, k, mi * 128:(mi + 1) * 128],
                               lambda k, h=h, hb=hb: hb[:, k, H[h]], lambda k, h=h, rh=rh: rh[k][h], rd, bank)
                    if g == 0:
                        p.op("dve", lambda v, m=m, h=h, bank=bank: v.scalar_tensor_tensor(
                            out=z[:, m, H[h]], in0=z[:, m, H[h]], scalar=ALPHA, in1=P[bank][:],
                            op0=ALU.mult, op1=ALU.add), reads=[rP[bank], rz[m][h]], writes=[rz[m][h]])
                    else:
                        p.op("dve", lambda v, m=m, h=h, bank=bank: v.tensor_tensor(
                            out=z[:, m, H[h]], in0=z[:, m, H[h]], in1=P[bank][:], op=ALU.add),
                            reads=[rP[bank], rz[m][h]], writes=[rz[m][h]])

    layer_norm(2, 3)

    finals = []
    xo = xout_d.rearrange("(kc p) t -> p kc t", p=128)
    for m in range(KC):
        finals.append(p.dma("sp", lambda q, m=m: q.dma_start(out=xo[:, m, :], in_=z[:, m, :]), s_out,
                            reads=[rz[m][0], rz[m][1]]))
    if mode_next == "even":
        xt = xTb_d.rearrange("(kc p) t -> p kc t", p=128)
        for m in range(KC):
            finals.append(p.dma("sp", lambda q, m=m: q.dma_start(out=xt[:, m, :], in_=xb[:, m, :]), s_out,
                                reads=[rxb[m][0], rxb[m][1]]))
    if mode_next == "mla":
        finals += mla_latents(p, nc, ws, xb, rxb, P, rP, bank_rr, tmpf, rtmp, H,
                              win_d, ng_d, cs_d, lat_d, s_in2, s_out,
                              hg, rhg, mean, rstd, rmean, rrstd)
    p.emit({"sp": [finals[-1]] if False else finals})
    p.close()
    return nc


def mla_latents(p, nc, ws, xb, rxb, P, rP, bank_rr, tmpf, rtmp, H,
                win_d, ng_d, cs_d, lat_d, s_in, s_out, hg, rhg, mean, rstd, rmean, rrstd):
    finals = []
    ng = p.sbuf("ng", [128, 8], F32)
    rng_ = Res("ng")
    p.dma("sp", lambda q: q.dma_start(out=ng[:], in_=ng_d[:, :]), s_in, writes=[rng_])
    all_mean = [rmean[0], rmean[1]]
    all_rstd = [rrstd[0], rrstd[1]]
    p.dma("sp", lambda q: q.dma_start(out=mean[0:64, :], in_=cs_d[:, 0:TOK]), s_in, writes=all_mean)
    p.dma("sp", lambda q: q.dma_start(out=rstd[0:64, :], in_=cs_d[:, TOK:2 * TOK]), s_in, writes=all_rstd)
    lat = hg[0].bitcast(F32)
    latb = hg[1]
    rl0 = [r for f in rhg[0] for r in f]
    rl1 = [r for f in rhg[1] for r in f]
    ones512 = p.sbuf("ones512", [128, 128], F32)
    r512 = Res("ones512")
    p.op("dve", lambda v: v.memset(ones512[:], 1.0 / 512), writes=[r512])
    wv = win_d.rearrange("(kc p) n -> p kc n", p=128)
    for li in range(2):
        slab, rs = ws.load((KC, 512), wv[:, :, li * 512:(li + 1) * 512])
        for h in range(2):
            for c in range(4):
                bank = bank_rr[0] % 4
                bank_rr[0] += 1
                for k in range(KC):
                    p.op("pe", lambda t, k=k, c=c, h=h, bank=bank, slab=slab: t.matmul(
                        P[bank][:], lhsT=slab[:, k, c * 128:(c + 1) * 128], rhs=xb[:, k, H[h]],
                        start=(k == 0), stop=(k == KC - 1)), reads=[rs, rxb[k][h]], writes=[rP[bank]])
                p.op("act", lambda a, c=c, bank=bank: a.activation(out=lat[:, c, :], in_=P[bank][:], func=AF.Copy),
                     reads=[rP[bank]], writes=rl0)
            for c in range(4):
                ti = c % 2
                p.op("act", lambda a, c=c, ti=ti: a.activation(out=tmpf[ti][:], in_=lat[:, c, :], func=AF.Square),
                     reads=rl0, writes=[rtmp[ti]])
                p.op("pe", lambda t, c=c, h=h, ti=ti: t.matmul(P[6 + h][:], lhsT=ones512[:], rhs=tmpf[ti][:],
                                                                start=(c == 0), stop=(c == 3)),
                     reads=[r512, rtmp[ti]], writes=[rP[6 + h]])
            p.op("dve", lambda v, h=h: v.tensor_scalar(out=tmpf[2][:], in0=P[6 + h][:], scalar1=1e-6, scalar2=None, op0=ALU.add),
                 reads=[rP[6 + h]], writes=[rtmp[2]])
            p.op("act", lambda a: a.activation(out=tmpf[3][:], in_=tmpf[2][:], func=AF.Sqrt),
                 reads=[rtmp[2]], writes=[rtmp[3]])
            p.op("dve", lambda v: v.reciprocal(out=tmpf[2][:], in_=tmpf[3][:]), reads=[rtmp[3]], writes=[rtmp[2]])
            for c in range(4):
                p.op("dve", lambda v, c=c: v.tensor_tensor(out=lat[:, c, :], in0=lat[:, c, :], in1=tmpf[2][:], op=ALU.mult),
                     reads=rl0 + [rtmp[2]], writes=rl0)
                p.op("act", lambda a, c=c, li=li: a.activation(out=latb[:, c, 0:512], in_=lat[:, c, :], func=AF.Identity,
                                                               scale=ng[:, li * 4 + c:li * 4 + c + 1]),
                     reads=rl0 + [rng_], writes=rl1)
            lv = lat_d[li * 512:(li + 1) * 512, H[h]].rearrange("(c p) t -> p c t", p=128)
            finals.append(p.dma("sp", lambda q, lv=lv: q.dma_start(out=lv, in_=latb[:, :, 0:512]), s_out, reads=rl1))
    slab, rs = ws.load((KC, 64), wv[:, :, 1024:1088])
    wrot = p.sbuf("wrot", [128, KC, 64], BF16)
    rwrot = Res("wrot")
    p.op("dve", lambda v: v.tensor_scalar(out=wrot[:, :, 0:32], in0=slab[:, :, 32:64], scalar1=-1.0, scalar2=None, op0=ALU.mult),
         reads=[rs], writes=[rwrot])
    p.op("dve", lambda v: v.tensor_copy(out=wrot[:, :, 32:64], in_=slab[:, :, 0:32]), reads=[rs], writes=[rwrot])
    kr = p.sbuf("kr", [64, TOK], BF16)
    rkr = [Res("kr0"), Res("kr1")]
    for h in range(2):
        b0 = bank_rr[0] % 4
        b1 = (bank_rr[0] + 1) % 4
        bank_rr[0] += 2
        for k in range(KC):
            p.op("pe", lambda t, k=k, h=h, b0=b0: t.matmul(P[b0][0:64, :], lhsT=slab[:, k, :], rhs=xb[:, k, H[h]],
                                                            start=(k == 0), stop=(k == KC - 1)),
                 reads=[rs, rxb[k][h]], writes=[rP[b0]])
        for k in range(KC):
            p.op("pe", lambda t, k=k, h=h, b1=b1: t.matmul(P[b1][0:64, :], lhsT=wrot[:, k, :], rhs=xb[:, k, H[h]],
                                                            start=(k == 0), stop=(k == KC - 1)),
                 reads=[rwrot, rxb[k][h]], writes=[rP[b1]])
        p.op("dve", lambda v, h=h, b0=b0: v.tensor_tensor(out=tmpf[0][0:64, :], in0=P[b0][0:64, :], in1=mean[0:64, H[h]], op=ALU.mult),
             reads=[rP[b0]] + all_mean, writes=[rtmp[0]])
        p.op("dve", lambda v, h=h, b1=b1: v.tensor_tensor(out=tmpf[1][0:64, :], in0=P[b1][0:64, :],
                                                          in1=rstd[0:64, H[h]], op=ALU.mult),
             reads=[rP[b1]] + all_rstd, writes=[rtmp[1]])
        p.op("dve", lambda v, h=h: v.tensor_tensor(out=kr[:, H[h]], in0=tmpf[0][0:64, :], in1=tmpf[1][0:64, :], op=ALU.add),
             reads=[rtmp[0], rtmp[1]], writes=[rkr[h]])
    finals.append(p.dma("sp", lambda q: q.dma_start(out=lat_d[1024:1088, :], in_=kr[:]), s_out, reads=rkr))
    return finals


NQG = S // 512
NKB = S // 128
YOFF = [0, 129, 258, 0]
YBNK = [0, 0, 0, 1]


class AttnCtx:
    def __init__(self, p, P, rP, identb, rident, trib, rtri, yT_d, s_out):
        self.p, self.P, self.rP = p, P, rP
        self.identb, self.rident, self.trib, self.rtri = identb, rident, trib, rtri
        self.pt = [p.sbuf(f"pt{i}", [128, 512], BF16) for i in range(3)]
        self.rpt = [Res(f"pt{i}") for i in range(3)]
        self.yn = [p.sbuf(f"yn{i}", [128, 128], BF16) for i in range(2)]
        self.ryn = [Res(f"yn{i}") for i in range(2)]
        self.ytb = [p.sbuf(f"ytb{i}", [128, 512], BF16) for i in range(2)]
        self.rytb = [Res(f"ytb{i}") for i in range(2)]
        self.rinv = [p.sbuf(f"rinv{i}", [128, 1], F32) for i in range(2)]
        self.rrinv = [Res(f"rinv{i}") for i in range(2)]
        self.tp = p.enter(p.nc.psum_tensor("ps_tp", [128, 1024], BF16))
        self.rtp = Res("tp")
        self.yT_d, self.s_out = yT_d, s_out
        self.finals = []
        self.kpt = 0
        self.ksb = 0
        self.kyn = 0
        self.kg = 0

    def finish_block(self, src_ap, sum_ap, src_res, s, row0, g, last):
        p = self.p
        i = self.kyn % 2
        self.kyn += 1
        gb = self.kg % 2
        p.op("dve", lambda v, i=i: v.reciprocal(out=self.rinv[i][:], in_=sum_ap), reads=src_res, writes=[self.rrinv[i]])
        p.op("dve", lambda v, i=i: v.tensor_scalar(out=self.yn[i][:], in0=src_ap, scalar1=self.rinv[i][:, 0:1], scalar2=None,
                                                   op0=ALU.mult),
             reads=list(src_res) + [self.rrinv[i]], writes=[self.ryn[i]])
        p.op("pe", lambda t, i=i: t.transpose(out=self.tp[:, 0:128], in_=self.yn[i][:], identity=self.identb[:]),
             reads=[self.ryn[i], self.rident], writes=[self.rtp])
        p.op("dve", lambda v, s=s, gb=gb: v.tensor_copy(out=self.ytb[gb][:, s * 128:(s + 1) * 128], in_=self.tp[:, 0:128]),
             reads=[self.rtp], writes=[self.rytb[gb]])
        if last:
            self.finals.append(p.dma("sp", lambda q, gb=gb: q.dma_start(
                out=self.yT_d[row0:row0 + 128, g * 512:(g + 1) * 512], in_=self.ytb[gb][:]), self.s_out,
                reads=[self.rytb[gb]]))
            self.kg += 1


def dense_causal_sweep(cx, qk_fn, bias_fn, V, rV, scale, row0):
    p, P, rP = cx.p, cx.P, cx.rP
    steps = [(g, J) for g in range(NQG) for J in range(4 * g + 4)]

    def part_a(g, J):
        r = J - 4 * g
        c0 = max(r, 0) * 128
        sb = cx.ksb % 2
        cx.ksb += 1
        terms = qk_fn(J, g, c0)
        for ti, (lh, rh, rd) in enumerate(terms):
            p.op("pe", lambda t, lh=lh, rh=rh, ti=ti, sb=sb, c0=c0, nt=len(terms), r=r: t.matmul(
                P[sb][:, c0:512], lhsT=lh, rhs=rh, start=(ti == 0), stop=(ti == nt - 1 and r < 0)),
                reads=rd, writes=[rP[sb]])
        if r >= 0:
            p.op("pe", lambda t, sb=sb, c0=c0: t.matmul(P[sb][:, c0:c0 + 128], lhsT=cx.identb[:], rhs=cx.trib[:],
                                                        start=False, stop=True),
                 reads=[cx.rident, cx.rtri], writes=[rP[sb]])
        pi = cx.kpt % 3
        cx.kpt += 1
        b = bias_fn(g, J) if bias_fn else None
        if b is None:
            p.op("act", lambda a, pi=pi, sb=sb, c0=c0: a.activation(out=cx.pt[pi][:, c0:512], in_=P[sb][:, c0:512],
                                                                      func=AF.Exp, scale=scale),
                 reads=[rP[sb]], writes=[cx.rpt[pi]])
        else:
            bap, brd = b
            p.op("act", lambda a, pi=pi, sb=sb, c0=c0, bap=bap: a.activation(
                out=cx.pt[pi][:, c0:512], in_=P[sb][:, c0:512], func=AF.Exp, scale=scale, bias=bap),
                reads=[rP[sb]] + brd, writes=[cx.rpt[pi]])
        return pi

    def part_b(g, J, pi):
        r = J - 4 * g
        for s in range(max(r, 0), 4):
            bank = 3 + s
            p.op("pe", lambda t, pi=pi, s=s, bank=bank, J=J, g=g: t.matmul(
                P[bank][:, 0:129], lhsT=cx.pt[pi][:, s * 128:(s + 1) * 128], rhs=V[:, J, 0:129],
                start=(J == 0), stop=(J == 4 * g + s)),
                reads=[cx.rpt[pi], rV[J // 4]], writes=[rP[bank]])
            if J == 4 * g + s:
                cx.finish_block(P[bank][:, 0:128], P[bank][:, 128:129], [rP[bank]], s, row0, g, s == 3)

    pis = [part_a(*steps[0])]
    for i, (g, J) in enumerate(steps):
        if i + 1 < len(steps):
            pis.append(part_a(*steps[i + 1]))
        part_b(g, J, pis[i])


def build_AE(x_f32=False, stop=None, flags="qkvgABFrlhmx"):
    nc = bass.Bass("TRN2", target_bir_lowering=False)
    dt = nc.dram_tensor
    xT_d = dt("xT", [NCORES, D, TOK], F32 if x_f32 else BF16, kind="ExternalInput").ap()
    wh_d = dt("wh", [D, 770], F32, kind="ExternalInput").ap()
    cst_d = dt("cst", [128, 576], F32, kind="ExternalInput").ap()
    gb_d = dt("gb", [128, 258], F32, kind="ExternalInput").ap()
    yT_d = dt("yT", [256, S], BF16, kind="ExternalOutput").ap()

    p = Prog(nc)
    s_in = p.new_dma_sem("s_in")
    s_w = p.new_dma_sem("s_w")
    s_out = p.new_dma_sem("s_out")
    s_x = [p.new_dma_sem(f"s_x{i}") for i in range(2)]
    s_xe = [p.new_dma_sem(f"s_xe{i}", exclusive=True) for i in range(2)]
    cst = p.sbuf("cst", [128, 576], F32)
    rcst = Res("cst")
    p.dma("sp", lambda q: q.dma_start(out=cst[:], in_=cst_d[:, :]), s_in, writes=[rcst])
    gb = p.sbuf("gb", [128, 258], F32)
    rgb = Res("gb")
    p.dma("sp", lambda q: q.dma_start(out=gb[:], in_=gb_d[:, :]), s_in, writes=[rgb])
    U, LST, E127, IDF, TRIF = cst[:, 0:128], cst[0:64, 128:192], cst[:, 192:320], cst[:, 320:448], cst[:, 448:576]
    identb = p.sbuf("identb", [128, 128], BF16)
    trib = p.sbuf("trib", [128, 128], BF16)
    d0b = p.sbuf("d0b", [128, 128], BF16)
    d1b = p.sbuf("d1b", [128, 128], BF16)
    rident, rtri, rd0, rd1 = Res("identb"), Res("trib"), Res("d0b"), Res("d1b")
    tmpc = p.sbuf("tmpc", [128, 128], F32)
    rtmpc = Res("tmpc")
    p.op("dve", lambda v: v.tensor_copy(out=identb[:], in_=IDF), reads=[rcst], writes=[rident])
    p.op("dve", lambda v: v.tensor_copy(out=trib[:], in_=TRIF), reads=[rcst], writes=[rtri])
    SQ = math.sqrt(HD)
    p.op("dve", lambda v: v.tensor_scalar(out=tmpc[:], in0=gb[:, 0:128], scalar1=gb[:, 256:257], scalar2=SQ,
                                          op0=ALU.subtract, op1=ALU.mult), reads=[rgb], writes=[rtmpc])
    p.op("dve", lambda v: v.tensor_tensor(out=d0b[:], in0=tmpc[:], in1=TRIF, op=ALU.add), reads=[rtmpc, rcst], writes=[rd0])
    p.op("dve", lambda v: v.tensor_scalar(out=d1b[:], in0=gb[:, 128:256], scalar1=gb[:, 256:257], scalar2=SQ,
                                          op0=ALU.subtract, op1=ALU.mult), reads=[rgb], writes=[rd1])
    Wh = p.sbuf("Wh", [128, KC, 770], BF16)
    rWh = Res("Wh")
    whv = wh_d.rearrange("(kc p) n -> p kc n", p=128)
    for k4 in range(4):
        p.dma("pool", lambda q, k4=k4: q.dma_start(out=Wh[:, k4 * 4:(k4 + 1) * 4, :], in_=whv[:, k4 * 4:(k4 + 1) * 4, :]),
              s_w, writes=[rWh])
    QAT = p.sbuf("QAT", [128, S], BF16)
    KAT = p.sbuf("KAT", [128, S], BF16)
    QBT = p.sbuf("QBT", [128, S], BF16)
    KBT = p.sbuf("KBT", [128, S], BF16)
    VA = p.sbuf("VA", [128, NKB, 130], BF16)
    VB = p.sbuf("VB", [128, NKB, 130], BF16)
    rQA = [Res(f"QA{i}") for i in range(NQG)]
    rKA = [Res(f"KA{i}") for i in range(NQG)]
    rQB = [Res(f"QB{i}") for i in range(NQG)]
    rKB = [Res(f"KB{i}") for i in range(NQG)]
    rVA = [Res(f"VA{i}") for i in range(NQG)]
    rVB = [Res(f"VB{i}") for i in range(NQG)]
    Fl = p.sbuf("Fl", [128, NKB], F32)
    rF = Res("F")
    kmT = p.sbuf("kmT", [128, 32], F32)
    rkm = Res("kmT")
    sel = p.sbuf("sel", [128, NKB, 32], F32)
    rsel = [Res(f"sel{i}") for i in range(NKB)]
    gm = p.sbuf("gm", [128, 32], F32)
    rgm = Res("gm")
    m8 = p.sbuf("m8", [128, 8], F32)
    rm8 = Res("m8")
    qlo = [p.sbuf(f"qlo{i}", [128, 512], BF16) for i in range(2)]
    kmh = p.sbuf("kmh", [128, 32], BF16)
    kml = p.sbuf("kml", [128, 32], BF16)
    rkmh = Res("kmh")
    rqf = [Res(f"qf{i}") for i in range(2)]
    xc = [p.sbuf(f"xc{i}", [128, KC, 512], BF16) for i in range(2)]
    rxc = [Res(f"xc{i}") for i in range(2)]
    P = [p.psum(f"P{i}", [128, 512], F32) for i in range(7)]
    rP = [Res(f"P{i}") for i in range(7)]
    p.op("dve", lambda v: v.memset(kmT[:], 0.0), writes=[rkm])
    p.op("dve", lambda v: v.memset(gm[:], -1e30), writes=[rgm])
    for i, (Vt, rVt) in enumerate(((VA, rVA), (VB, rVB))):
        p.op("pool", lambda g_, Vt=Vt: g_.memset(Vt[:, :, 128:130], 1.0), writes=rVt)

    rr = [0]

    def nb():
        b = rr[0] % 7
        rr[0] += 1
        return b

    for tc in range(NQG):
        xi = tc % 2
        src = xT_d[tc // 2, :, (tc % 2) * 512:(tc % 2 + 1) * 512].rearrange("(kc p) t -> p kc t", p=128)
        if x_f32:
            for k4 in range(4):
                ev_ = p.dma("pool", lambda q, xi=xi, src=src, k4=k4: q.dma_start(out=xc[xi][:, k4 * 4:(k4 + 1) * 4, :],
                                                                            in_=src[:, k4 * 4:(k4 + 1) * 4, :]),
                            s_xe[xi], writes=[rxc[xi]])
        else:
            p.dma("sp", lambda q, xi=xi, src=src: q.dma_start(out=xc[xi][:], in_=src), s_x[xi], writes=[rxc[xi]])
        tsl = slice(tc * 512, (tc + 1) * 512)
        for (c0, dst, rdst, extra) in ((0, QAT, rQA, None), (128, KAT, rKA, None), (384, KBT, rKB, "km"), (256, QBT, rQB, "qf")):
            b = nb()
            for k in range(KC):
                p.op("pe", lambda t, k=k, b=b, c0=c0, xi=xi: t.matmul(P[b][:], lhsT=Wh[:, k, c0:c0 + 128], rhs=xc[xi][:, k, :],
                                                                      start=(k == 0), stop=(k == KC - 1)),
                     reads=[rWh, rxc[xi]], writes=[rP[b]])
            if extra is None:
                p.op("act", lambda a, b=b, dst=dst, tsl=tsl: a.activation(out=dst[:, tsl], in_=P[b][:], func=AF.Copy),
                     reads=[rP[b]], writes=[rdst[tc]])
            else:
                p.op("dve", lambda v, b=b, dst=dst, tsl=tsl: v.tensor_copy(out=dst[:, tsl], in_=P[b][:]),
                     reads=[rP[b]], writes=[rdst[tc]])
            if extra == "km" and "r" in flags:
                p.op("dve", lambda v, b=b, tc=tc: v.tensor_reduce(out=kmT[:, 2 * tc:2 * tc + 2],
                                                                  in_=P[b][:].rearrange("p (n k) -> p n k", n=2),
                                                                  axis=AX.X, op=ALU.add),
                     reads=[rP[b]], writes=[rkm])
            if extra == "qf" and "l" in flags:
                p.op("dve", lambda v, b=b, xi=xi, tsl=tsl: v.tensor_tensor(out=qlo[xi][:], in0=P[b][:], in1=QBT[:, tsl], op=ALU.subtract),
                     reads=[rP[b], rQB[tc]], writes=[rqf[xi]])
        for tb in range(4 if "v" in flags else 0):
            b = nb()
            blk = tc * 4 + tb
            for k in range(KC):
                p.op("pe", lambda t, k=k, b=b, tb=tb, xi=xi: t.matmul(P[b][:, 0:257], lhsT=xc[xi][:, k, tb * 128:(tb + 1) * 128],
                                                                      rhs=Wh[:, k, 512:769], start=(k == 0), stop=(k == KC - 1)),
                     reads=[rWh, rxc[xi]], writes=[rP[b]])
            if "A" in flags:
                p.op("dve", lambda v, b=b, blk=blk: v.tensor_copy(out=VA[:, blk, 0:128], in_=P[b][:, 0:128]),
                     reads=[rP[b]], writes=[rVA[tc]])
            if "B" in flags:
                p.op("dve", lambda v, b=b, blk=blk: v.tensor_copy(out=VB[:, blk, 0:128], in_=P[b][:, 129:257]),
                     reads=[rP[b]], writes=[rVB[tc]])
            if "F" in flags:
                p.op("dve", lambda v, b=b, blk=blk: v.tensor_copy(out=Fl[:, blk:blk + 1], in_=P[b][:, 128:129]),
                     reads=[rP[b]], writes=[rF])
        for s in range(4):
            qb = tc * 4 + s
            own = qb // 2
            if own == 0 or "g" not in flags:
                continue
            b = nb()
            if s in (0, 2) and "h" in flags:
                p.op("dve", lambda v: v.tensor_copy(out=kmh[:], in_=kmT[:]), reads=[rkm], writes=[rkmh])
                p.op("dve", lambda v: v.tensor_tensor(out=kml[:], in0=kmT[:], in1=kmh[:], op=ALU.subtract),
                     reads=[rkm, rkmh], writes=[rkmh])
            qsl = slice(tc * 512 + s * 128, tc * 512 + (s + 1) * 128)
            if "m" not in flags:
                continue
            p.op("pe", lambda t, b=b, qsl=qsl: t.matmul(P[b][:, 0:32], lhsT=QBT[:, qsl], rhs=kmh[:, :], start=True, stop=False),
                 reads=[rQB[tc], rkmh], writes=[rP[b]])
            p.op("pe", lambda t, b=b, qsl=qsl: t.matmul(P[b][:, 0:32], lhsT=QBT[:, qsl], rhs=kml[:, :], start=False, stop=False),
                 reads=[rQB[tc], rkmh], writes=[rP[b]])
            p.op("pe", lambda t, b=b, s=s, xi=xi: t.matmul(P[b][:, 0:32], lhsT=qlo[xi][:, s * 128:(s + 1) * 128], rhs=kmh[:, :],
                                                            start=False, stop=True),
                 reads=[rqf[xi], rkmh], writes=[rP[b]])
            if "x" not in flags:
                continue
            p.op("dve", lambda v, b=b, own=own: v.tensor_copy(out=gm[:, 0:own], in_=P[b][:, 0:own]), reads=[rP[b]], writes=[rgm])
            p.op("dve", lambda v: v.max(out=m8[:], in_=gm[:]), reads=[rgm], writes=[rm8])
            p.op("dve", lambda v, qb=qb, own=own: v.tensor_scalar(out=sel[:, qb, 0:own], in0=gm[:, 0:own], scalar1=m8[:, 2:3],
                                                                  scalar2=None, op0=ALU.is_ge),
                 reads=[rgm, rm8], writes=[rsel[qb]])

    if stop == "proj":
        ev = p.dma("sp", lambda q: q.dma_start(out=yT_d[0:128, :], in_=QAT[:, :]), None, reads=rQA + rKA + rQB + rKB + rVA + rVB + rsel + [rF])
        p.emit({"sp": [ev]})
        p.close()
        return nc
    nbf = p.sbuf("nbf", [128, 1], F32)
    rnbf = Res("nbf")
    p.op("dve", lambda v: v.tensor_scalar(out=nbf[:], in0=gb[:, 257:258], scalar1=-1.0, scalar2=None, op0=ALU.mult),
         reads=[rgb], writes=[rnbf])
    ef = p.sbuf("ef", [128, NKB], F32)
    lf = p.sbuf("lf", [128, NKB], F32)
    ref_, rlf = Res("ef"), Res("lf")
    p.op("act", lambda a: a.activation(out=ef[:], in_=Fl[:], func=AF.Exp, scale=-1.0, bias=nbf[:, 0:1]),
         reads=[rF, rnbf], writes=[ref_])
    p.op("act", lambda a: a.activation(out=lf[:], in_=ef[:], func=AF.Ln, bias=1.0, scale=1.0), reads=[ref_], writes=[rlf])
    p.op("dve", lambda v: v.tensor_scalar(out=lf[:], in0=lf[:], scalar1=-1.0, scalar2=None, op0=ALU.mult),
         reads=[rlf], writes=[rlf])
    onesf = p.sbuf("onesf", [128, 128], F32)
    ronesf = Res("onesf")
    p.op("dve", lambda v: v.memset(onesf[:], 1.0), writes=[ronesf])
    b0 = nb()
    p.op("pe", lambda t: t.matmul(P[b0][0:64, 0:1], lhsT=lf[:, :], rhs=onesf[:, 0:1], start=True, stop=True),
         reads=[rlf, ronesf], writes=[rP[b0]])
    totbc = p.sbuf("totbc", [64, 128], F32)
    rtot = Res("totbc")
    p.op("dve", lambda v: v.tensor_scalar(out=totbc[:], in0=onesf[0:64, :], scalar1=P[b0][0:64, 0:1], scalar2=None, op0=ALU.mult),
         reads=[rP[b0], ronesf], writes=[rtot])
    b1 = nb()
    p.op("pe", lambda t: t.matmul(P[b1][:, 0:NKB], lhsT=U, rhs=lf[:, :], start=True, stop=False),
         reads=[rcst, rlf], writes=[rP[b1]])
    p.op("pe", lambda t: t.matmul(P[b1][:, 0:NKB], lhsT=totbc[:, :], rhs=LST, start=False, stop=True),
         reads=[rcst, rtot], writes=[rP[b1]])
    csb = p.sbuf("csb", [128, NKB], F32)
    negc = p.sbuf("negc", [128, NKB], F32)
    rcsb, rnegc = Res("csb"), Res("negc")
    p.op("dve", lambda v: v.tensor_copy(out=csb[:], in_=P[b1][:, 0:NKB]), reads=[rP[b1]], writes=[rcsb])
    p.op("dve", lambda v: v.tensor_scalar(out=negc[:], in0=P[b1][:, 0:NKB], scalar1=-1.0, scalar2=None, op0=ALU.mult),
         reads=[rP[b1]], writes=[rnegc])
    cendc = p.sbuf("cendc", [128, NQG], F32)
    rcendc = Res("cendc")
    p.op("dve", lambda v: v.tensor_copy(out=cendc[:], in_=csb[:].rearrange("p (g f) -> p g f", f=4)[:, :, 3]),
         reads=[rcsb], writes=[rcendc])
    b2 = nb()
    p.op("pe", lambda t: t.matmul(P[b2][:, 0:NQG], lhsT=E127, rhs=cendc[:, :], start=True, stop=True),
         reads=[rcst, rcendc], writes=[rP[b2]])
    cend = p.sbuf("cend", [128, NQG], F32)
    rcend = Res("cend")
    p.op("dve", lambda v: v.tensor_copy(out=cend[:], in_=P[b2][:, 0:NQG]), reads=[rP[b2]], writes=[rcend])
    ball = p.sbuf("ball", [128, NQG, NKB], F32)
    rball = Res("ball")
    for g in range(NQG):
        p.op("dve", lambda v, g=g: v.tensor_scalar(out=ball[:, g, :], in0=negc[:, :], scalar1=cend[:, g:g + 1], scalar2=None,
                                                   op0=ALU.add), reads=[rnegc, rcend], writes=[rball])

    cx = AttnCtx(p, P, rP, identb, rident, trib, rtri, yT_d, s_out)
    SCALE = HD ** -0.5

    def qk_fox(J, g, c0):
        return [(KAT[:, J * 128:(J + 1) * 128], QAT[:, g * 512 + c0:(g + 1) * 512], [rKA[J // 4], rQA[g]])]

    def bias_fox(g, J):
        return (ball[:, g, J:J + 1], [rball])

    dense_causal_sweep(cx, qk_fox, bias_fox, VA, rVA, SCALE, 0)

    if stop == "fox":
        p.emit({"sp": cx.finals})
        p.close()
        return nc
    acc = p.sbuf("acc", [128, 4, 129], F32)
    racc = [Res(f"acc{s}") for s in range(4)]
    msteps = [(g, n, J) for g in range(NQG) for n in range(2 * g + 2) for J in (2 * n, 2 * n + 1)]
    started = {}

    def moba_a(g, n, J):
        r = J - 4 * g
        c0 = max(r, 0) * 128
        sb = cx.ksb % 2
        cx.ksb += 1
        adds = []
        for s in range(max(r, 0), 4):
            if J == 4 * g + s:
                adds.append((s, d0b, rd0))
            elif J == 4 * g + s - 1:
                adds.append((s, d1b, rd1))
        p.op("pe", lambda t, J=J, g=g, c0=c0, sb=sb, na=len(adds): t.matmul(
            P[sb][:, c0:512], lhsT=KBT[:, J * 128:(J + 1) * 128], rhs=QBT[:, g * 512 + c0:(g + 1) * 512],
            start=True, stop=(na == 0)), reads=[rKB[J // 4], rQB[g]], writes=[rP[sb]])
        for ai, (s, dt_, rdt) in enumerate(adds):
            p.op("pe", lambda t, s=s, dt_=dt_, sb=sb, ai=ai, na=len(adds): t.matmul(
                P[sb][:, s * 128:(s + 1) * 128], lhsT=identb[:], rhs=dt_[:], start=False, stop=(ai == na - 1)),
                reads=[rident, rdt], writes=[rP[sb]])
        pi = cx.kpt % 3
        cx.kpt += 1
        p.op("act", lambda a, pi=pi, sb=sb, c0=c0: a.activation(out=cx.pt[pi][:, c0:512], in_=P[sb][:, c0:512],
                                                                  func=AF.Exp, scale=SCALE),
             reads=[rP[sb]], writes=[cx.rpt[pi]])
        return pi

    def moba_b(g, n, J, pi):
        r = J - 4 * g
        yb = [3 + 2 * (n % 2), 4 + 2 * (n % 2)]
        for s in range(max(r, 0), 4):
            bi = YBNK[s]
            bank = yb[bi]
            st = not started.get((g, n, bi), False)
            started[(g, n, bi)] = True
            p.op("pe", lambda t, pi=pi, s=s, bank=bank, J=J, st=st: t.matmul(
                P[bank][:, YOFF[s]:YOFF[s] + 129], lhsT=cx.pt[pi][:, s * 128:(s + 1) * 128], rhs=VB[:, J, 0:129],
                start=st, stop=True, skip_group_check=True),
                reads=[cx.rpt[pi], rVB[J // 4]], writes=[rP[bank]])
        if J != 2 * n + 1:
            return
        smin_n = max(2 * n - 4 * g, 0)
        for s in range(smin_n, 4):
            qb = 4 * g + s
            own = qb // 2
            bank = yb[YBNK[s]]
            src = P[bank][:, YOFF[s]:YOFF[s] + 129]
            if n == own:
                if n == 0:
                    p.op("dve", lambda v, s=s, src=src: v.tensor_copy(out=acc[:, s, :], in_=src),
                         reads=[rP[bank]], writes=[racc[s]])
                else:
                    p.op("dve", lambda v, s=s, src=src: v.tensor_tensor(out=acc[:, s, :], in0=acc[:, s, :], in1=src, op=ALU.add),
                         reads=[rP[bank], racc[s]], writes=[racc[s]])
                cx.finish_block(acc[:, s, 0:128], acc[:, s, 128:129], [racc[s]], s, 128, g, s == 3)
            elif n == 0:
                p.op("dve", lambda v, s=s, src=src, qb=qb: v.tensor_scalar(out=acc[:, s, :], in0=src, scalar1=sel[:, qb, 0:1],
                                                                          scalar2=None, op0=ALU.mult),
                     reads=[rP[bank], rsel[qb]], writes=[racc[s]])
            else:
                p.op("dve", lambda v, s=s, src=src, qb=qb, n=n: v.scalar_tensor_tensor(
                    out=acc[:, s, :], in0=src, scalar=sel[:, qb, n:n + 1], in1=acc[:, s, :], op0=ALU.mult, op1=ALU.add),
                    reads=[rP[bank], rsel[qb], racc[s]], writes=[racc[s]])

    mp = [moba_a(*msteps[0])]
    for i, st_ in enumerate(msteps):
        if i + 1 < len(msteps):
            mp.append(moba_a(*msteps[i + 1]))
        moba_b(*st_, mp[i])

    p.emit({"sp": cx.finals})
    p.close()
    return nc


def _t5_bucket(rel):
    n = np.maximum(rel, 0)
    nf = np.maximum(n, 1).astype(np.float32)
    large = 16 + (np.log(nf / np.float32(16)) / np.float32(math.log(128 / 16)) * np.float32(16)).astype(np.int32)
    large = np.minimum(large, 31)
    return np.where(n < 16, n, large)


def consts_AE():
    j = np.arange(128)
    cst = np.zeros((128, 576), np.float32)
    cst[:, 0:128] = (j[:, None] <= j[None, :])
    b = np.arange(64)
    cst[0:64, 128:192] = (b[:, None] < b[None, :])
    cst[127, 192:320] = 1.0
    cst[:, 320:448] = np.eye(128)
    cst[:, 448:576] = np.where(j[:, None] <= j[None, :], 0.0, NEG)
    return cst


def gb_AE(rel_bias, b_f, head):
    j = np.arange(128)
    rel0 = j[None, :] - j[:, None]
    gbm = np.zeros((128, 258), np.float32)
    col = rel_bias[:, head]
    gbm[:, 0:128] = col[_t5_bucket(rel0)]
    gbm[:, 128:256] = col[_t5_bucket(128 + rel0)]
    gbm[:, 256] = col[31]
    gbm[:, 257] = b_f[head]
    return gbm


def build_AO():
    nc = bass.Bass("TRN2", target_bir_lowering=False)
    dt = nc.dram_tensor
    lat_d = dt("lat", [NCORES, MLA_IN, TOK], BF16, kind="ExternalInput").ap()
    wuq_d = dt("wuq", [512, 384], F32, kind="ExternalInput").ap()
    wukv_d = dt("wukv", [512, 512], F32, kind="ExternalInput").ap()
    cs_d = dt("cs", [64, 2 * S], F32, kind="ExternalInput").ap()
    cst_d = dt("cst", [128, 576], F32, kind="ExternalInput").ap()
    yT_d = dt("yT", [256, S], BF16, kind="ExternalOutput").ap()

    p = Prog(nc)
    cst = p.sbuf("cst", [128, 576], F32)
    rcst = Res("cst")
    p.dma("sp", lambda q: q.dma_start(out=cst[:], in_=cst_d[:, :]), None, writes=[rcst])
    identb = p.sbuf("identb", [128, 128], BF16)
    trib = p.sbuf("trib", [128, 128], BF16)
    rident, rtri = Res("identb"), Res("trib")
    p.op("dve", lambda v: v.tensor_copy(out=identb[:], in_=cst[:, 320:448]), reads=[rcst], writes=[rident])
    p.op("dve", lambda v: v.tensor_copy(out=trib[:], in_=cst[:, 448:576]), reads=[rcst], writes=[rtri])
    Wq = p.sbuf("Wq", [128, 4, 384], BF16)
    Wkv = p.sbuf("Wkv", [128, 4, 512], BF16)
    Wrot = p.sbuf("Wrot", [128, 2, 4, 64], BF16)
    rWq, rWkv, rWrot = Res("Wq"), Res("Wkv"), Res("Wrot")
    s_w = p.new_dma_sem("s_w", exclusive=True)
    p.dma("pool", lambda q: q.dma_start(out=Wq[:], in_=wuq_d.rearrange("(kc p) n -> p kc n", p=128)), s_w, writes=[rWq])
    s_w2 = p.new_dma_sem("s_w2", exclusive=True)
    p.dma("pool", lambda q: q.dma_start(out=Wkv[:], in_=wukv_d.rearrange("(kc p) n -> p kc n", p=128)), s_w2, writes=[rWkv])
    for h in range(2):
        r0 = h * 192 + 128
        p.op("dve", lambda v, h=h, r0=r0: v.tensor_scalar(out=Wrot[:, h, :, 0:32], in0=Wq[:, :, r0 + 32:r0 + 64], scalar1=-1.0,
                                                          scalar2=None, op0=ALU.mult), reads=[rWq], writes=[rWrot])
        p.op("dve", lambda v, h=h, r0=r0: v.tensor_copy(out=Wrot[:, h, :, 32:64], in_=Wq[:, :, r0:r0 + 32]), reads=[rWq], writes=[rWrot])
    QN = p.sbuf("QN", [128, S], BF16)
    KN = p.sbuf("KN", [128, S], BF16)
    QR = p.sbuf("QR", [64, S], BF16)
    KR = p.sbuf("KR", [64, S], BF16)
    V = p.sbuf("V", [128, NKB, 130], BF16)
    rQN = [Res(f"QN{i}") for i in range(NQG)]
    rKN = [Res(f"KN{i}") for i in range(NQG)]
    rQR = [Res(f"QR{i}") for i in range(NQG)]
    rKR = [Res(f"KR{i}") for i in range(NQG)]
    rV = [Res(f"V{i}") for i in range(NQG)]
    latc = [p.sbuf(f"latc{i}", [128, 8, 512], BF16) for i in range(2)]
    rlatc = [Res(f"latc{i}") for i in range(2)]
    csc = [p.sbuf(f"csc{i}", [64, 2, 512], F32) for i in range(2)]
    rcsc = [Res(f"csc{i}") for i in range(2)]
    t0 = p.sbuf("t0", [64, 512], F32)
    t1 = p.sbuf("t1", [64, 512], F32)
    rt0, rt1 = Res("t0"), Res("t1")
    P = [p.psum(f"P{i}", [128, 512], F32) for i in range(7)]
    rP = [Res(f"P{i}") for i in range(7)]
    p.op("pool", lambda g_: g_.memset(V[:, :, 128:130], 1.0), writes=rV)
    cx = AttnCtx(p, P, rP, identb, rident, trib, rtri, yT_d, None)
    SCALE = 192.0 ** -0.5
    rr = [0]

    def nb():
        b = rr[0] % 3
        rr[0] += 1
        return b

    kq = 0
    for h in range(2):
        for tc in range(NQG):
            xi = kq % 2
            kq += 1
            r_, c_ = tc // 2, (tc % 2) * 512
            tsl = slice(tc * 512, (tc + 1) * 512)
            p.dma("sp", lambda q, xi=xi, r_=r_, c_=c_: q.dma_start(
                out=latc[xi][:], in_=lat_d[r_, 0:1024, c_:c_ + 512].rearrange("(c p) t -> p c t", p=128)), None,
                writes=[rlatc[xi]])
            p.dma("sp", lambda q, xi=xi, tsl=tsl: q.dma_start(out=csc[xi][:, 0, :], in_=cs_d[:, tsl]), None, writes=[rcsc[xi]])
            p.dma("sp", lambda q, xi=xi, tc=tc: q.dma_start(out=csc[xi][:, 1, :], in_=cs_d[:, S + tc * 512:S + (tc + 1) * 512]), None,
                  writes=[rcsc[xi]])
            if h == 0:
                p.dma("sp", lambda q, r_=r_, c_=c_, tsl=tsl: q.dma_start(out=KR[:, tsl], in_=lat_d[r_, 1024:1088, c_:c_ + 512]), None,
                      writes=[rKR[tc]])
            for (W, rW, c0, kof, dst, rdst) in ((Wq, rWq, h * 192, 0, QN, rQN), (Wkv, rWkv, h * 256, 4, KN, rKN)):
                b = nb()
                for k in range(4):
                    p.op("pe", lambda t, k=k, b=b, W=W, c0=c0, kof=kof, xi=xi: t.matmul(
                        P[b][:], lhsT=W[:, k, c0:c0 + 128], rhs=latc[xi][:, kof + k, :], start=(k == 0), stop=(k == 3)),
                        reads=[rW, rlatc[xi]], writes=[rP[b]])
                p.op("act", lambda a, b=b, dst=dst, tsl=tsl: a.activation(out=dst[:, tsl], in_=P[b][:], func=AF.Copy),
                     reads=[rP[b]], writes=[rdst[tc]])
            b0, b1 = nb(), nb()
            for k in range(4):
                p.op("pe", lambda t, k=k, b0=b0, h=h, xi=xi: t.matmul(P[b0][0:64, :], lhsT=Wq[:, k, h * 192 + 128:h * 192 + 192],
                                                                       rhs=latc[xi][:, k, :], start=(k == 0), stop=(k == 3)),
                     reads=[rWq, rlatc[xi]], writes=[rP[b0]])
            for k in range(4):
                p.op("pe", lambda t, k=k, b1=b1, h=h, xi=xi: t.matmul(P[b1][0:64, :], lhsT=Wrot[:, h, k, :],
                                                                       rhs=latc[xi][:, k, :], start=(k == 0), stop=(k == 3)),
                     reads=[rWrot, rlatc[xi]], writes=[rP[b1]])
            p.op("dve", lambda v, b0=b0, xi=xi: v.tensor_tensor(out=t0[:], in0=P[b0][0:64, :], in1=csc[xi][:, 0, :], op=ALU.mult),
                 reads=[rP[b0], rcsc[xi]], writes=[rt0])
            p.op("dve", lambda v, b1=b1, xi=xi: v.tensor_tensor(out=t1[:], in0=P[b1][0:64, :], in1=csc[xi][:, 1, :], op=ALU.mult),
                 reads=[rP[b1], rcsc[xi]], writes=[rt1])
            p.op("dve", lambda v, tsl=tsl: v.tensor_tensor(out=QR[:, tsl], in0=t0[:], in1=t1[:], op=ALU.add),
                 reads=[rt0, rt1], writes=[rQR[tc]])
            for tb in range(4):
                b = nb()
                blk = tc * 4 + tb
                for k in range(4):
                    p.op("pe", lambda t, k=k, b=b, tb=tb, h=h, xi=xi: t.matmul(
                        P[b][:, 0:128], lhsT=latc[xi][:, 4 + k, tb * 128:(tb + 1) * 128], rhs=Wkv[:, k, h * 256 + 128:h * 256 + 256],
                        start=(k == 0), stop=(k == 3)), reads=[rWkv, rlatc[xi]], writes=[rP[b]])
                p.op("dve", lambda v, b=b, blk=blk: v.tensor_copy(out=V[:, blk, 0:128], in_=P[b][:, 0:128]),
                     reads=[rP[b]], writes=[rV[tc]])

        def qk_mla(J, g, c0):
            return [(KN[:, J * 128:(J + 1) * 128], QN[:, g * 512 + c0:(g + 1) * 512], [rKN[J // 4], rQN[g]]),
                    (KR[:, J * 128:(J + 1) * 128], QR[:, g * 512 + c0:(g + 1) * 512], [rKR[J // 4], rQR[g]])]

        dense_causal_sweep(cx, qk_mla, None, V, rV, SCALE, h * 128)

    p.emit({"sp": cx.finals})
    p.close()
    return nc


_NC_CACHE = {}


def _prog(name, fn):
    if name not in _NC_CACHE:
        _NC_CACHE[name] = fn()
    return _NC_CACHE[name]


def _lay128(v):
    return np.ascontiguousarray(v.reshape(-1, 128).T)


def _run(nc, in_maps):
    res = run_bass_kernel_spmd(nc, in_maps, core_ids=list(range(NCORES)))
    return res.results


def kernel(x, ab_w_in, ab_forget_bias, ab_w_out, rel_bias, mla_w_in, mla_q_norm, mla_kv_norm, mla_w_uq,
           mla_w_ukv, mla_w_out, ffn_w_gate, ffn_w_up, ffn_w_down, ln_g, ln_b):
    f32 = np.float32
    x = np.asarray(x, f32)
    xT = np.ascontiguousarray(x[0].T)
    xres = [np.ascontiguousarray(xT[:, c * TOK:(c + 1) * TOK]) for c in range(NCORES)]
    cstAE = consts_AE()
    inv = (np.float32(10000.0) ** (-np.arange(0, 64, 2, dtype=f32) / np.float32(64))).astype(f32)
    ang = np.arange(S, dtype=f32)[:, None] * inv[None, :]
    cos, sin = np.cos(ang).astype(f32), np.sin(ang).astype(f32)
    csT = np.ascontiguousarray(np.concatenate([np.concatenate([cos, cos], 1).T, np.concatenate([sin, sin], 1).T], axis=1))

    gathered = None
    for layer in range(DEPTH):
        j = layer // 2
        if layer % 2 == 0:
            if layer == 0:
                nc = _prog("AE32", lambda: build_AE(x_f32=True))
                gathered = np.ascontiguousarray(xT.reshape(D, NCORES, TOK).transpose(1, 0, 2))
            else:
                nc = _prog("AE16", lambda: build_AE(x_f32=False))
            w = np.asarray(ab_w_in[j], f32)
            maps = []
            for c in range(NCORES):
                cols = [w[:, c * 128:(c + 1) * 128], w[:, 1024 + c * 128:1024 + (c + 1) * 128],
                        w[:, 3080 + c * 128:3080 + (c + 1) * 128], w[:, 4104 + c * 128:4104 + (c + 1) * 128],
                        w[:, 2048 + c * 128:2048 + (c + 1) * 128], w[:, 3072 + c:3072 + c + 1],
                        w[:, 5128 + c * 128:5128 + (c + 1) * 128], np.zeros((D, 1), f32)]
                maps.append({"xT": gathered, "wh": np.ascontiguousarray(np.concatenate(cols, axis=1)), "cst": cstAE,
                             "gb": gb_AE(np.asarray(rel_bias, f32), np.asarray(ab_forget_bias[j], f32), c)})
            outs = _run(nc, maps)
            yT_full = np.concatenate([o["yT"][0:128] for o in outs] + [o["yT"][128:256] for o in outs], axis=0)
            w_out = np.asarray(ab_w_out[j], f32)
        else:
            nc = _prog("AO", build_AO)
            wq = np.asarray(mla_w_uq[j], f32)
            wkv = np.asarray(mla_w_ukv[j], f32)
            maps = []
            for c in range(NCORES):
                maps.append({"lat": gathered, "wuq": np.ascontiguousarray(wq[:, c * 384:(c + 1) * 384]),
                             "wukv": np.ascontiguousarray(wkv[:, c * 512:(c + 1) * 512]), "cs": csT, "cst": cstAE})
            outs = _run(nc, maps)
            yT_full = np.concatenate([o["yT"] for o in outs], axis=0)
            w_out = np.asarray(mla_w_out[j], f32)
        mode = "final" if layer == DEPTH - 1 else ("mla" if (layer + 1) % 2 == 1 else "even")
        nc = _prog("T" + mode, lambda: build_T(mode))
        lnp = np.ascontiguousarray(np.concatenate([_lay128(np.asarray(ln_g[layer, 0], f32)), _lay128(np.asarray(ln_b[layer, 0], f32)),
                                                   _lay128(np.asarray(ln_g[layer, 1], f32)), _lay128(np.asarray(ln_b[layer, 1], f32))], axis=1))
        maps = []
        for c in range(NCORES):
            m = {"yT": np.ascontiguousarray(yT_full[:, c * TOK:(c + 1) * TOK]), "xres": xres[c], "wout": w_out, "lnp": lnp,
                 "wg": np.asarray(ffn_w_gate[layer], f32), "wu": np.asarray(ffn_w_up[layer], f32),
                 "wd": np.asarray(ffn_w_down[layer], f32)}
            if mode == "mla":
                jn = (layer + 1) // 2
                m["win"] = np.asarray(mla_w_in[jn], f32)
                m["ng"] = np.ascontiguousarray(np.concatenate([_lay128(np.asarray(mla_q_norm[jn], f32)),
                                                               _lay128(np.asarray(mla_kv_norm[jn], f32))], axis=1))
                m["cs"] = np.ascontiguousarray(np.concatenate([csT[:, c * TOK:(c + 1) * TOK], csT[:, S + c * TOK:S + (c + 1) * TOK]], axis=1))
            maps.append(m)
        outs = _run(nc, maps)
        xres = [o["xout"] for o in outs]
        if mode == "mla":
            gathered = np.ascontiguousarray(np.stack([o["lat"] for o in outs], axis=0))
        elif mode == "even":
            gathered = np.ascontiguousarray(np.stack([o["xTb"] for o in outs], axis=0))
    out = np.concatenate(xres, axis=1)
    return np.ascontiguousarray(out.T)[None].astype(np.float32)
```

```python
import math
import numpy as np
import ml_dtypes
import concourse.bass as bass
import concourse.mybir as mybir
from concourse.bass_utils import run_bass_kernel_spmd

F32 = mybir.dt.float32
BF16 = mybir.dt.bfloat16
AF = mybir.ActivationFunctionType
ALU = mybir.AluOpType
AX = mybir.AxisListType

NCORES = 8
D = 2048
S = 8192
DEPTH = 4
TOK = S // NCORES
DFF = 5632
NFC = DFF // 128
KC = D // 128
ALPHA = (2 * DEPTH) ** 0.25
HD = 128
MLA_IN = 1088
NEG = -30000.0

EPOCH = 20000
SAME_ENGINE_SYNC = True
COMPUTE = ("pe", "act", "dve", "pool")


class Res:
    __slots__ = ("w", "rs", "name")

    def __init__(self, name=""):
        self.w = None
        self.rs = {}
        self.name = name


class Prog:
    def __init__(self, nc, same_engine_sync=None):
        if same_engine_sync is None:
            same_engine_sync = SAME_ENGINE_SYNC
        self.nc = nc
        self.ops = {e: [] for e in ("pe", "act", "dve", "pool", "sp")}
        self.same_engine_sync = same_engine_sync
        self.dma_sems = []
        self._ctx = []

    def enter(self, cm):
        v = cm.__enter__()
        self._ctx.append(cm)
        return v

    def sbuf(self, name, shape, dtype):
        return self.enter(self.nc.sbuf_tensor("sb_" + name, list(shape), dtype))

    def psum(self, name, shape, dtype=F32):
        return self.enter(self.nc.psum_tensor("ps_" + name, list(shape), dtype))

    def new_dma_sem(self, name, exclusive=False):
        if not exclusive:
            return None
        h = self.enter(self.nc.semaphore(name))
        self.dma_sems.append([h, 0])
        return len(self.dma_sems) - 1

    NRING = 24

    def _ring_sem(self, queue):
        if not hasattr(self, "_rings"):
            self._rings = {}
        if queue not in self._rings:
            n = self.NRING if queue == "sp" else 8
            idx = []
            for i in range(n):
                h = self.enter(self.nc.semaphore(f"ring_{queue}{i}"))
                self.dma_sems.append([h, 0])
                idx.append(len(self.dma_sems) - 1)
            self._rings[queue] = [idx, [None] * n, 0]
        rg = self._rings[queue]
        i = rg[2] % len(rg[0])
        rg[2] += 1
        return rg, i

    def _deps(self, eng, reads, writes):
        deps = set()
        for r in reads:
            if r.w is not None:
                deps.add(r.w)
        for w in writes:
            if w.w is not None:
                deps.add(w.w)
            deps.update(w.rs.values())
        out = []
        for d in deps:
            if d[0] == "E":
                if d[1] == eng and (eng == "pe" or not self.same_engine_sync):
                    continue
                self.ops[d[1]][d[2]][2] = True
            out.append(d)
        return out

    def op(self, eng, fn, reads=(), writes=()):
        deps = self._deps(eng, reads, writes)
        idx = len(self.ops[eng])
        ev = ("E", eng, idx)
        self.ops[eng].append([fn, deps, False, None])
        for r in reads:
            r.rs[eng] = ev
        for w in writes:
            w.w = ev
            w.rs = {}
        return ev

    def dma(self, queue, fn, sem, reads=(), writes=()):
        deps = self._deps(queue, reads, writes)
        if sem is None:
            rg, ri = self._ring_sem(queue)
            sem = rg[0][ri]
            if rg[1][ri] is not None:
                deps.append(rg[1][ri])
            rg[1][ri] = ("S", sem, self.dma_sems[sem][1] + 16)
        self.dma_sems[sem][1] += 16
        ev = ("S", sem, self.dma_sems[sem][1])
        self.ops[queue].append([fn, deps, False, sem])
        for r in reads:
            r.rs[("S", sem)] = ev
        for w in writes:
            w.w = ev
            w.rs = {}
        return ev

    def emit(self, final_events):
        nc = self.nc
        sigval = {}
        esems = {}
        for e in COMPUTE + ("sp",):
            c = 0
            vals = []
            for o in self.ops[e]:
                if o[2] and o[3] is None:
                    c += 1
                vals.append(c)
            sigval[e] = vals
            n_ep = max((c + EPOCH - 1) // EPOCH, 1)
            esems[e] = [self.enter(nc.semaphore(f"s_{e}_{k}")) for k in range(n_ep)]

        def resolve(d):
            if d[0] == "S":
                return ("S", d[1]), self.dma_sems[d[1]][0], d[2]
            v = sigval[d[1]][d[2]]
            ep = (v - 1) // EPOCH
            return ("E", d[1], ep), esems[d[1]][ep], v - ep * EPOCH

        engmap = {"pe": "tensor", "act": "scalar", "dve": "vector", "pool": "gpsimd", "sp": "sync"}
        prog = self

        def body_for(e):
            def body(engine):
                waited = {}
                for i, (fn, deps, sig, dsem) in enumerate(prog.ops[e]):
                    for d in deps:
                        key, sem, val = resolve(d)
                        if waited.get(key, 0) < val:
                            engine.wait_ge(sem, val)
                            waited[key] = val
                    inst = fn(engine)
                    if dsem is not None:
                        inst.then_inc(prog.dma_sems[dsem][0], 16)
                    elif sig:
                        v = sigval[e][i]
                        ep = (v - 1) // EPOCH
                        inst.then_inc(esems[e][ep], 1)
                for d in final_events.get(e, ()):
                    key, sem, val = resolve(d)
                    engine.wait_ge(sem, val)
            return body

        with nc.Block() as block:
            for e in COMPUTE + ("sp",):
                if not self.ops[e] and not final_events.get(e):
                    continue
                getattr(block, engmap[e])(body_for(e))

    def close(self):
        while self._ctx:
            self._ctx.pop().__exit__(None, None, None)


class Slots:
    def __init__(self, p, name, n, nbytes):
        self.p = p
        self.n = n
        self.t = [p.sbuf(f"{name}{i}", [128, nbytes // 2], BF16) for i in range(n)]
        self.r = [Res(f"{name}{i}") for i in range(n)]
        self.sem = [p.new_dma_sem(f"{name}s{i}", exclusive=True) for i in range(n)]
        self.k = 0

    def load(self, shape, src_ap, queue="pool"):
        i = self.k % self.n
        self.k += 1
        a, b = shape
        view = self.t[i][:, 0:a * b].rearrange("p (a b) -> p a b", a=a)
        self.p.dma(queue, lambda q, v=view, s=src_ap: q.dma_start(out=v, in_=s), self.sem[i],
                   writes=[self.r[i]])
        return view, self.r[i]


def build_T(mode_next, has_mixer=True):
    nc = bass.Bass("TRN2", target_bir_lowering=False)
    dt = nc.dram_tensor
    yT_d = dt("yT", [D, TOK], BF16, kind="ExternalInput").ap()
    xres_d = dt("xres", [D, TOK], F32, kind="ExternalInput").ap()
    wout_d = dt("wout", [D, D], F32, kind="ExternalInput").ap()
    lnp_d = dt("lnp", [128, 4 * KC], F32, kind="ExternalInput").ap()
    wg_d = dt("wg", [D, DFF], F32, kind="ExternalInput").ap()
    wu_d = dt("wu", [D, DFF], F32, kind="ExternalInput").ap()
    wd_d = dt("wd", [DFF, D], F32, kind="ExternalInput").ap()
    xout_d = dt("xout", [D, TOK], F32, kind="ExternalOutput").ap()
    if mode_next == "even":
        xTb_d = dt("xTb", [D, TOK], BF16, kind="ExternalOutput").ap()
    if mode_next == "mla":
        win_d = dt("win", [D, MLA_IN], F32, kind="ExternalInput").ap()
        ng_d = dt("ng", [128, 8], F32, kind="ExternalInput").ap()
        cs_d = dt("cs", [64, 2 * TOK], F32, kind="ExternalInput").ap()
        lat_d = dt("lat", [MLA_IN, TOK], BF16, kind="ExternalOutput").ap()

    p = Prog(nc)
    z = p.sbuf("z", [128, KC, TOK], F32)
    xb = p.sbuf("xb", [128, KC, TOK], BF16)
    hg = [p.sbuf(f"hg{i}", [128, 4, TOK], BF16) for i in range(2)]
    ws = Slots(p, "ws", 4, 16384)
    lnp = p.sbuf("lnp", [128, 4 * KC], F32)
    ones = p.sbuf("ones", [128, 128], F32)
    mean = p.sbuf("mean", [128, TOK], F32)
    rstd = p.sbuf("rstd", [128, TOK], F32)
    tmpf = [p.sbuf(f"tmpf{i}", [128, 512], F32) for i in range(4)]
    P = [p.psum(f"P{i}", [128, 512], F32) for i in range(8)]
    rP = [Res(f"P{i}") for i in range(8)]
    rz = [[Res(f"z{m}_{h}") for h in range(2)] for m in range(KC)]
    rxb = [[Res(f"xb{m}_{h}") for h in range(2)] for m in range(KC)]
    rhg = [[[Res(f"hg{i}_{f}_{h}") for h in range(2)] for f in range(4)] for i in range(2)]
    rtmp = [Res(f"tmp{i}") for i in range(4)]
    rlnp, rones, rmean, rrstd = Res("lnp"), Res("ones"), [Res("mean0"), Res("mean1")], [Res("rstd0"), Res("rstd1")]
    s_in = p.new_dma_sem("s_in")
    s_in2 = p.new_dma_sem("s_in2")
    s_out = p.new_dma_sem("s_out")
    H = [slice(0, 512), slice(512, 1024)]

    p.dma("sp", lambda q: q.dma_start(out=lnp[:], in_=lnp_d[:, :]), s_in2, writes=[rlnp])
    p.op("dve", lambda v: v.memset(ones[:], 1.0 / D), writes=[rones])
    yv = yT_d.rearrange("(kc p) t -> p kc t", p=128)
    xv = xres_d.rearrange("(kc p) t -> p kc t", p=128)
    for m in range(KC):
        if has_mixer:
            p.dma("sp", lambda q, m=m: q.dma_start(out=xb[:, m, :], in_=yv[:, m, :]), s_in,
                  writes=[rxb[m][0], rxb[m][1]])
        p.dma("sp", lambda q, m=m: q.dma_start(out=z[:, m, :], in_=xv[:, m, :]), s_in2,
              writes=[rz[m][0], rz[m][1]])

    bank_rr = [0]

    def proj_accum(nk, lhs_fn, rhs_fn, rhs_res, w_res, bank):
        for k in range(nk):
            p.op("pe", lambda t, k=k, bank=bank: t.matmul(P[bank][:], lhsT=lhs_fn(k), rhs=rhs_fn(k),
                                                           start=(k == 0), stop=(k == nk - 1)),
                 reads=[w_res, rhs_res(k)], writes=[rP[bank]])

    if has_mixer:
        wv = wout_d.rearrange("(kc p) n -> p kc n", p=128)
        for m4 in range(4):
            slab, rs = ws.load((KC, 512), wv[:, :, m4 * 512:(m4 + 1) * 512])
            for mi in range(4):
                m = m4 * 4 + mi
                for h in range(2):
                    bank = bank_rr[0] % 4
                    bank_rr[0] += 1
                    proj_accum(KC, lambda k, slab=slab, mi=mi: slab[:, k, mi * 128:(mi + 1) * 128],
                               lambda k, h=h: xb[:, k, H[h]], lambda k, h=h: rxb[k][h], rs, bank)
                    p.op("dve", lambda v, m=m, h=h, bank=bank: v.scalar_tensor_tensor(
                        out=z[:, m, H[h]], in0=z[:, m, H[h]], scalar=ALPHA, in1=P[bank][:],
                        op0=ALU.mult, op1=ALU.add), reads=[rP[bank], rz[m][h]], writes=[rz[m][h]])

    def layer_norm(gi, bi):
        for h in range(2):
            for k in range(KC):
                p.op("pe", lambda t, k=k, h=h: t.matmul(P[4 + h][:], lhsT=ones[:], rhs=z[:, k, H[h]],
                                                         start=(k == 0), stop=(k == KC - 1)),
                     reads=[rones, rz[k][h]], writes=[rP[4 + h]])
            for k in range(KC):
                ti = k % 2
                p.op("act", lambda a, k=k, h=h, ti=ti: a.activation(out=tmpf[ti][:], in_=z[:, k, H[h]], func=AF.Square),
                     reads=[rz[k][h]], writes=[rtmp[ti]])
                p.op("pe", lambda t, k=k, h=h, ti=ti: t.matmul(P[6 + h][:], lhsT=ones[:], rhs=tmpf[ti][:],
                                                                start=(k == 0), stop=(k == KC - 1)),
                     reads=[rones, rtmp[ti]], writes=[rP[6 + h]])
            p.op("dve", lambda v, h=h: v.tensor_copy(out=mean[:, H[h]], in_=P[4 + h][:]),
                 reads=[rP[4 + h]], writes=[rmean[h]])
            p.op("dve", lambda v, h=h: v.tensor_tensor(out=tmpf[2][:], in0=mean[:, H[h]], in1=mean[:, H[h]], op=ALU.mult),
                 reads=[rmean[h]], writes=[rtmp[2]])
            p.op("dve", lambda v, h=h: v.tensor_tensor(out=tmpf[2][:], in0=P[6 + h][:], in1=tmpf[2][:], op=ALU.subtract),
                 reads=[rP[6 + h], rtmp[2]], writes=[rtmp[2]])
            p.op("dve", lambda v, h=h: v.tensor_scalar(out=tmpf[2][:], in0=tmpf[2][:], scalar1=1e-5, scalar2=None, op0=ALU.add),
                 reads=[rtmp[2]], writes=[rtmp[2]])
            p.op("act", lambda a, h=h: a.activation(out=tmpf[3][:], in_=tmpf[2][:], func=AF.Sqrt),
                 reads=[rtmp[2]], writes=[rtmp[3]])
            p.op("dve", lambda v, h=h: v.reciprocal(out=rstd[:, H[h]], in_=tmpf[3][:]),
                 reads=[rtmp[3]], writes=[rrstd[h]])
        for h in range(2):
            for k in range(KC):
                p.op("dve", lambda g, k=k, h=h: g.tensor_tensor(out=z[:, k, H[h]], in0=z[:, k, H[h]], in1=mean[:, H[h]], op=ALU.subtract),
                     reads=[rz[k][h], rmean[h]], writes=[rz[k][h]])
                p.op("dve", lambda v, k=k, h=h: v.tensor_tensor(out=z[:, k, H[h]], in0=z[:, k, H[h]], in1=rstd[:, H[h]], op=ALU.mult),
                     reads=[rz[k][h], rrstd[h]], writes=[rz[k][h]])
                p.op("act", lambda a, k=k, h=h: a.activation(out=z[:, k, H[h]], in_=z[:, k, H[h]], func=AF.Identity,
                                                              scale=lnp[:, gi * KC + k:gi * KC + k + 1],
                                                              bias=lnp[:, bi * KC + k:bi * KC + k + 1]),
                     reads=[rz[k][h], rlnp], writes=[rz[k][h]])
                p.op("act", lambda v, k=k, h=h: v.activation(out=xb[:, k, H[h]], in_=z[:, k, H[h]], func=AF.Copy),
                     reads=[rz[k][h]], writes=[rxb[k][h]])

    layer_norm(0, 1)

    gv = wg_d.rearrange("(kc p) n -> p kc n", p=128)
    uv = wu_d.rearrange("(kc p) n -> p kc n", p=128)
    dv = wd_d.rearrange("(fc p) n -> p fc n", p=128)
    NG = NFC // 4
    for g in range(NG):
        gslab, rg = ws.load((KC, 512), gv[:, :, g * 512:(g + 1) * 512])
        uslab, ru = ws.load((KC, 512), uv[:, :, g * 512:(g + 1) * 512])
        hb = hg[g % 2]
        rh = rhg[g % 2]
        for fi in range(4):
            for h in range(2):
                bg = bank_rr[0] % 4
                bu = (bank_rr[0] + 1) % 4
                bank_rr[0] += 2
                proj_accum(KC, lambda k, fi=fi: gslab[:, k, fi * 128:(fi + 1) * 128],
                           lambda k, h=h: xb[:, k, H[h]], lambda k, h=h: rxb[k][h], rg, bg)
                proj_accum(KC, lambda k, fi=fi: uslab[:, k, fi * 128:(fi + 1) * 128],
                           lambda k, h=h: xb[:, k, H[h]], lambda k, h=h: rxb[k][h], ru, bu)
                ti = (fi * 2 + h) % 2
                p.op("act", lambda a, bg=bg, ti=ti: a.activation(out=tmpf[ti][:], in_=P[bg][:], func=AF.Silu),
                     reads=[rP[bg]], writes=[rtmp[ti]])
                p.op("dve", lambda v, bu=bu, ti=ti, fi=fi, h=h, hb=hb: v.tensor_tensor(
                    out=hb[:, fi, H[h]], in0=P[bu][:], in1=tmpf[ti][:], op=ALU.mult),
                    reads=[rP[bu], rtmp[ti]], writes=[rh[fi][h]])
        for half in range(2):
            dslab, rd = ws.load((4, 1024), dv[:, g * 4:(g + 1) * 4, half * 1024:(half + 1) * 1024])
            for mi in range(8):
                m = half * 8 + mi
                for h in range(2):
                    bank = 4 + (bank_rr[0] % 4)
                    bank_rr[0] += 1
                    proj_accum(4, lambda k, mi=mi, dslab=dslab: dslab[:, k, mi * 128:(mi + 1) * 128],
                               lambda k, h=h, hb=hb: hb[:, k, H[h]], lambda k, h=h, rh=rh: rh[k][h], rd, bank)
                    if g == 0:
                        p.op("dve", lambda v, m=m, h=h, bank=bank: v.scalar_tensor_tensor(
                            out=z[:, m, H[h]], in0=z[:, m, H[h]], scalar=ALPHA, in1=P[bank][:],
                            op0=ALU.mult, op1=ALU.add), reads=[rP[bank], rz[m][h]], writes=[rz[m][h]])
                    else:
                        p.op("dve", lambda v, m=m, h=h, bank=bank: v.tensor_tensor(
                            out=z[:, m, H[h]], in0=z[:, m, H[h]], in1=P[bank][:], op=ALU.add),
                            reads=[rP[bank], rz[m][h]], writes=[rz[m][h]])

    layer_norm(2, 3)

    finals = []
    xo = xout_d.rearrange("(kc p) t -> p kc t", p=128)
    for m in range(KC):
        finals.append(p.dma("sp", lambda q, m=m: q.dma_start(out=xo[:, m, :], in_=z[:, m, :]), s_out,
                            reads=[rz[m][0], rz[m][1]]))
    if mode_next == "even":
        xt = xTb_d.rearrange("(kc p) t -> p kc t", p=128)
        for m in range(KC):
            finals.append(p.dma("sp", lambda q, m=m: q.dma_start(out=xt[:, m, :], in_=xb[:, m, :]), s_out,
                                reads=[rxb[m][0], rxb[m][1]]))
    if mode_next == "mla":
        finals += mla_latents(p, nc, ws, xb, rxb, P, rP, bank_rr, tmpf, rtmp, H,
                              win_d, ng_d, cs_d, lat_d, s_in2, s_out,
                              hg, rhg, mean, rstd, rmean, rrstd)
    p.emit({"sp": [finals[-1]] if False else finals})
    p.close()
    return nc


def mla_latents(p, nc, ws, xb, rxb, P, rP, bank_rr, tmpf, rtmp, H,
                win_d, ng_d, cs_d, lat_d, s_in, s_out, hg, rhg, mean, rstd, rmean, rrstd):
    finals = []
    ng = p.sbuf("ng", [128, 8], F32)
    rng_ = Res("ng")
    p.dma("sp", lambda q: q.dma_start(out=ng[:], in_=ng_d[:, :]), s_in, writes=[rng_])
    all_mean = [rmean[0], rmean[1]]
    all_rstd = [rrstd[0], rrstd[1]]
    p.dma("sp", lambda q: q.dma_start(out=mean[0:64, :], in_=cs_d[:, 0:TOK]), s_in, writes=all_mean)
    p.dma("sp", lambda q: q.dma_start(out=rstd[0:64, :], in_=cs_d[:, TOK:2 * TOK]), s_in, writes=all_rstd)
    lat = hg[0].bitcast(F32)
    latb = hg[1]
    rl0 = [r for f in rhg[0] for r in f]
    rl1 = [r for f in rhg[1] for r in f]
    ones512 = p.sbuf("ones512", [128, 128], F32)
    r512 = Res("ones512")
    p.op("dve", lambda v: v.memset(ones512[:], 1.0 / 512), writes=[r512])
    wv = win_d.rearrange("(kc p) n -> p kc n", p=128)
    for li in range(2):
        slab, rs = ws.load((KC, 512), wv[:, :, li * 512:(li + 1) * 512])
        for h in range(2):
            for c in range(4):
                bank = bank_rr[0] % 4
                bank_rr[0] += 1
                for k in range(KC):
                    p.op("pe", lambda t, k=k, c=c, h=h, bank=bank, slab=slab: t.matmul(
                        P[bank][:], lhsT=slab[:, k, c * 128:(c + 1) * 128], rhs=xb[:, k, H[h]],
                        start=(k == 0), stop=(k == KC - 1)), reads=[rs, rxb[k][h]], writes=[rP[bank]])
                p.op("act", lambda a, c=c, bank=bank: a.activation(out=lat[:, c, :], in_=P[bank][:], func=AF.Copy),
                     reads=[rP[bank]], writes=rl0)
            for c in range(4):
                ti = c % 2
                p.op("act", lambda a, c=c, ti=ti: a.activation(out=tmpf[ti][:], in_=lat[:, c, :], func=AF.Square),
                     reads=rl0, writes=[rtmp[ti]])
                p.op("pe", lambda t, c=c, h=h, ti=ti: t.matmul(P[6 + h][:], lhsT=ones512[:], rhs=tmpf[ti][:],
                                                                start=(c == 0), stop=(c == 3)),
                     reads=[r512, rtmp[ti]], writes=[rP[6 + h]])
            p.op("dve", lambda v, h=h: v.tensor_scalar(out=tmpf[2][:], in0=P[6 + h][:], scalar1=1e-6, scalar2=None, op0=ALU.add),
                 reads=[rP[6 + h]], writes=[rtmp[2]])
            p.op("act", lambda a: a.activation(out=tmpf[3][:], in_=tmpf[2][:], func=AF.Sqrt),
                 reads=[rtmp[2]], writes=[rtmp[3]])
            p.op("dve", lambda v: v.reciprocal(out=tmpf[2][:], in_=tmpf[3][:]), reads=[rtmp[3]], writes=[rtmp[2]])
            for c in range(4):
                p.op("dve", lambda v, c=c: v.tensor_tensor(out=lat[:, c, :], in0=lat[:, c, :], in1=tmpf[2][:], op=ALU.mult),
                     reads=rl0 + [rtmp[2]], writes=rl0)
                p.op("act", lambda a, c=c, li=li: a.activation(out=latb[:, c, 0:512], in_=lat[:, c, :], func=AF.Identity,
                                                               scale=ng[:, li * 4 + c:li * 4 + c + 1]),
                     reads=rl0 + [rng_], writes=rl1)
            lv = lat_d[li * 512:(li + 1) * 512, H[h]].rearrange("(c p) t -> p c t", p=128)
            finals.append(p.dma("sp", lambda q, lv=lv: q.dma_start(out=lv, in_=latb[:, :, 0:512]), s_out, reads=rl1))
    slab, rs = ws.load((KC, 64), wv[:, :, 1024:1088])
    wrot = p.sbuf("wrot", [128, KC, 64], BF16)
    rwrot = Res("wrot")
    p.op("dve", lambda v: v.tensor_scalar(out=wrot[:, :, 0:32], in0=slab[:, :, 32:64], scalar1=-1.0, scalar2=None, op0=ALU.mult),
         reads=[rs], writes=[rwrot])
    p.op("dve", lambda v: v.tensor_copy(out=wrot[:, :, 32:64], in_=slab[:, :, 0:32]), reads=[rs], writes=[rwrot])
    kr = p.sbuf("kr", [64, TOK], BF16)
    rkr = [Res("kr0"), Res("kr1")]
    for h in range(2):
        b0 = bank_rr[0] % 4
        b1 = (bank_rr[0] + 1) % 4
        bank_rr[0] += 2
        for k in range(KC):
            p.op("pe", lambda t, k=k, h=h, b0=b0: t.matmul(P[b0][0:64, :], lhsT=slab[:, k, :], rhs=xb[:, k, H[h]],
                                                            start=(k == 0), stop=(k == KC - 1)),
                 reads=[rs, rxb[k][h]], writes=[rP[b0]])
        for k in range(KC):
            p.op("pe", lambda t, k=k, h=h, b1=b1: t.matmul(P[b1][0:64, :], lhsT=wrot[:, k, :], rhs=xb[:, k, H[h]],
                                                            start=(k == 0), stop=(k == KC - 1)),
                 reads=[rwrot, rxb[k][h]], writes=[rP[b1]])
        p.op("dve", lambda v, h=h, b0=b0: v.tensor_tensor(out=tmpf[0][0:64, :], in0=P[b0][0:64, :], in1=mean[0:64, H[h]], op=ALU.mult),
             reads=[rP[b0]] + all_mean, writes=[rtmp[0]])
        p.op("dve", lambda v, h=h, b1=b1: v.tensor_tensor(out=tmpf[1][0:64, :], in0=P[b1][0:64, :],
                                                          in1=rstd[0:64, H[h]], op=ALU.mult),
             reads=[rP[b1]] + all_rstd, writes=[rtmp[1]])
        p.op("dve", lambda v, h=h: v.tensor_tensor(out=kr[:, H[h]], in0=tmpf[0][0:64, :], in1=tmpf[1][0:64, :], op=ALU.add),
             reads=[rtmp[0], rtmp[1]], writes=[rkr[h]])
    finals.append(p.dma("sp", lambda q: q.dma_start(out=lat_d[1024:1088, :], in_=kr[:]), s_out, reads=rkr))
    return finals


NQG = S // 512
NKB = S // 128
YOFF = [0, 129, 258, 0]
YBNK = [0, 0, 0, 1]


class AttnCtx:
    def __init__(self, p, P, rP, identb, rident, trib, rtri, yT_d, s_out):
        self.p, self.P, self.rP = p, P, rP
        self.identb, self.rident, self.trib, self.rtri = identb, rident, trib, rtri
        self.pt = [p.sbuf(f"pt{i}", [128, 512], BF16) for i in range(3)]
        self.rpt = [Res(f"pt{i}") for i in range(3)]
        self.yn = [p.sbuf(f"yn{i}", [128, 128], BF16) for i in range(4)]
        self.ryn = [Res(f"yn{i}") for i in range(4)]
        self.pending = []
        self.ytb = [p.sbuf(f"ytb{i}", [128, 512], BF16) for i in range(2)]
        self.rytb = [Res(f"ytb{i}") for i in range(2)]
        self.rinv = [p.sbuf(f"rinv{i}", [128, 1], F32) for i in range(4)]
        self.rrinv = [Res(f"rinv{i}") for i in range(4)]
        self.tp = p.enter(p.nc.psum_tensor("ps_tp", [128, 1024], BF16))
        self.rtp = Res("tp")
        self.yT_d, self.s_out = yT_d, s_out
        self.finals = []
        self.kpt = 0
        self.ksb = 0
        self.kyn = 0
        self.kg = 0

    def finish_block(self, src_ap, sum_ap, src_res, s, row0, g, last):
        p = self.p
        i = self.kyn % 4
        self.kyn += 1
        p.op("dve", lambda v, i=i: v.reciprocal(out=self.rinv[i][:], in_=sum_ap), reads=src_res, writes=[self.rrinv[i]])
        p.op("dve", lambda v, i=i: v.tensor_scalar(out=self.yn[i][:], in0=src_ap, scalar1=self.rinv[i][:, 0:1], scalar2=None,
                                                   op0=ALU.mult),
             reads=list(src_res) + [self.rrinv[i]], writes=[self.ryn[i]])
        self.pending.append([2, i, s, row0, g, last])

    def tick(self, flush=False):
        keep = []
        for item in self.pending:
            item[0] -= 1
            if item[0] > 0 and not flush:
                keep.append(item)
                continue
            _, i, s, row0, g, last = item
            p = self.p
            gb = self.kg % 2
            p.op("pe", lambda t, i=i: t.transpose(out=self.tp[:, 0:128], in_=self.yn[i][:], identity=self.identb[:]),
                 reads=[self.ryn[i], self.rident], writes=[self.rtp])
            p.op("dve", lambda v, s=s, gb=gb: v.tensor_copy(out=self.ytb[gb][:, s * 128:(s + 1) * 128], in_=self.tp[:, 0:128]),
                 reads=[self.rtp], writes=[self.rytb[gb]])
            if last:
                self.finals.append(p.dma("sp", lambda q, gb=gb, row0=row0, g=g: q.dma_start(
                    out=self.yT_d[row0:row0 + 128, g * 512:(g + 1) * 512], in_=self.ytb[gb][:]), self.s_out,
                    reads=[self.rytb[gb]]))
                self.kg += 1
        self.pending = keep


def dense_causal_sweep(cx, qk_fn, bias_fn, V, rV, scale, row0):
    p, P, rP = cx.p, cx.P, cx.rP
    steps = [(g, J) for g in range(NQG) for J in range(4 * g + 4)]

    def part_a(g, J):
        r = J - 4 * g
        c0 = max(r, 0) * 128
        sb = cx.ksb % 2
        cx.ksb += 1
        terms = qk_fn(J, g, c0)
        for ti, (lh, rh, rd) in enumerate(terms):
            p.op("pe", lambda t, lh=lh, rh=rh, ti=ti, sb=sb, c0=c0, nt=len(terms), r=r: t.matmul(
                P[sb][:, c0:512], lhsT=lh, rhs=rh, start=(ti == 0), stop=(ti == nt - 1 and r < 0)),
                reads=rd, writes=[rP[sb]])
        if r >= 0:
            p.op("pe", lambda t, sb=sb, c0=c0: t.matmul(P[sb][:, c0:c0 + 128], lhsT=cx.identb[:], rhs=cx.trib[:],
                                                        start=False, stop=True),
                 reads=[cx.rident, cx.rtri], writes=[rP[sb]])
        pi = cx.kpt % 3
        cx.kpt += 1
        b = bias_fn(g, J) if bias_fn else None
        if b is None:
            p.op("act", lambda a, pi=pi, sb=sb, c0=c0: a.activation(out=cx.pt[pi][:, c0:512], in_=P[sb][:, c0:512],
                                                                      func=AF.Exp, scale=scale),
                 reads=[rP[sb]], writes=[cx.rpt[pi]])
        else:
            bap, brd = b
            p.op("act", lambda a, pi=pi, sb=sb, c0=c0, bap=bap: a.activation(
                out=cx.pt[pi][:, c0:512], in_=P[sb][:, c0:512], func=AF.Exp, scale=scale, bias=bap),
                reads=[rP[sb]] + brd, writes=[cx.rpt[pi]])
        return pi

    def part_b(g, J, pi):
        r = J - 4 * g
        for s in range(max(r, 0), 4):
            bank = 3 + s
            p.op("pe", lambda t, pi=pi, s=s, bank=bank, J=J, g=g: t.matmul(
                P[bank][:, 0:129], lhsT=cx.pt[pi][:, s * 128:(s + 1) * 128], rhs=V[:, J, 0:129],
                start=(J == 0), stop=(J == 4 * g + s)),
                reads=[cx.rpt[pi], rV[J // 4]], writes=[rP[bank]])
            if J == 4 * g + s:
                cx.finish_block(P[bank][:, 0:128], P[bank][:, 128:129], [rP[bank]], s, row0, g, s == 3)

    pis = [part_a(*steps[0])]
    for i, (g, J) in enumerate(steps):
        if i + 1 < len(steps):
            pis.append(part_a(*steps[i + 1]))
        part_b(g, J, pis[i])
        cx.tick()
    cx.tick(flush=True)


def build_AE(x_f32=False, stop=None, flags="qkvgABFrlhmx"):
    nc = bass.Bass("TRN2", target_bir_lowering=False)
    dt = nc.dram_tensor
    xT_d = dt("xT", [NCORES, D, TOK], F32 if x_f32 else BF16, kind="ExternalInput").ap()
    wh_d = dt("wh", [D, 770], F32, kind="ExternalInput").ap()
    cst_d = dt("cst", [128, 576], F32, kind="ExternalInput").ap()
    gb_d = dt("gb", [128, 258], F32, kind="ExternalInput").ap()
    yT_d = dt("yT", [256, S], BF16, kind="ExternalOutput").ap()

    p = Prog(nc)
    s_in = p.new_dma_sem("s_in")
    s_w = p.new_dma_sem("s_w")
    s_out = p.new_dma_sem("s_out")
    s_x = [p.new_dma_sem(f"s_x{i}") for i in range(2)]
    s_xe = [p.new_dma_sem(f"s_xe{i}", exclusive=True) for i in range(2)]
    cst = p.sbuf("cst", [128, 576], F32)
    rcst = Res("cst")
    p.dma("sp", lambda q: q.dma_start(out=cst[:], in_=cst_d[:, :]), s_in, writes=[rcst])
    gb = p.sbuf("gb", [128, 258], F32)
    rgb = Res("gb")
    p.dma("sp", lambda q: q.dma_start(out=gb[:], in_=gb_d[:, :]), s_in, writes=[rgb])
    U, LST, E127, IDF, TRIF = cst[:, 0:128], cst[0:64, 128:192], cst[:, 192:320], cst[:, 320:448], cst[:, 448:576]
    identb = p.sbuf("identb", [128, 128], BF16)
    trib = p.sbuf("trib", [128, 128], BF16)
    d0b = p.sbuf("d0b", [128, 128], BF16)
    d1b = p.sbuf("d1b", [128, 128], BF16)
    rident, rtri, rd0, rd1 = Res("identb"), Res("trib"), Res("d0b"), Res("d1b")
    tmpc = p.sbuf("tmpc", [128, 128], F32)
    rtmpc = Res("tmpc")
    p.op("dve", lambda v: v.tensor_copy(out=identb[:], in_=IDF), reads=[rcst], writes=[rident])
    p.op("dve", lambda v: v.tensor_copy(out=trib[:], in_=TRIF), reads=[rcst], writes=[rtri])
    SQ = math.sqrt(HD)
    p.op("dve", lambda v: v.tensor_scalar(out=tmpc[:], in0=gb[:, 0:128], scalar1=gb[:, 256:257], scalar2=SQ,
                                          op0=ALU.subtract, op1=ALU.mult), reads=[rgb], writes=[rtmpc])
    p.op("dve", lambda v: v.tensor_tensor(out=d0b[:], in0=tmpc[:], in1=TRIF, op=ALU.add), reads=[rtmpc, rcst], writes=[rd0])
    p.op("dve", lambda v: v.tensor_scalar(out=d1b[:], in0=gb[:, 128:256], scalar1=gb[:, 256:257], scalar2=SQ,
                                          op0=ALU.subtract, op1=ALU.mult), reads=[rgb], writes=[rd1])
    Wh = p.sbuf("Wh", [128, KC, 770], BF16)
    rWh = Res("Wh")
    whv = wh_d.rearrange("(kc p) n -> p kc n", p=128)
    for k4 in range(4):
        p.dma("pool", lambda q, k4=k4: q.dma_start(out=Wh[:, k4 * 4:(k4 + 1) * 4, :], in_=whv[:, k4 * 4:(k4 + 1) * 4, :]),
              s_w, writes=[rWh])
    QAT = p.sbuf("QAT", [128, S], BF16)
    KAT = p.sbuf("KAT", [128, S], BF16)
    QBT = p.sbuf("QBT", [128, S], BF16)
    KBT = p.sbuf("KBT", [128, S], BF16)
    VA = p.sbuf("VA", [128, NKB, 130], BF16)
    VB = p.sbuf("VB", [128, NKB, 130], BF16)
    rQA = [Res(f"QA{i}") for i in range(NQG)]
    rKA = [Res(f"KA{i}") for i in range(NQG)]
    rQB = [Res(f"QB{i}") for i in range(NQG)]
    rKB = [Res(f"KB{i}") for i in range(NQG)]
    rVA = [Res(f"VA{i}") for i in range(NQG)]
    rVB = [Res(f"VB{i}") for i in range(NQG)]
    Fl = p.sbuf("Fl", [128, NKB], F32)
    rF = Res("F")
    kmT = p.sbuf("kmT", [128, 32], F32)
    rkm = Res("kmT")
    sel = p.sbuf("sel", [128, NKB, 32], F32)
    rsel = [Res(f"sel{i}") for i in range(NKB)]
    gm = p.sbuf("gm", [128, 32], F32)
    rgm = Res("gm")
    m8 = p.sbuf("m8", [128, 8], F32)
    rm8 = Res("m8")
    qlo = [p.sbuf(f"qlo{i}", [128, 512], BF16) for i in range(2)]
    kmh = p.sbuf("kmh", [128, 32], BF16)
    kml = p.sbuf("kml", [128, 32], BF16)
    rkmh = Res("kmh")
    rqf = [Res(f"qf{i}") for i in range(2)]
    xc = [p.sbuf(f"xc{i}", [128, KC, 512], BF16) for i in range(2)]
    rxc = [Res(f"xc{i}") for i in range(2)]
    P = [p.psum(f"P{i}", [128, 512], F32) for i in range(7)]
    rP = [Res(f"P{i}") for i in range(7)]
    p.op("dve", lambda v: v.memset(kmT[:], 0.0), writes=[rkm])
    p.op("dve", lambda v: v.memset(gm[:], -1e30), writes=[rgm])
    for i, (Vt, rVt) in enumerate(((VA, rVA), (VB, rVB))):
        p.op("pool", lambda g_, Vt=Vt: g_.memset(Vt[:, :, 128:130], 1.0), writes=rVt)

    rr = [0]

    def nb():
        b = rr[0] % 7
        rr[0] += 1
        return b

    for tc in range(NQG):
        xi = tc % 2
        src = xT_d[tc // 2, :, (tc % 2) * 512:(tc % 2 + 1) * 512].rearrange("(kc p) t -> p kc t", p=128)
        if x_f32:
            for k4 in range(4):
                ev_ = p.dma("pool", lambda q, xi=xi, src=src, k4=k4: q.dma_start(out=xc[xi][:, k4 * 4:(k4 + 1) * 4, :],
                                                                            in_=src[:, k4 * 4:(k4 + 1) * 4, :]),
                            s_xe[xi], writes=[rxc[xi]])
        else:
            p.dma("sp", lambda q, xi=xi, src=src: q.dma_start(out=xc[xi][:], in_=src), s_x[xi], writes=[rxc[xi]])
        tsl = slice(tc * 512, (tc + 1) * 512)
        for (c0, dst, rdst, extra) in ((0, QAT, rQA, None), (128, KAT, rKA, None), (384, KBT, rKB, "km"), (256, QBT, rQB, "qf")):
            b = nb()
            for k in range(KC):
                p.op("pe", lambda t, k=k, b=b, c0=c0, xi=xi: t.matmul(P[b][:], lhsT=Wh[:, k, c0:c0 + 128], rhs=xc[xi][:, k, :],
                                                                      start=(k == 0), stop=(k == KC - 1)),
                     reads=[rWh, rxc[xi]], writes=[rP[b]])
            if extra is None:
                p.op("act", lambda a, b=b, dst=dst, tsl=tsl: a.activation(out=dst[:, tsl], in_=P[b][:], func=AF.Copy),
                     reads=[rP[b]], writes=[rdst[tc]])
            else:
                p.op("dve", lambda v, b=b, dst=dst, tsl=tsl: v.tensor_copy(out=dst[:, tsl], in_=P[b][:]),
                     reads=[rP[b]], writes=[rdst[tc]])
            if extra == "km" and "r" in flags:
                p.op("dve", lambda v, b=b, tc=tc: v.tensor_reduce(out=kmT[:, 2 * tc:2 * tc + 2],
                                                                  in_=P[b][:].rearrange("p (n k) -> p n k", n=2),
                                                                  axis=AX.X, op=ALU.add),
                     reads=[rP[b]], writes=[rkm])
            if extra == "qf" and "l" in flags:
                p.op("dve", lambda v, b=b, xi=xi, tsl=tsl: v.tensor_tensor(out=qlo[xi][:], in0=P[b][:], in1=QBT[:, tsl], op=ALU.subtract),
                     reads=[rP[b], rQB[tc]], writes=[rqf[xi]])
        for tb in range(4 if "v" in flags else 0):
            b = nb()
            blk = tc * 4 + tb
            for k in range(KC):
                p.op("pe", lambda t, k=k, b=b, tb=tb, xi=xi: t.matmul(P[b][:, 0:257], lhsT=xc[xi][:, k, tb * 128:(tb + 1) * 128],
                                                                      rhs=Wh[:, k, 512:769], start=(k == 0), stop=(k == KC - 1)),
                     reads=[rWh, rxc[xi]], writes=[rP[b]])
            if "A" in flags:
                p.op("dve", lambda v, b=b, blk=blk: v.tensor_copy(out=VA[:, blk, 0:128], in_=P[b][:, 0:128]),
                     reads=[rP[b]], writes=[rVA[tc]])
            if "B" in flags:
                p.op("dve", lambda v, b=b, blk=blk: v.tensor_copy(out=VB[:, blk, 0:128], in_=P[b][:, 129:257]),
                     reads=[rP[b]], writes=[rVB[tc]])
            if "F" in flags:
                p.op("dve", lambda v, b=b, blk=blk: v.tensor_copy(out=Fl[:, blk:blk + 1], in_=P[b][:, 128:129]),
                     reads=[rP[b]], writes=[rF])
        for s in range(4):
            qb = tc * 4 + s
            own = qb // 2
            if own == 0 or "g" not in flags:
                continue
            b = nb()
            if s in (0, 2) and "h" in flags:
                p.op("dve", lambda v: v.tensor_copy(out=kmh[:], in_=kmT[:]), reads=[rkm], writes=[rkmh])
                p.op("dve", lambda v: v.tensor_tensor(out=kml[:], in0=kmT[:], in1=kmh[:], op=ALU.subtract),
                     reads=[rkm, rkmh], writes=[rkmh])
            qsl = slice(tc * 512 + s * 128, tc * 512 + (s + 1) * 128)
            if "m" not in flags:
                continue
            p.op("pe", lambda t, b=b, qsl=qsl: t.matmul(P[b][:, 0:32], lhsT=QBT[:, qsl], rhs=kmh[:, :], start=True, stop=False),
                 reads=[rQB[tc], rkmh], writes=[rP[b]])
            p.op("pe", lambda t, b=b, qsl=qsl: t.matmul(P[b][:, 0:32], lhsT=QBT[:, qsl], rhs=kml[:, :], start=False, stop=False),
                 reads=[rQB[tc], rkmh], writes=[rP[b]])
            p.op("pe", lambda t, b=b, s=s, xi=xi: t.matmul(P[b][:, 0:32], lhsT=qlo[xi][:, s * 128:(s + 1) * 128], rhs=kmh[:, :],
                                                            start=False, stop=True),
                 reads=[rqf[xi], rkmh], writes=[rP[b]])
            if "x" not in flags:
                continue
            p.op("dve", lambda v, b=b, own=own: v.tensor_copy(out=gm[:, 0:own], in_=P[b][:, 0:own]), reads=[rP[b]], writes=[rgm])
            p.op("dve", lambda v: v.max(out=m8[:], in_=gm[:]), reads=[rgm], writes=[rm8])
            p.op("dve", lambda v, qb=qb, own=own: v.tensor_scalar(out=sel[:, qb, 0:own], in0=gm[:, 0:own], scalar1=m8[:, 2:3],
                                                                  scalar2=None, op0=ALU.is_ge),
                 reads=[rgm, rm8], writes=[rsel[qb]])

    if stop == "proj":
        ev = p.dma("sp", lambda q: q.dma_start(out=yT_d[0:128, :], in_=QAT[:, :]), None, reads=rQA + rKA + rQB + rKB + rVA + rVB + rsel + [rF])
        p.emit({"sp": [ev]})
        p.close()
        return nc
    nbf = p.sbuf("nbf", [128, 1], F32)
    rnbf = Res("nbf")
    p.op("dve", lambda v: v.tensor_scalar(out=nbf[:], in0=gb[:, 257:258], scalar1=-1.0, scalar2=None, op0=ALU.mult),
         reads=[rgb], writes=[rnbf])
    ef = p.sbuf("ef", [128, NKB], F32)
    lf = p.sbuf("lf", [128, NKB], F32)
    ref_, rlf = Res("ef"), Res("lf")
    p.op("act", lambda a: a.activation(out=ef[:], in_=Fl[:], func=AF.Exp, scale=-1.0, bias=nbf[:, 0:1]),
         reads=[rF, rnbf], writes=[ref_])
    p.op("act", lambda a: a.activation(out=lf[:], in_=ef[:], func=AF.Ln, bias=1.0, scale=1.0), reads=[ref_], writes=[rlf])
    p.op("dve", lambda v: v.tensor_scalar(out=lf[:], in0=lf[:], scalar1=-1.0, scalar2=None, op0=ALU.mult),
         reads=[rlf], writes=[rlf])
    onesf = p.sbuf("onesf", [128, 128], F32)
    ronesf = Res("onesf")
    p.op("dve", lambda v: v.memset(onesf[:], 1.0), writes=[ronesf])
    b0 = nb()
    p.op("pe", lambda t: t.matmul(P[b0][0:64, 0:1], lhsT=lf[:, :], rhs=onesf[:, 0:1], start=True, stop=True),
         reads=[rlf, ronesf], writes=[rP[b0]])
    totbc = p.sbuf("totbc", [64, 128], F32)
    rtot = Res("totbc")
    p.op("dve", lambda v: v.tensor_scalar(out=totbc[:], in0=onesf[0:64, :], scalar1=P[b0][0:64, 0:1], scalar2=None, op0=ALU.mult),
         reads=[rP[b0], ronesf], writes=[rtot])
    b1 = nb()
    p.op("pe", lambda t: t.matmul(P[b1][:, 0:NKB], lhsT=U, rhs=lf[:, :], start=True, stop=False),
         reads=[rcst, rlf], writes=[rP[b1]])
    p.op("pe", lambda t: t.matmul(P[b1][:, 0:NKB], lhsT=totbc[:, :], rhs=LST, start=False, stop=True),
         reads=[rcst, rtot], writes=[rP[b1]])
    csb = p.sbuf("csb", [128, NKB], F32)
    negc = p.sbuf("negc", [128, NKB], F32)
    rcsb, rnegc = Res("csb"), Res("negc")
    p.op("dve", lambda v: v.tensor_copy(out=csb[:], in_=P[b1][:, 0:NKB]), reads=[rP[b1]], writes=[rcsb])
    p.op("dve", lambda v: v.tensor_scalar(out=negc[:], in0=P[b1][:, 0:NKB], scalar1=-1.0, scalar2=None, op0=ALU.mult),
         reads=[rP[b1]], writes=[rnegc])
    cendc = p.sbuf("cendc", [128, NQG], F32)
    rcendc = Res("cendc")
    p.op("dve", lambda v: v.tensor_copy(out=cendc[:], in_=csb[:].rearrange("p (g f) -> p g f", f=4)[:, :, 3]),
         reads=[rcsb], writes=[rcendc])
    b2 = nb()
    p.op("pe", lambda t: t.matmul(P[b2][:, 0:NQG], lhsT=E127, rhs=cendc[:, :], start=True, stop=True),
         reads=[rcst, rcendc], writes=[rP[b2]])
    cend = p.sbuf("cend", [128, NQG], F32)
    rcend = Res("cend")
    p.op("dve", lambda v: v.tensor_copy(out=cend[:], in_=P[b2][:, 0:NQG]), reads=[rP[b2]], writes=[rcend])
    ball = p.sbuf("ball", [128, NQG, NKB], F32)
    rball = Res("ball")
    for g in range(NQG):
        p.op("dve", lambda v, g=g: v.tensor_scalar(out=ball[:, g, :], in0=negc[:, :], scalar1=cend[:, g:g + 1], scalar2=None,
                                                   op0=ALU.add), reads=[rnegc, rcend], writes=[rball])

    cx = AttnCtx(p, P, rP, identb, rident, trib, rtri, yT_d, s_out)
    SCALE = HD ** -0.5

    def qk_fox(J, g, c0):
        return [(KAT[:, J * 128:(J + 1) * 128], QAT[:, g * 512 + c0:(g + 1) * 512], [rKA[J // 4], rQA[g]])]

    def bias_fox(g, J):
        return (ball[:, g, J:J + 1], [rball])

    dense_causal_sweep(cx, qk_fox, bias_fox, VA, rVA, SCALE, 0)

    if stop == "fox":
        p.emit({"sp": cx.finals})
        p.close()
        return nc
    acc = p.sbuf("acc", [128, 4, 129], F32)
    racc = [Res(f"acc{s}") for s in range(4)]
    msteps = [(g, n, J) for g in range(NQG) for n in range(2 * g + 2) for J in (2 * n, 2 * n + 1)]
    started = {}

    def moba_a(g, n, J):
        r = J - 4 * g
        c0 = max(r, 0) * 128
        sb = cx.ksb % 2
        cx.ksb += 1
        adds = []
        for s in range(max(r, 0), 4):
            if J == 4 * g + s:
                adds.append((s, d0b, rd0))
            elif J == 4 * g + s - 1:
                adds.append((s, d1b, rd1))
        p.op("pe", lambda t, J=J, g=g, c0=c0, sb=sb, na=len(adds): t.matmul(
            P[sb][:, c0:512], lhsT=KBT[:, J * 128:(J + 1) * 128], rhs=QBT[:, g * 512 + c0:(g + 1) * 512],
            start=True, stop=(na == 0)), reads=[rKB[J // 4], rQB[g]], writes=[rP[sb]])
        for ai, (s, dt_, rdt) in enumerate(adds):
            p.op("pe", lambda t, s=s, dt_=dt_, sb=sb, ai=ai, na=len(adds): t.matmul(
                P[sb][:, s * 128:(s + 1) * 128], lhsT=identb[:], rhs=dt_[:], start=False, stop=(ai == na - 1)),
                reads=[rident, rdt], writes=[rP[sb]])
        pi = cx.kpt % 3
        cx.kpt += 1
        p.op("act", lambda a, pi=pi, sb=sb, c0=c0: a.activation(out=cx.pt[pi][:, c0:512], in_=P[sb][:, c0:512],
                                                                  func=AF.Exp, scale=SCALE),
             reads=[rP[sb]], writes=[cx.rpt[pi]])
        return pi

    def moba_b(g, n, J, pi):
        r = J - 4 * g
        yb = [3 + 2 * (n % 2), 4 + 2 * (n % 2)]
        for s in range(max(r, 0), 4):
            bi = YBNK[s]
            bank = yb[bi]
            st = not started.get((g, n, bi), False)
            started[(g, n, bi)] = True
            p.op("pe", lambda t, pi=pi, s=s, bank=bank, J=J, st=st: t.matmul(
                P[bank][:, YOFF[s]:YOFF[s] + 129], lhsT=cx.pt[pi][:, s * 128:(s + 1) * 128], rhs=VB[:, J, 0:129],
                start=st, stop=True, skip_group_check=True),
                reads=[cx.rpt[pi], rVB[J // 4]], writes=[rP[bank]])
        if J != 2 * n + 1:
            return
        smin_n = max(2 * n - 4 * g, 0)
        for s in range(smin_n, 4):
            qb = 4 * g + s
            own = qb // 2
            bank = yb[YBNK[s]]
            src = P[bank][:, YOFF[s]:YOFF[s] + 129]
            if n == own:
                if n == 0:
                    p.op("dve", lambda v, s=s, src=src: v.tensor_copy(out=acc[:, s, :], in_=src),
                         reads=[rP[bank]], writes=[racc[s]])
                else:
                    p.op("dve", lambda v, s=s, src=src: v.tensor_tensor(out=acc[:, s, :], in0=acc[:, s, :], in1=src, op=ALU.add),
                         reads=[rP[bank], racc[s]], writes=[racc[s]])
                cx.finish_block(acc[:, s, 0:128], acc[:, s, 128:129], [racc[s]], s, 128, g, s == 3)
            elif n == 0:
                p.op("dve", lambda v, s=s, src=src, qb=qb: v.tensor_scalar(out=acc[:, s, :], in0=src, scalar1=sel[:, qb, 0:1],
                                                                          scalar2=None, op0=ALU.mult),
                     reads=[rP[bank], rsel[qb]], writes=[racc[s]])
            else:
                p.op("dve", lambda v, s=s, src=src, qb=qb, n=n: v.scalar_tensor_tensor(
                    out=acc[:, s, :], in0=src, scalar=sel[:, qb, n:n + 1], in1=acc[:, s, :], op0=ALU.mult, op1=ALU.add),
                    reads=[rP[bank], rsel[qb], racc[s]], writes=[racc[s]])

    mp = [moba_a(*msteps[0])]
    for i, st_ in enumerate(msteps):
        if i + 1 < len(msteps):
            mp.append(moba_a(*msteps[i + 1]))
        moba_b(*st_, mp[i])
        cx.tick()
    cx.tick(flush=True)

    p.emit({"sp": cx.finals})
    p.close()
    return nc


def _t5_bucket(rel):
    n = np.maximum(rel, 0)
    nf = np.maximum(n, 1).astype(np.float32)
    large = 16 + (np.log(nf / np.float32(16)) / np.float32(math.log(128 / 16)) * np.float32(16)).astype(np.int32)
    large = np.minimum(large, 31)
    return np.where(n < 16, n, large)


def consts_AE():
    j = np.arange(128)
    cst = np.zeros((128, 576), np.float32)
    cst[:, 0:128] = (j[:, None] <= j[None, :])
    b = np.arange(64)
    cst[0:64, 128:192] = (b[:, None] < b[None, :])
    cst[127, 192:320] = 1.0
    cst[:, 320:448] = np.eye(128)
    cst[:, 448:576] = np.where(j[:, None] <= j[None, :], 0.0, NEG)
    return cst


def gb_AE(rel_bias, b_f, head):
    j = np.arange(128)
    rel0 = j[None, :] - j[:, None]
    gbm = np.zeros((128, 258), np.float32)
    col = rel_bias[:, head]
    gbm[:, 0:128] = col[_t5_bucket(rel0)]
    gbm[:, 128:256] = col[_t5_bucket(128 + rel0)]
    gbm[:, 256] = col[31]
    gbm[:, 257] = b_f[head]
    return gbm


def build_AO():
    nc = bass.Bass("TRN2", target_bir_lowering=False)
    dt = nc.dram_tensor
    lat_d = dt("lat", [NCORES, MLA_IN, TOK], BF16, kind="ExternalInput").ap()
    wuq_d = dt("wuq", [512, 384], F32, kind="ExternalInput").ap()
    wukv_d = dt("wukv", [512, 512], F32, kind="ExternalInput").ap()
    cs_d = dt("cs", [64, 2 * S], F32, kind="ExternalInput").ap()
    cst_d = dt("cst", [128, 576], F32, kind="ExternalInput").ap()
    yT_d = dt("yT", [256, S], BF16, kind="ExternalOutput").ap()

    p = Prog(nc)
    cst = p.sbuf("cst", [128, 576], F32)
    rcst = Res("cst")
    p.dma("sp", lambda q: q.dma_start(out=cst[:], in_=cst_d[:, :]), None, writes=[rcst])
    identb = p.sbuf("identb", [128, 128], BF16)
    trib = p.sbuf("trib", [128, 128], BF16)
    rident, rtri = Res("identb"), Res("trib")
    p.op("dve", lambda v: v.tensor_copy(out=identb[:], in_=cst[:, 320:448]), reads=[rcst], writes=[rident])
    p.op("dve", lambda v: v.tensor_copy(out=trib[:], in_=cst[:, 448:576]), reads=[rcst], writes=[rtri])
    Wq = p.sbuf("Wq", [128, 4, 384], BF16)
    Wkv = p.sbuf("Wkv", [128, 4, 512], BF16)
    Wrot = p.sbuf("Wrot", [128, 2, 4, 64], BF16)
    rWq, rWkv, rWrot = Res("Wq"), Res("Wkv"), Res("Wrot")
    s_w = p.new_dma_sem("s_w", exclusive=True)
    p.dma("pool", lambda q: q.dma_start(out=Wq[:], in_=wuq_d.rearrange("(kc p) n -> p kc n", p=128)), s_w, writes=[rWq])
    s_w2 = p.new_dma_sem("s_w2", exclusive=True)
    p.dma("pool", lambda q: q.dma_start(out=Wkv[:], in_=wukv_d.rearrange("(kc p) n -> p kc n", p=128)), s_w2, writes=[rWkv])
    for h in range(2):
        r0 = h * 192 + 128
        p.op("dve", lambda v, h=h, r0=r0: v.tensor_scalar(out=Wrot[:, h, :, 0:32], in0=Wq[:, :, r0 + 32:r0 + 64], scalar1=-1.0,
                                                          scalar2=None, op0=ALU.mult), reads=[rWq], writes=[rWrot])
        p.op("dve", lambda v, h=h, r0=r0: v.tensor_copy(out=Wrot[:, h, :, 32:64], in_=Wq[:, :, r0:r0 + 32]), reads=[rWq], writes=[rWrot])
    QN = p.sbuf("QN", [128, S], BF16)
    KN = p.sbuf("KN", [128, S], BF16)
    QR = p.sbuf("QR", [64, S], BF16)
    KR = p.sbuf("KR", [64, S], BF16)
    V = p.sbuf("V", [128, NKB, 130], BF16)
    rQN = [Res(f"QN{i}") for i in range(NQG)]
    rKN = [Res(f"KN{i}") for i in range(NQG)]
    rQR = [Res(f"QR{i}") for i in range(NQG)]
    rKR = [Res(f"KR{i}") for i in range(NQG)]
    rV = [Res(f"V{i}") for i in range(NQG)]
    latc = [p.sbuf(f"latc{i}", [128, 8, 512], BF16) for i in range(2)]
    rlatc = [Res(f"latc{i}") for i in range(2)]
    csc = [p.sbuf(f"csc{i}", [64, 2, 512], F32) for i in range(2)]
    rcsc = [Res(f"csc{i}") for i in range(2)]
    t0 = p.sbuf("t0", [64, 512], F32)
    t1 = p.sbuf("t1", [64, 512], F32)
    rt0, rt1 = Res("t0"), Res("t1")
    P = [p.psum(f"P{i}", [128, 512], F32) for i in range(7)]
    rP = [Res(f"P{i}") for i in range(7)]
    p.op("pool", lambda g_: g_.memset(V[:, :, 128:130], 1.0), writes=rV)
    cx = AttnCtx(p, P, rP, identb, rident, trib, rtri, yT_d, None)
    SCALE = 192.0 ** -0.5
    rr = [0]

    def nb():
        b = rr[0] % 3
        rr[0] += 1
        return b

    kq = 0
    for h in range(2):
        for tc in range(NQG):
            xi = kq % 2
            kq += 1
            r_, c_ = tc // 2, (tc % 2) * 512
            tsl = slice(tc * 512, (tc + 1) * 512)
            p.dma("sp", lambda q, xi=xi, r_=r_, c_=c_: q.dma_start(
                out=latc[xi][:], in_=lat_d[r_, 0:1024, c_:c_ + 512].rearrange("(c p) t -> p c t", p=128)), None,
                writes=[rlatc[xi]])
            p.dma("sp", lambda q, xi=xi, tsl=tsl: q.dma_start(out=csc[xi][:, 0, :], in_=cs_d[:, tsl]), None, writes=[rcsc[xi]])
            p.dma("sp", lambda q, xi=xi, tc=tc: q.dma_start(out=csc[xi][:, 1, :], in_=cs_d[:, S + tc * 512:S + (tc + 1) * 512]), None,
                  writes=[rcsc[xi]])
            if h == 0:
                p.dma("sp", lambda q, r_=r_, c_=c_, tsl=tsl: q.dma_start(out=KR[:, tsl], in_=lat_d[r_, 1024:1088, c_:c_ + 512]), None,
                      writes=[rKR[tc]])
            for (W, rW, c0, kof, dst, rdst) in ((Wq, rWq, h * 192, 0, QN, rQN), (Wkv, rWkv, h * 256, 4, KN, rKN)):
                b = nb()
                for k in range(4):
                    p.op("pe", lambda t, k=k, b=b, W=W, c0=c0, kof=kof, xi=xi: t.matmul(
                        P[b][:], lhsT=W[:, k, c0:c0 + 128], rhs=latc[xi][:, kof + k, :], start=(k == 0), stop=(k == 3)),
                        reads=[rW, rlatc[xi]], writes=[rP[b]])
                p.op("act", lambda a, b=b, dst=dst, tsl=tsl: a.activation(out=dst[:, tsl], in_=P[b][:], func=AF.Copy),
                     reads=[rP[b]], writes=[rdst[tc]])
            b0, b1 = nb(), nb()
            for k in range(4):
                p.op("pe", lambda t, k=k, b0=b0, h=h, xi=xi: t.matmul(P[b0][0:64, :], lhsT=Wq[:, k, h * 192 + 128:h * 192 + 192],
                                                                       rhs=latc[xi][:, k, :], start=(k == 0), stop=(k == 3)),
                     reads=[rWq, rlatc[xi]], writes=[rP[b0]])
            for k in range(4):
                p.op("pe", lambda t, k=k, b1=b1, h=h, xi=xi: t.matmul(P[b1][0:64, :], lhsT=Wrot[:, h, k, :],
                                                                       rhs=latc[xi][:, k, :], start=(k == 0), stop=(k == 3)),
                     reads=[rWrot, rlatc[xi]], writes=[rP[b1]])
            p.op("dve", lambda v, b0=b0, xi=xi: v.tensor_tensor(out=t0[:], in0=P[b0][0:64, :], in1=csc[xi][:, 0, :], op=ALU.mult),
                 reads=[rP[b0], rcsc[xi]], writes=[rt0])
            p.op("dve", lambda v, b1=b1, xi=xi: v.tensor_tensor(out=t1[:], in0=P[b1][0:64, :], in1=csc[xi][:, 1, :], op=ALU.mult),
                 reads=[rP[b1], rcsc[xi]], writes=[rt1])
            p.op("dve", lambda v, tsl=tsl: v.tensor_tensor(out=QR[:, tsl], in0=t0[:], in1=t1[:], op=ALU.add),
                 reads=[rt0, rt1], writes=[rQR[tc]])
            for tb in range(4):
                b = nb()
                blk = tc * 4 + tb
                for k in range(4):
                    p.op("pe", lambda t, k=k, b=b, tb=tb, h=h, xi=xi: t.matmul(
                        P[b][:, 0:128], lhsT=latc[xi][:, 4 + k, tb * 128:(tb + 1) * 128], rhs=Wkv[:, k, h * 256 + 128:h * 256 + 256],
                        start=(k == 0), stop=(k == 3)), reads=[rWkv, rlatc[xi]], writes=[rP[b]])
                p.op("dve", lambda v, b=b, blk=blk: v.tensor_copy(out=V[:, blk, 0:128], in_=P[b][:, 0:128]),
                     reads=[rP[b]], writes=[rV[tc]])

        def qk_mla(J, g, c0):
            return [(KN[:, J * 128:(J + 1) * 128], QN[:, g * 512 + c0:(g + 1) * 512], [rKN[J // 4], rQN[g]]),
                    (KR[:, J * 128:(J + 1) * 128], QR[:, g * 512 + c0:(g + 1) * 512], [rKR[J // 4], rQR[g]])]

        dense_causal_sweep(cx, qk_mla, None, V, rV, SCALE, h * 128)

    p.emit({"sp": cx.finals})
    p.close()
    return nc


_NC_CACHE = {}


def _prog(name, fn):
    if name not in _NC_CACHE:
        _NC_CACHE[name] = fn()
    return _NC_CACHE[name]


def _lay128(v):
    return np.ascontiguousarray(v.reshape(-1, 128).T)


def _run(nc, in_maps):
    res = run_bass_kernel_spmd(nc, in_maps, core_ids=list(range(NCORES)))
    return res.results


def kernel(x, ab_w_in, ab_forget_bias, ab_w_out, rel_bias, mla_w_in, mla_q_norm, mla_kv_norm, mla_w_uq,
           mla_w_ukv, mla_w_out, ffn_w_gate, ffn_w_up, ffn_w_down, ln_g, ln_b):
    f32 = np.float32
    x = np.asarray(x, f32)
    xT = np.ascontiguousarray(x[0].T)
    xres = [np.ascontiguousarray(xT[:, c * TOK:(c + 1) * TOK]) for c in range(NCORES)]
    cstAE = consts_AE()
    inv = (np.float32(10000.0) ** (-np.arange(0, 64, 2, dtype=f32) / np.float32(64))).astype(f32)
    ang = np.arange(S, dtype=f32)[:, None] * inv[None, :]
    cos, sin = np.cos(ang).astype(f32), np.sin(ang).astype(f32)
    csT = np.ascontiguousarray(np.concatenate([np.concatenate([cos, cos], 1).T, np.concatenate([sin, sin], 1).T], axis=1))

    gathered = None
    for layer in range(DEPTH):
        j = layer // 2
        if layer % 2 == 0:
            if layer == 0:
                nc = _prog("AE32", lambda: build_AE(x_f32=True))
                gathered = np.ascontiguousarray(xT.reshape(D, NCORES, TOK).transpose(1, 0, 2))
            else:
                nc = _prog("AE16", lambda: build_AE(x_f32=False))
            w = np.asarray(ab_w_in[j], f32)
            maps = []
            for c in range(NCORES):
                cols = [w[:, c * 128:(c + 1) * 128], w[:, 1024 + c * 128:1024 + (c + 1) * 128],
                        w[:, 3080 + c * 128:3080 + (c + 1) * 128], w[:, 4104 + c * 128:4104 + (c + 1) * 128],
                        w[:, 2048 + c * 128:2048 + (c + 1) * 128], w[:, 3072 + c:3072 + c + 1],
                        w[:, 5128 + c * 128:5128 + (c + 1) * 128], np.zeros((D, 1), f32)]
                maps.append({"xT": gathered, "wh": np.ascontiguousarray(np.concatenate(cols, axis=1)), "cst": cstAE,
                             "gb": gb_AE(np.asarray(rel_bias, f32), np.asarray(ab_forget_bias[j], f32), c)})
            outs = _run(nc, maps)
            yT_full = np.concatenate([o["yT"][0:128] for o in outs] + [o["yT"][128:256] for o in outs], axis=0)
            w_out = np.asarray(ab_w_out[j], f32)
        else:
            nc = _prog("AO", build_AO)
            wq = np.asarray(mla_w_uq[j], f32)
            wkv = np.asarray(mla_w_ukv[j], f32)
            maps = []
            for c in range(NCORES):
                maps.append({"lat": gathered, "wuq": np.ascontiguousarray(wq[:, c * 384:(c + 1) * 384]),
                             "wukv": np.ascontiguousarray(wkv[:, c * 512:(c + 1) * 512]), "cs": csT, "cst": cstAE})
            outs = _run(nc, maps)
            yT_full = np.concatenate([o["yT"] for o in outs], axis=0)
            w_out = np.asarray(mla_w_out[j], f32)
        mode = "final" if layer == DEPTH - 1 else ("mla" if (layer + 1) % 2 == 1 else "even")
        nc = _prog("T" + mode, lambda: build_T(mode))
        lnp = np.ascontiguousarray(np.concatenate([_lay128(np.asarray(ln_g[layer, 0], f32)), _lay128(np.asarray(ln_b[layer, 0], f32)),
                                                   _lay128(np.asarray(ln_g[layer, 1], f32)), _lay128(np.asarray(ln_b[layer, 1], f32))], axis=1))
        maps = []
        for c in range(NCORES):
            m = {"yT": np.ascontiguousarray(yT_full[:, c * TOK:(c + 1) * TOK]), "xres": xres[c], "wout": w_out, "lnp": lnp,
                 "wg": np.asarray(ffn_w_gate[layer], f32), "wu": np.asarray(ffn_w_up[layer], f32),
                 "wd": np.asarray(ffn_w_down[layer], f32)}
            if mode == "mla":
                jn = (layer + 1) // 2
                m["win"] = np.asarray(mla_w_in[jn], f32)
                m["ng"] = np.ascontiguousarray(np.concatenate([_lay128(np.asarray(mla_q_norm[jn], f32)),
                                                               _lay128(np.asarray(mla_kv_norm[jn], f32))], axis=1))
                m["cs"] = np.ascontiguousarray(np.concatenate([csT[:, c * TOK:(c + 1) * TOK], csT[:, S + c * TOK:S + (c + 1) * TOK]], axis=1))
            maps.append(m)
        outs = _run(nc, maps)
        xres = [o["xout"] for o in outs]
        if mode == "mla":
            gathered = np.ascontiguousarray(np.stack([o["lat"] for o in outs], axis=0))
        elif mode == "even":
            gathered = np.ascontiguousarray(np.stack([o["xTb"] for o in outs], axis=0))
    out = np.concatenate(xres, axis=1)
    return np.ascontiguousarray(out.T)[None].astype(np.float32)
```

```python
import math
import numpy as np
import ml_dtypes
import concourse.bass as bass
import concourse.mybir as mybir
from concourse.bass_utils import run_bass_kernel_spmd

F32 = mybir.dt.float32
BF16 = mybir.dt.bfloat16
AF = mybir.ActivationFunctionType
ALU = mybir.AluOpType
AX = mybir.AxisListType

NCORES = 8
D = 2048
S = 8192
DEPTH = 4
TOK = S // NCORES
DFF = 5632
NFC = DFF // 128
KC = D // 128
ALPHA = (2 * DEPTH) ** 0.25
HD = 128
MLA_IN = 1088
NEG = -30000.0

EPOCH = 20000
SAME_ENGINE_SYNC = True
COMPUTE = ("pe", "act", "dve", "pool")


class Res:
    __slots__ = ("w", "rs", "name")

    def __init__(self, name=""):
        self.w = None
        self.rs = {}
        self.name = name


class Prog:
    def __init__(self, nc, same_engine_sync=None):
        if same_engine_sync is None:
            same_engine_sync = SAME_ENGINE_SYNC
        self.nc = nc
        self.ops = {e: [] for e in ("pe", "act", "dve", "pool", "sp")}
        self.same_engine_sync = same_engine_sync
        self.dma_sems = []
        self._ctx = []

    def enter(self, cm):
        v = cm.__enter__()
        self._ctx.append(cm)
        return v

    def sbuf(self, name, shape, dtype):
        return self.enter(self.nc.sbuf_tensor("sb_" + name, list(shape), dtype))

    def psum(self, name, shape, dtype=F32):
        return self.enter(self.nc.psum_tensor("ps_" + name, list(shape), dtype))

    def new_dma_sem(self, name, exclusive=False):
        if not exclusive:
            return None
        h = self.enter(self.nc.semaphore(name))
        self.dma_sems.append([h, 0])
        return len(self.dma_sems) - 1

    NRING = 24

    def _ring_sem(self, queue):
        if not hasattr(self, "_rings"):
            self._rings = {}
        if queue not in self._rings:
            n = self.NRING if queue == "sp" else 8
            idx = []
            for i in range(n):
                h = self.enter(self.nc.semaphore(f"ring_{queue}{i}"))
                self.dma_sems.append([h, 0])
                idx.append(len(self.dma_sems) - 1)
            self._rings[queue] = [idx, [None] * n, 0]
        rg = self._rings[queue]
        i = rg[2] % len(rg[0])
        rg[2] += 1
        return rg, i

    def _deps(self, eng, reads, writes):
        deps = set()
        for r in reads:
            if r.w is not None:
                deps.add(r.w)
        for w in writes:
            if w.w is not None:
                deps.add(w.w)
            deps.update(w.rs.values())
        out = []
        for d in deps:
            if d[0] == "E":
                if d[1] == eng and (eng == "pe" or not self.same_engine_sync):
                    continue
                self.ops[d[1]][d[2]][2] = True
            out.append(d)
        return out

    def op(self, eng, fn, reads=(), writes=()):
        deps = self._deps(eng, reads, writes)
        idx = len(self.ops[eng])
        ev = ("E", eng, idx)
        self.ops[eng].append([fn, deps, False, None])
        for r in reads:
            r.rs[eng] = ev
        for w in writes:
            w.w = ev
            w.rs = {}
        return ev

    def dma(self, queue, fn, sem, reads=(), writes=()):
        deps = self._deps(queue, reads, writes)
        if sem is None:
            rg, ri = self._ring_sem(queue)
            sem = rg[0][ri]
            if rg[1][ri] is not None:
                deps.append(rg[1][ri])
            rg[1][ri] = ("S", sem, self.dma_sems[sem][1] + 16)
        self.dma_sems[sem][1] += 16
        ev = ("S", sem, self.dma_sems[sem][1])
        self.ops[queue].append([fn, deps, False, sem])
        for r in reads:
            r.rs[("S", sem)] = ev
        for w in writes:
            w.w = ev
            w.rs = {}
        return ev

    def emit(self, final_events):
        nc = self.nc
        sigval = {}
        esems = {}
        for e in COMPUTE + ("sp",):
            c = 0
            vals = []
            for o in self.ops[e]:
                if o[2] and o[3] is None:
                    c += 1
                vals.append(c)
            sigval[e] = vals
            n_ep = max((c + EPOCH - 1) // EPOCH, 1)
            esems[e] = [self.enter(nc.semaphore(f"s_{e}_{k}")) for k in range(n_ep)]

        def resolve(d):
            if d[0] == "S":
                return ("S", d[1]), self.dma_sems[d[1]][0], d[2]
            v = sigval[d[1]][d[2]]
            ep = (v - 1) // EPOCH
            return ("E", d[1], ep), esems[d[1]][ep], v - ep * EPOCH

        engmap = {"pe": "tensor", "act": "scalar", "dve": "vector", "pool": "gpsimd", "sp": "sync"}
        prog = self

        def body_for(e):
            def body(engine):
                waited = {}
                for i, (fn, deps, sig, dsem) in enumerate(prog.ops[e]):
                    for d in deps:
                        key, sem, val = resolve(d)
                        if waited.get(key, 0) < val:
                            engine.wait_ge(sem, val)
                            waited[key] = val
                    inst = fn(engine)
                    if dsem is not None:
                        inst.then_inc(prog.dma_sems[dsem][0], 16)
                    elif sig:
                        v = sigval[e][i]
                        ep = (v - 1) // EPOCH
                        inst.then_inc(esems[e][ep], 1)
                for d in final_events.get(e, ()):
                    key, sem, val = resolve(d)
                    engine.wait_ge(sem, val)
            return body

        with nc.Block() as block:
            for e in COMPUTE + ("sp",):
                if not self.ops[e] and not final_events.get(e):
                    continue
                getattr(block, engmap[e])(body_for(e))

    def close(self):
        while self._ctx:
            self._ctx.pop().__exit__(None, None, None)


class Slots:
    def __init__(self, p, name, n, nbytes):
        self.p = p
        self.n = n
        self.t = [p.sbuf(f"{name}{i}", [128, nbytes // 2], BF16) for i in range(n)]
        self.r = [Res(f"{name}{i}") for i in range(n)]
        self.sem = [p.new_dma_sem(f"{name}s{i}", exclusive=True) for i in range(n)]
        self.k = 0

    def load(self, shape, src_ap, queue="pool"):
        i = self.k % self.n
        self.k += 1
        a, b = shape
        view = self.t[i][:, 0:a * b].rearrange("p (a b) -> p a b", a=a)
        self.p.dma(queue, lambda q, v=view, s=src_ap: q.dma_start(out=v, in_=s), self.sem[i],
                   writes=[self.r[i]])
        return view, self.r[i]


def build_T(mode_next, has_mixer=True):
    nc = bass.Bass("TRN2", target_bir_lowering=False)
    dt = nc.dram_tensor
    yT_d = dt("yT", [D, TOK], BF16, kind="ExternalInput").ap()
    xres_d = dt("xres", [D, TOK], F32, kind="ExternalInput").ap()
    wout_d = dt("wout", [D, D], F32, kind="ExternalInput").ap()
    lnp_d = dt("lnp", [128, 4 * KC], F32, kind="ExternalInput").ap()
    wg_d = dt("wg", [D, DFF], F32, kind="ExternalInput").ap()
    wu_d = dt("wu", [D, DFF], F32, kind="ExternalInput").ap()
    wd_d = dt("wd", [DFF, D], F32, kind="ExternalInput").ap()
    xout_d = dt("xout", [D, TOK], F32, kind="ExternalOutput").ap()
    if mode_next == "even":
        xTb_d = dt("xTb", [D, TOK], BF16, kind="ExternalOutput").ap()
    if mode_next == "mla":
        win_d = dt("win", [D, MLA_IN], F32, kind="ExternalInput").ap()
        ng_d = dt("ng", [128, 8], F32, kind="ExternalInput").ap()
        cs_d = dt("cs", [64, 2 * TOK], F32, kind="ExternalInput").ap()
        lat_d = dt("lat", [MLA_IN, TOK], BF16, kind="ExternalOutput").ap()

    p = Prog(nc)
    z = p.sbuf("z", [128, KC, TOK], F32)
    xb = p.sbuf("xb", [128, KC, TOK], BF16)
    hg = [p.sbuf(f"hg{i}", [128, 4, TOK], BF16) for i in range(2)]
    ws = Slots(p, "ws", 4, 16384)
    lnp = p.sbuf("lnp", [128, 4 * KC], F32)
    ones = p.sbuf("ones", [128, 128], F32)
    mean = p.sbuf("mean", [128, TOK], F32)
    rstd = p.sbuf("rstd", [128, TOK], F32)
    tmpf = [p.sbuf(f"tmpf{i}", [128, 512], F32) for i in range(4)]
    P = [p.psum(f"P{i}", [128, 512], F32) for i in range(8)]
    rP = [Res(f"P{i}") for i in range(8)]
    rz = [[Res(f"z{m}_{h}") for h in range(2)] for m in range(KC)]
    rxb = [[Res(f"xb{m}_{h}") for h in range(2)] for m in range(KC)]
    rhg = [[[Res(f"hg{i}_{f}_{h}") for h in range(2)] for f in range(4)] for i in range(2)]
    rtmp = [Res(f"tmp{i}") for i in range(4)]
    rlnp, rones, rmean, rrstd = Res("lnp"), Res("ones"), [Res("mean0"), Res("mean1")], [Res("rstd0"), Res("rstd1")]
    s_in = p.new_dma_sem("s_in")
    s_in2 = p.new_dma_sem("s_in2")
    s_out = p.new_dma_sem("s_out")
    H = [slice(0, 512), slice(512, 1024)]

    p.dma("sp", lambda q: q.dma_start(out=lnp[:], in_=lnp_d[:, :]), s_in2, writes=[rlnp])
    p.op("dve", lambda v: v.memset(ones[:], 1.0 / D), writes=[rones])
    yv = yT_d.rearrange("(kc p) t -> p kc t", p=128)
    xv = xres_d.rearrange("(kc p) t -> p kc t", p=128)
    for m in range(KC):
        if has_mixer:
            p.dma("sp", lambda q, m=m: q.dma_start(out=xb[:, m, :], in_=yv[:, m, :]), s_in,
                  writes=[rxb[m][0], rxb[m][1]])
        p.dma("sp", lambda q, m=m: q.dma_start(out=z[:, m, :], in_=xv[:, m, :]), s_in2,
              writes=[rz[m][0], rz[m][1]])

    bank_rr = [0]

    def proj_accum(nk, lhs_fn, rhs_fn, rhs_res, w_res, bank):
        for k in range(nk):
            p.op("pe", lambda t, k=k, bank=bank: t.matmul(P[bank][:], lhsT=lhs_fn(k), rhs=rhs_fn(k),
                                                           start=(k == 0), stop=(k == nk - 1)),
                 reads=[w_res, rhs_res(k)], writes=[rP[bank]])

    if has_mixer:
        wv = wout_d.rearrange("(kc p) n -> p kc n", p=128)
        for m4 in range(4):
            slab, rs = ws.load((KC, 512), wv[:, :, m4 * 512:(m4 + 1) * 512])
            for mi in range(4):
                m = m4 * 4 + mi
                for h in range(2):
                    bank = bank_rr[0] % 4
                    bank_rr[0] += 1
                    proj_accum(KC, lambda k, slab=slab, mi=mi: slab[:, k, mi * 128:(mi + 1) * 128],
                               lambda k, h=h: xb[:, k, H[h]], lambda k, h=h: rxb[k][h], rs, bank)
                    p.op("dve", lambda v, m=m, h=h, bank=bank: v.scalar_tensor_tensor(
                        out=z[:, m, H[h]], in0=z[:, m, H[h]], scalar=ALPHA, in1=P[bank][:],
                        op0=ALU.mult, op1=ALU.add), reads=[rP[bank], rz[m][h]], writes=[rz[m][h]])

    def layer_norm(gi, bi):
        for h in range(2):
            for k in range(KC):
                p.op("pe", lambda t, k=k, h=h: t.matmul(P[4 + h][:], lhsT=ones[:], rhs=z[:, k, H[h]],
                                                         start=(k == 0), stop=(k == KC - 1)),
                     reads=[rones, rz[k][h]], writes=[rP[4 + h]])
            for k in range(KC):
                ti = k % 2
                p.op("act", lambda a, k=k, h=h, ti=ti: a.activation(out=tmpf[ti][:], in_=z[:, k, H[h]], func=AF.Square),
                     reads=[rz[k][h]], writes=[rtmp[ti]])
                p.op("pe", lambda t, k=k, h=h, ti=ti: t.matmul(P[6 + h][:], lhsT=ones[:], rhs=tmpf[ti][:],
                                                                start=(k == 0), stop=(k == KC - 1)),
                     reads=[rones, rtmp[ti]], writes=[rP[6 + h]])
            p.op("dve", lambda v, h=h: v.tensor_copy(out=mean[:, H[h]], in_=P[4 + h][:]),
                 reads=[rP[4 + h]], writes=[rmean[h]])
            p.op("dve", lambda v, h=h: v.tensor_tensor(out=tmpf[2][:], in0=mean[:, H[h]], in1=mean[:, H[h]], op=ALU.mult),
                 reads=[rmean[h]], writes=[rtmp[2]])
            p.op("dve", lambda v, h=h: v.tensor_tensor(out=tmpf[2][:], in0=P[6 + h][:], in1=tmpf[2][:], op=ALU.subtract),
                 reads=[rP[6 + h], rtmp[2]], writes=[rtmp[2]])
            p.op("dve", lambda v, h=h: v.tensor_scalar(out=tmpf[2][:], in0=tmpf[2][:], scalar1=1e-5, scalar2=None, op0=ALU.add),
                 reads=[rtmp[2]], writes=[rtmp[2]])
            p.op("act", lambda a, h=h: a.activation(out=tmpf[3][:], in_=tmpf[2][:], func=AF.Sqrt),
                 reads=[rtmp[2]], writes=[rtmp[3]])
            p.op("dve", lambda v, h=h: v.reciprocal(out=rstd[:, H[h]], in_=tmpf[3][:]),
                 reads=[rtmp[3]], writes=[rrstd[h]])
        for h in range(2):
            for k in range(KC):
                p.op("dve", lambda g, k=k, h=h: g.tensor_tensor(out=z[:, k, H[h]], in0=z[:, k, H[h]], in1=mean[:, H[h]], op=ALU.subtract),
                     reads=[rz[k][h], rmean[h]], writes=[rz[k][h]])
                p.op("dve", lambda v, k=k, h=h: v.tensor_tensor(out=z[:, k, H[h]], in0=z[:, k, H[h]], in1=rstd[:, H[h]], op=ALU.mult),
                     reads=[rz[k][h], rrstd[h]], writes=[rz[k][h]])
                p.op("act", lambda a, k=k, h=h: a.activation(out=z[:, k, H[h]], in_=z[:, k, H[h]], func=AF.Identity,
                                                              scale=lnp[:, gi * KC + k:gi * KC + k + 1],
                                                              bias=lnp[:, bi * KC + k:bi * KC + k + 1]),
                     reads=[rz[k][h], rlnp], writes=[rz[k][h]])
                p.op("act", lambda v, k=k, h=h: v.activation(out=xb[:, k, H[h]], in_=z[:, k, H[h]], func=AF.Copy),
                     reads=[rz[k][h]], writes=[rxb[k][h]])

    layer_norm(0, 1)

    gv = wg_d.rearrange("(kc p) n -> p kc n", p=128)
    uv = wu_d.rearrange("(kc p) n -> p kc n", p=128)
    dv = wd_d.rearrange("(fc p) n -> p fc n", p=128)
    NG = NFC // 4
    for g in range(NG):
        gslab, rg = ws.load((KC, 512), gv[:, :, g * 512:(g + 1) * 512])
        uslab, ru = ws.load((KC, 512), uv[:, :, g * 512:(g + 1) * 512])
        hb = hg[g % 2]
        rh = rhg[g % 2]
        for fi in range(4):
            for h in range(2):
                bg = bank_rr[0] % 4
                bu = (bank_rr[0] + 1) % 4
                bank_rr[0] += 2
                proj_accum(KC, lambda k, fi=fi: gslab[:, k, fi * 128:(fi + 1) * 128],
                           lambda k, h=h: xb[:, k, H[h]], lambda k, h=h: rxb[k][h], rg, bg)
                proj_accum(KC, lambda k, fi=fi: uslab[:, k, fi * 128:(fi + 1) * 128],
                           lambda k, h=h: xb[:, k, H[h]], lambda k, h=h: rxb[k][h], ru, bu)
                ti = (fi * 2 + h) % 2
                p.op("act", lambda a, bg=bg, ti=ti: a.activation(out=tmpf[ti][:], in_=P[bg][:], func=AF.Silu),
                     reads=[rP[bg]], writes=[rtmp[ti]])
                p.op("dve", lambda v, bu=bu, ti=ti, fi=fi, h=h, hb=hb: v.tensor_tensor(
                    out=hb[:, fi, H[h]], in0=P[bu][:], in1=tmpf[ti][:], op=ALU.mult),
                    reads=[rP[bu], rtmp[ti]], writes=[rh[fi][h]])
        for half in range(2):
            dslab, rd = ws.load((4, 1024), dv[:, g * 4:(g + 1) * 4, half * 1024:(half + 1) * 1024])
            for mi in range(8):
                m = half * 8 + mi
                for h in range(2):
                    bank = 4 + (bank_rr[0] % 4)
                    bank_rr[0] += 1
                    proj_accum(4, lambda k, mi=mi, dslab=dslab: dslab[:, k, mi * 128:(mi + 1) * 128],
                               lambda k, h=h, hb=hb: hb[:, k, H[h]], lambda k, h=h, rh=rh: rh[k][h], rd, bank)
                    if g == 0:
                        p.op("dve", lambda v, m=m, h=h, bank=bank: v.scalar_tensor_tensor(
                            out=z[:, m, H[h]], in0=z[:, m, H[h]], scalar=ALPHA, in1=P[bank][:],
                            op0=ALU.mult, op1=ALU.add), reads=[rP[bank], rz[m][h]], writes=[rz[m][h]])
                    else:
                        p.op("dve", lambda v, m=m, h=h, bank=bank: v.tensor_tensor(
                            out=z[:, m, H[h]], in0=z[:, m, H[h]], in1=P[bank][:], op=ALU.add),
                            reads=[rP[bank], rz[m][h]], writes=[rz[m][h]])

    layer_norm(2, 3)

    finals = []
    xo = xout_d.rearrange("(kc p) t -> p kc t", p=128)
    for m in range(KC):
        finals.append(p.dma("sp", lambda q, m=m: q.dma_start(out=xo[:, m, :], in_=z[:, m, :]), s_out,
                            reads=[rz[m][0], rz[m][1]]))
    if mode_next == "even":
        xt = xTb_d.rearrange("(kc p) t -> p kc t", p=128)
        for m in range(KC):
            finals.append(p.dma("sp", lambda q, m=m: q.dma_start(out=xt[:, m, :], in_=xb[:, m, :]), s_out,
                                reads=[rxb[m][0], rxb[m][1]]))
    if mode_next == "mla":
        finals += mla_latents(p, nc, ws, xb, rxb, P, rP, bank_rr, tmpf, rtmp, H,
                              win_d, ng_d, cs_d, lat_d, s_in2, s_out,
                              hg, rhg, mean, rstd, rmean, rrstd)
    p.emit({"sp": [finals[-1]] if False else finals})
    p.close()
    return nc


def mla_latents(p, nc, ws, xb, rxb, P, rP, bank_rr, tmpf, rtmp, H,
                win_d, ng_d, cs_d, lat_d, s_in, s_out, hg, rhg, mean, rstd, rmean, rrstd):
    finals = []
    ng = p.sbuf("ng", [128, 8], F32)
    rng_ = Res("ng")
    p.dma("sp", lambda q: q.dma_start(out=ng[:], in_=ng_d[:, :]), s_in, writes=[rng_])
    all_mean = [rmean[0], rmean[1]]
    all_rstd = [rrstd[0], rrstd[1]]
    p.dma("sp", lambda q: q.dma_start(out=mean[0:64, :], in_=cs_d[:, 0:TOK]), s_in, writes=all_mean)
    p.dma("sp", lambda q: q.dma_start(out=rstd[0:64, :], in_=cs_d[:, TOK:2 * TOK]), s_in, writes=all_rstd)
    lat = hg[0].bitcast(F32)
    latb = hg[1]
    rl0 = [r for f in rhg[0] for r in f]
    rl1 = [r for f in rhg[1] for r in f]
    ones512 = p.sbuf("ones512", [128, 128], F32)
    r512 = Res("ones512")
    p.op("dve", lambda v: v.memset(ones512[:], 1.0 / 512), writes=[r512])
    wv = win_d.rearrange("(kc p) n -> p kc n", p=128)
    for li in range(2):
        slab, rs = ws.load((KC, 512), wv[:, :, li * 512:(li + 1) * 512])
        for h in range(2):
            for c in range(4):
                bank = bank_rr[0] % 4
                bank_rr[0] += 1
                for k in range(KC):
                    p.op("pe", lambda t, k=k, c=c, h=h, bank=bank, slab=slab: t.matmul(
                        P[bank][:], lhsT=slab[:, k, c * 128:(c + 1) * 128], rhs=xb[:, k, H[h]],
                        start=(k == 0), stop=(k == KC - 1)), reads=[rs, rxb[k][h]], writes=[rP[bank]])
                p.op("act", lambda a, c=c, bank=bank: a.activation(out=lat[:, c, :], in_=P[bank][:], func=AF.Copy),
                     reads=[rP[bank]], writes=rl0)
            for c in range(4):
                ti = c % 2
                p.op("act", lambda a, c=c, ti=ti: a.activation(out=tmpf[ti][:], in_=lat[:, c, :], func=AF.Square),
                     reads=rl0, writes=[rtmp[ti]])
                p.op("pe", lambda t, c=c, h=h, ti=ti: t.matmul(P[6 + h][:], lhsT=ones512[:], rhs=tmpf[ti][:],
                                                                start=(c == 0), stop=(c == 3)),
                     reads=[r512, rtmp[ti]], writes=[rP[6 + h]])
            p.op("dve", lambda v, h=h: v.tensor_scalar(out=tmpf[2][:], in0=P[6 + h][:], scalar1=1e-6, scalar2=None, op0=ALU.add),
                 reads=[rP[6 + h]], writes=[rtmp[2]])
            p.op("act", lambda a: a.activation(out=tmpf[3][:], in_=tmpf[2][:], func=AF.Sqrt),
                 reads=[rtmp[2]], writes=[rtmp[3]])
            p.op("dve", lambda v: v.reciprocal(out=tmpf[2][:], in_=tmpf[3][:]), reads=[rtmp[3]], writes=[rtmp[2]])
            for c in range(4):
                p.op("dve", lambda v, c=c: v.tensor_tensor(out=lat[:, c, :], in0=lat[:, c, :], in1=tmpf[2][:], op=ALU.mult),
                     reads=rl0 + [rtmp[2]], writes=rl0)
                p.op("act", lambda a, c=c, li=li: a.activation(out=latb[:, c, 0:512], in_=lat[:, c, :], func=AF.Identity,
                                                               scale=ng[:, li * 4 + c:li * 4 + c + 1]),
                     reads=rl0 + [rng_], writes=rl1)
            lv = lat_d[li * 512:(li + 1) * 512, H[h]].rearrange("(c p) t -> p c t", p=128)
            finals.append(p.dma("sp", lambda q, lv=lv: q.dma_start(out=lv, in_=latb[:, :, 0:512]), s_out, reads=rl1))
    slab, rs = ws.load((KC, 64), wv[:, :, 1024:1088])
    wrot = p.sbuf("wrot", [128, KC, 64], BF16)
    rwrot = Res("wrot")
    p.op("dve", lambda v: v.tensor_scalar(out=wrot[:, :, 0:32], in0=slab[:, :, 32:64], scalar1=-1.0, scalar2=None, op0=ALU.mult),
         reads=[rs], writes=[rwrot])
    p.op("dve", lambda v: v.tensor_copy(out=wrot[:, :, 32:64], in_=slab[:, :, 0:32]), reads=[rs], writes=[rwrot])
    kr = p.sbuf("kr", [64, TOK], BF16)
    rkr = [Res("kr0"), Res("kr1")]
    for h in range(2):
        b0 = bank_rr[0] % 4
        b1 = (bank_rr[0] + 1) % 4
        bank_rr[0] += 2
        for k in range(KC):
            p.op("pe", lambda t, k=k, h=h, b0=b0: t.matmul(P[b0][0:64, :], lhsT=slab[:, k, :], rhs=xb[:, k, H[h]],
                                                            start=(k == 0), stop=(k == KC - 1)),
                 reads=[rs, rxb[k][h]], writes=[rP[b0]])
        for k in range(KC):
            p.op("pe", lambda t, k=k, h=h, b1=b1: t.matmul(P[b1][0:64, :], lhsT=wrot[:, k, :], rhs=xb[:, k, H[h]],
                                                            start=(k == 0), stop=(k == KC - 1)),
                 reads=[rwrot, rxb[k][h]], writes=[rP[b1]])
        p.op("dve", lambda v, h=h, b0=b0: v.tensor_tensor(out=tmpf[0][0:64, :], in0=P[b0][0:64, :], in1=mean[0:64, H[h]], op=ALU.mult),
             reads=[rP[b0]] + all_mean, writes=[rtmp[0]])
        p.op("dve", lambda v, h=h, b1=b1: v.tensor_tensor(out=tmpf[1][0:64, :], in0=P[b1][0:64, :],
                                                          in1=rstd[0:64, H[h]], op=ALU.mult),
             reads=[rP[b1]] + all_rstd, writes=[rtmp[1]])
        p.op("dve", lambda v, h=h: v.tensor_tensor(out=kr[:, H[h]], in0=tmpf[0][0:64, :], in1=tmpf[1][0:64, :], op=ALU.add),
             reads=[rtmp[0], rtmp[1]], writes=[rkr[h]])
    finals.append(p.dma("sp", lambda q: q.dma_start(out=lat_d[1024:1088, :], in_=kr[:]), s_out, reads=rkr))
    return finals


NQG = S // 512
NKB = S // 128
YOFF = [0, 129, 258, 0]
YBNK = [0, 0, 0, 1]


class AttnCtx:
    def __init__(self, p, P, rP, identb, rident, trib, rtri, yT_d, s_out):
        self.p, self.P, self.rP = p, P, rP
        self.identb, self.rident, self.trib, self.rtri = identb, rident, trib, rtri
        self.pt = [p.sbuf(f"pt{i}", [128, 512], BF16) for i in range(4)]
        self.rpt = [Res(f"pt{i}") for i in range(4)]
        self.yn = [p.sbuf(f"yn{i}", [128, 128], BF16) for i in range(4)]
        self.ryn = [Res(f"yn{i}") for i in range(4)]
        self.pending = []
        self.ytb = [p.sbuf(f"ytb{i}", [128, 512], BF16) for i in range(2)]
        self.rytb = [Res(f"ytb{i}") for i in range(2)]
        self.rinv = [p.sbuf(f"rinv{i}", [128, 1], F32) for i in range(4)]
        self.rrinv = [Res(f"rinv{i}") for i in range(4)]
        self.tp = p.enter(p.nc.psum_tensor("ps_tp", [128, 1024], BF16))
        self.rtp = Res("tp")
        self.yT_d, self.s_out = yT_d, s_out
        self.finals = []
        self.kpt = 0
        self.ksb = 0
        self.kyn = 0
        self.kg = 0

    def finish_block(self, src_ap, sum_ap, src_res, s, row0, g, last):
        p = self.p
        i = self.kyn % 4
        self.kyn += 1
        p.op("dve", lambda v, i=i: v.reciprocal(out=self.rinv[i][:], in_=sum_ap), reads=src_res, writes=[self.rrinv[i]])
        p.op("dve", lambda v, i=i: v.tensor_scalar(out=self.yn[i][:], in0=src_ap, scalar1=self.rinv[i][:, 0:1], scalar2=None,
                                                   op0=ALU.mult),
             reads=list(src_res) + [self.rrinv[i]], writes=[self.ryn[i]])
        self.pending.append([2, i, s, row0, g, last])

    def tick(self, flush=False):
        keep = []
        for item in self.pending:
            item[0] -= 1
            if item[0] > 0 and not flush:
                keep.append(item)
                continue
            _, i, s, row0, g, last = item
            p = self.p
            gb = self.kg % 2
            p.op("pe", lambda t, i=i: t.transpose(out=self.tp[:, 0:128], in_=self.yn[i][:], identity=self.identb[:]),
                 reads=[self.ryn[i], self.rident], writes=[self.rtp])
            p.op("dve", lambda v, s=s, gb=gb: v.tensor_copy(out=self.ytb[gb][:, s * 128:(s + 1) * 128], in_=self.tp[:, 0:128]),
                 reads=[self.rtp], writes=[self.rytb[gb]])
            if last:
                self.finals.append(p.dma("sp", lambda q, gb=gb, row0=row0, g=g: q.dma_start(
                    out=self.yT_d[row0:row0 + 128, g * 512:(g + 1) * 512], in_=self.ytb[gb][:]), self.s_out,
                    reads=[self.rytb[gb]]))
                self.kg += 1
        self.pending = keep


def dense_causal_sweep(cx, qk_fn, bias_fn, V, rV, scale, row0):
    p, P, rP = cx.p, cx.P, cx.rP
    steps = [(g, J) for g in range(NQG) for J in range(4 * g + 4)]

    def part_a(g, J):
        r = J - 4 * g
        c0 = max(r, 0) * 128
        sb = cx.ksb % 3
        cx.ksb += 1
        terms = qk_fn(J, g, c0)
        for ti, (lh, rh, rd) in enumerate(terms):
            p.op("pe", lambda t, lh=lh, rh=rh, ti=ti, sb=sb, c0=c0, nt=len(terms), r=r: t.matmul(
                P[sb][:, c0:512], lhsT=lh, rhs=rh, start=(ti == 0), stop=(ti == nt - 1 and r < 0)),
                reads=rd, writes=[rP[sb]])
        if r >= 0:
            p.op("pe", lambda t, sb=sb, c0=c0: t.matmul(P[sb][:, c0:c0 + 128], lhsT=cx.identb[:], rhs=cx.trib[:],
                                                        start=False, stop=True),
                 reads=[cx.rident, cx.rtri], writes=[rP[sb]])
        pi = cx.kpt % 4
        cx.kpt += 1
        b = bias_fn(g, J) if bias_fn else None
        if b is None:
            p.op("act", lambda a, pi=pi, sb=sb, c0=c0: a.activation(out=cx.pt[pi][:, c0:512], in_=P[sb][:, c0:512],
                                                                      func=AF.Exp, scale=scale),
                 reads=[rP[sb]], writes=[cx.rpt[pi]])
        else:
            bap, brd = b
            p.op("act", lambda a, pi=pi, sb=sb, c0=c0, bap=bap: a.activation(
                out=cx.pt[pi][:, c0:512], in_=P[sb][:, c0:512], func=AF.Exp, scale=scale, bias=bap),
                reads=[rP[sb]] + brd, writes=[cx.rpt[pi]])
        return pi

    def part_b(g, J, pi):
        r = J - 4 * g
        for s in range(max(r, 0), 4):
            bank = 3 + s
            p.op("pe", lambda t, pi=pi, s=s, bank=bank, J=J, g=g: t.matmul(
                P[bank][:, 0:129], lhsT=cx.pt[pi][:, s * 128:(s + 1) * 128], rhs=V[:, J, 0:129],
                start=(J == 0), stop=(J == 4 * g + s)),
                reads=[cx.rpt[pi], rV[J // 4]], writes=[rP[bank]])
            if J == 4 * g + s:
                cx.finish_block(P[bank][:, 0:128], P[bank][:, 128:129], [rP[bank]], s, row0, g, s == 3)

    pis = [part_a(*steps[0]), part_a(*steps[1])]
    for i, (g, J) in enumerate(steps):
        if i + 2 < len(steps):
            pis.append(part_a(*steps[i + 2]))
        part_b(g, J, pis[i])
        cx.tick()
    cx.tick(flush=True)


def build_AE(x_f32=False, stop=None, flags="qkvgABFrlhmx"):
    nc = bass.Bass("TRN2", target_bir_lowering=False)
    dt = nc.dram_tensor
    xT_d = dt("xT", [NCORES, D, TOK], F32 if x_f32 else BF16, kind="ExternalInput").ap()
    wh_d = dt("wh", [D, 770], F32, kind="ExternalInput").ap()
    cst_d = dt("cst", [128, 576], F32, kind="ExternalInput").ap()
    gb_d = dt("gb", [128, 258], F32, kind="ExternalInput").ap()
    yT_d = dt("yT", [256, S], BF16, kind="ExternalOutput").ap()

    p = Prog(nc)
    s_in = p.new_dma_sem("s_in")
    s_w = p.new_dma_sem("s_w")
    s_out = p.new_dma_sem("s_out")
    s_x = [p.new_dma_sem(f"s_x{i}") for i in range(2)]
    s_xe = [p.new_dma_sem(f"s_xe{i}", exclusive=True) for i in range(2)]
    cst = p.sbuf("cst", [128, 576], F32)
    rcst = Res("cst")
    p.dma("sp", lambda q: q.dma_start(out=cst[:], in_=cst_d[:, :]), s_in, writes=[rcst])
    gb = p.sbuf("gb", [128, 258], F32)
    rgb = Res("gb")
    p.dma("sp", lambda q: q.dma_start(out=gb[:], in_=gb_d[:, :]), s_in, writes=[rgb])
    U, LST, E127, IDF, TRIF = cst[:, 0:128], cst[0:64, 128:192], cst[:, 192:320], cst[:, 320:448], cst[:, 448:576]
    identb = p.sbuf("identb", [128, 128], BF16)
    trib = p.sbuf("trib", [128, 128], BF16)
    d0b = p.sbuf("d0b", [128, 128], BF16)
    d1b = p.sbuf("d1b", [128, 128], BF16)
    rident, rtri, rd0, rd1 = Res("identb"), Res("trib"), Res("d0b"), Res("d1b")
    tmpc = p.sbuf("tmpc", [128, 128], F32)
    rtmpc = Res("tmpc")
    p.op("dve", lambda v: v.tensor_copy(out=identb[:], in_=IDF), reads=[rcst], writes=[rident])
    p.op("dve", lambda v: v.tensor_copy(out=trib[:], in_=TRIF), reads=[rcst], writes=[rtri])
    SQ = math.sqrt(HD)
    p.op("dve", lambda v: v.tensor_scalar(out=tmpc[:], in0=gb[:, 0:128], scalar1=gb[:, 256:257], scalar2=SQ,
                                          op0=ALU.subtract, op1=ALU.mult), reads=[rgb], writes=[rtmpc])
    p.op("dve", lambda v: v.tensor_tensor(out=d0b[:], in0=tmpc[:], in1=TRIF, op=ALU.add), reads=[rtmpc, rcst], writes=[rd0])
    p.op("dve", lambda v: v.tensor_scalar(out=d1b[:], in0=gb[:, 128:256], scalar1=gb[:, 256:257], scalar2=SQ,
                                          op0=ALU.subtract, op1=ALU.mult), reads=[rgb], writes=[rd1])
    Wh = p.sbuf("Wh", [128, KC, 770], BF16)
    rWh = Res("Wh")
    whv = wh_d.rearrange("(kc p) n -> p kc n", p=128)
    for k4 in range(4):
        p.dma("pool", lambda q, k4=k4: q.dma_start(out=Wh[:, k4 * 4:(k4 + 1) * 4, :], in_=whv[:, k4 * 4:(k4 + 1) * 4, :]),
              s_w, writes=[rWh])
    QAT = p.sbuf("QAT", [128, S], BF16)
    KAT = p.sbuf("KAT", [128, S], BF16)
    QBT = p.sbuf("QBT", [128, S], BF16)
    KBT = p.sbuf("KBT", [128, S], BF16)
    VA = p.sbuf("VA", [128, NKB, 130], BF16)
    VB = p.sbuf("VB", [128, NKB, 130], BF16)
    rQA = [Res(f"QA{i}") for i in range(NQG)]
    rKA = [Res(f"KA{i}") for i in range(NQG)]
    rQB = [Res(f"QB{i}") for i in range(NQG)]
    rKB = [Res(f"KB{i}") for i in range(NQG)]
    rVA = [Res(f"VA{i}") for i in range(NQG)]
    rVB = [Res(f"VB{i}") for i in range(NQG)]
    Fl = p.sbuf("Fl", [128, NKB], F32)
    rF = Res("F")
    kmT = p.sbuf("kmT", [128, 32], F32)
    rkm = Res("kmT")
    sel = p.sbuf("sel", [128, NKB, 32], F32)
    rsel = [Res(f"sel{i}") for i in range(NKB)]
    gm = p.sbuf("gm", [128, 32], F32)
    rgm = Res("gm")
    m8 = p.sbuf("m8", [128, 8], F32)
    rm8 = Res("m8")
    qlo = [p.sbuf(f"qlo{i}", [128, 512], BF16) for i in range(2)]
    kmh = p.sbuf("kmh", [128, 32], BF16)
    kml = p.sbuf("kml", [128, 32], BF16)
    rkmh = Res("kmh")
    rqf = [Res(f"qf{i}") for i in range(2)]
    xc = [p.sbuf(f"xc{i}", [128, KC, 512], BF16) for i in range(2)]
    rxc = [Res(f"xc{i}") for i in range(2)]
    P = [p.psum(f"P{i}", [128, 512], F32) for i in range(7)]
    rP = [Res(f"P{i}") for i in range(7)]
    p.op("dve", lambda v: v.memset(kmT[:], 0.0), writes=[rkm])
    p.op("dve", lambda v: v.memset(gm[:], -1e30), writes=[rgm])
    for i, (Vt, rVt) in enumerate(((VA, rVA), (VB, rVB))):
        p.op("pool", lambda g_, Vt=Vt: g_.memset(Vt[:, :, 128:130], 1.0), writes=rVt)

    rr = [0]

    def nb():
        b = rr[0] % 7
        rr[0] += 1
        return b

    for tc in range(NQG):
        xi = tc % 2
        src = xT_d[tc // 2, :, (tc % 2) * 512:(tc % 2 + 1) * 512].rearrange("(kc p) t -> p kc t", p=128)
        if x_f32:
            for k4 in range(4):
                ev_ = p.dma("pool", lambda q, xi=xi, src=src, k4=k4: q.dma_start(out=xc[xi][:, k4 * 4:(k4 + 1) * 4, :],
                                                                            in_=src[:, k4 * 4:(k4 + 1) * 4, :]),
                            s_xe[xi], writes=[rxc[xi]])
        else:
            p.dma("sp", lambda q, xi=xi, src=src: q.dma_start(out=xc[xi][:], in_=src), s_x[xi], writes=[rxc[xi]])
        tsl = slice(tc * 512, (tc + 1) * 512)
        for (c0, dst, rdst, extra) in ((0, QAT, rQA, None), (128, KAT, rKA, None), (384, KBT, rKB, "km"), (256, QBT, rQB, "qf")):
            b = nb()
            for k in range(KC):
                p.op("pe", lambda t, k=k, b=b, c0=c0, xi=xi: t.matmul(P[b][:], lhsT=Wh[:, k, c0:c0 + 128], rhs=xc[xi][:, k, :],
                                                                      start=(k == 0), stop=(k == KC - 1)),
                     reads=[rWh, rxc[xi]], writes=[rP[b]])
            if extra is None:
                p.op("act", lambda a, b=b, dst=dst, tsl=tsl: a.activation(out=dst[:, tsl], in_=P[b][:], func=AF.Copy),
                     reads=[rP[b]], writes=[rdst[tc]])
            else:
                p.op("dve", lambda v, b=b, dst=dst, tsl=tsl: v.tensor_copy(out=dst[:, tsl], in_=P[b][:]),
                     reads=[rP[b]], writes=[rdst[tc]])
            if extra == "km" and "r" in flags:
                p.op("dve", lambda v, b=b, tc=tc: v.tensor_reduce(out=kmT[:, 2 * tc:2 * tc + 2],
                                                                  in_=P[b][:].rearrange("p (n k) -> p n k", n=2),
                                                                  axis=AX.X, op=ALU.add),
                     reads=[rP[b]], writes=[rkm])
            if extra == "qf" and "l" in flags:
                p.op("dve", lambda v, b=b, xi=xi, tsl=tsl: v.tensor_tensor(out=qlo[xi][:], in0=P[b][:], in1=QBT[:, tsl], op=ALU.subtract),
                     reads=[rP[b], rQB[tc]], writes=[rqf[xi]])
        for tb in range(4 if "v" in flags else 0):
            b = nb()
            blk = tc * 4 + tb
            for k in range(KC):
                p.op("pe", lambda t, k=k, b=b, tb=tb, xi=xi: t.matmul(P[b][:, 0:257], lhsT=xc[xi][:, k, tb * 128:(tb + 1) * 128],
                                                                      rhs=Wh[:, k, 512:769], start=(k == 0), stop=(k == KC - 1)),
                     reads=[rWh, rxc[xi]], writes=[rP[b]])
            if "A" in flags:
                p.op("dve", lambda v, b=b, blk=blk: v.tensor_copy(out=VA[:, blk, 0:128], in_=P[b][:, 0:128]),
                     reads=[rP[b]], writes=[rVA[tc]])
            if "B" in flags:
                p.op("dve", lambda v, b=b, blk=blk: v.tensor_copy(out=VB[:, blk, 0:128], in_=P[b][:, 129:257]),
                     reads=[rP[b]], writes=[rVB[tc]])
            if "F" in flags:
                p.op("dve", lambda v, b=b, blk=blk: v.tensor_copy(out=Fl[:, blk:blk + 1], in_=P[b][:, 128:129]),
                     reads=[rP[b]], writes=[rF])
        for s in range(4):
            qb = tc * 4 + s
            own = qb // 2
            if own == 0 or "g" not in flags:
                continue
            b = nb()
            if s in (0, 2) and "h" in flags:
                p.op("dve", lambda v: v.tensor_copy(out=kmh[:], in_=kmT[:]), reads=[rkm], writes=[rkmh])
                p.op("dve", lambda v: v.tensor_tensor(out=kml[:], in0=kmT[:], in1=kmh[:], op=ALU.subtract),
                     reads=[rkm, rkmh], writes=[rkmh])
            qsl = slice(tc * 512 + s * 128, tc * 512 + (s + 1) * 128)
            if "m" not in flags:
                continue
            p.op("pe", lambda t, b=b, qsl=qsl: t.matmul(P[b][:, 0:32], lhsT=QBT[:, qsl], rhs=kmh[:, :], start=True, stop=False),
                 reads=[rQB[tc], rkmh], writes=[rP[b]])
            p.op("pe", lambda t, b=b, qsl=qsl: t.matmul(P[b][:, 0:32], lhsT=QBT[:, qsl], rhs=kml[:, :], start=False, stop=False),
                 reads=[rQB[tc], rkmh], writes=[rP[b]])
            p.op("pe", lambda t, b=b, s=s, xi=xi: t.matmul(P[b][:, 0:32], lhsT=qlo[xi][:, s * 128:(s + 1) * 128], rhs=kmh[:, :],
                                                            start=False, stop=True),
                 reads=[rqf[xi], rkmh], writes=[rP[b]])
            if "x" not in flags:
                continue
            p.op("dve", lambda v, b=b, own=own: v.tensor_copy(out=gm[:, 0:own], in_=P[b][:, 0:own]), reads=[rP[b]], writes=[rgm])
            p.op("dve", lambda v: v.max(out=m8[:], in_=gm[:]), reads=[rgm], writes=[rm8])
            p.op("dve", lambda v, qb=qb, own=own: v.tensor_scalar(out=sel[:, qb, 0:own], in0=gm[:, 0:own], scalar1=m8[:, 2:3],
                                                                  scalar2=None, op0=ALU.is_ge),
                 reads=[rgm, rm8], writes=[rsel[qb]])

    if stop == "proj":
        ev = p.dma("sp", lambda q: q.dma_start(out=yT_d[0:128, :], in_=QAT[:, :]), None, reads=rQA + rKA + rQB + rKB + rVA + rVB + rsel + [rF])
        p.emit({"sp": [ev]})
        p.close()
        return nc
    nbf = p.sbuf("nbf", [128, 1], F32)
    rnbf = Res("nbf")
    p.op("dve", lambda v: v.tensor_scalar(out=nbf[:], in0=gb[:, 257:258], scalar1=-1.0, scalar2=None, op0=ALU.mult),
         reads=[rgb], writes=[rnbf])
    ef = p.sbuf("ef", [128, NKB], F32)
    lf = p.sbuf("lf", [128, NKB], F32)
    ref_, rlf = Res("ef"), Res("lf")
    p.op("act", lambda a: a.activation(out=ef[:], in_=Fl[:], func=AF.Exp, scale=-1.0, bias=nbf[:, 0:1]),
         reads=[rF, rnbf], writes=[ref_])
    p.op("act", lambda a: a.activation(out=lf[:], in_=ef[:], func=AF.Ln, bias=1.0, scale=1.0), reads=[ref_], writes=[rlf])
    p.op("dve", lambda v: v.tensor_scalar(out=lf[:], in0=lf[:], scalar1=-1.0, scalar2=None, op0=ALU.mult),
         reads=[rlf], writes=[rlf])
    onesf = p.sbuf("onesf", [128, 128], F32)
    ronesf = Res("onesf")
    p.op("dve", lambda v: v.memset(onesf[:], 1.0), writes=[ronesf])
    b0 = nb()
    p.op("pe", lambda t: t.matmul(P[b0][0:64, 0:1], lhsT=lf[:, :], rhs=onesf[:, 0:1], start=True, stop=True),
         reads=[rlf, ronesf], writes=[rP[b0]])
    totbc = p.sbuf("totbc", [64, 128], F32)
    rtot = Res("totbc")
    p.op("dve", lambda v: v.tensor_scalar(out=totbc[:], in0=onesf[0:64, :], scalar1=P[b0][0:64, 0:1], scalar2=None, op0=ALU.mult),
         reads=[rP[b0], ronesf], writes=[rtot])
    b1 = nb()
    p.op("pe", lambda t: t.matmul(P[b1][:, 0:NKB], lhsT=U, rhs=lf[:, :], start=True, stop=False),
         reads=[rcst, rlf], writes=[rP[b1]])
    p.op("pe", lambda t: t.matmul(P[b1][:, 0:NKB], lhsT=totbc[:, :], rhs=LST, start=False, stop=True),
         reads=[rcst, rtot], writes=[rP[b1]])
    csb = p.sbuf("csb", [128, NKB], F32)
    negc = p.sbuf("negc", [128, NKB], F32)
    rcsb, rnegc = Res("csb"), Res("negc")
    p.op("dve", lambda v: v.tensor_copy(out=csb[:], in_=P[b1][:, 0:NKB]), reads=[rP[b1]], writes=[rcsb])
    p.op("dve", lambda v: v.tensor_scalar(out=negc[:], in0=P[b1][:, 0:NKB], scalar1=-1.0, scalar2=None, op0=ALU.mult),
         reads=[rP[b1]], writes=[rnegc])
    cendc = p.sbuf("cendc", [128, NQG], F32)
    rcendc = Res("cendc")
    p.op("dve", lambda v: v.tensor_copy(out=cendc[:], in_=csb[:].rearrange("p (g f) -> p g f", f=4)[:, :, 3]),
         reads=[rcsb], writes=[rcendc])
    b2 = nb()
    p.op("pe", lambda t: t.matmul(P[b2][:, 0:NQG], lhsT=E127, rhs=cendc[:, :], start=True, stop=True),
         reads=[rcst, rcendc], writes=[rP[b2]])
    cend = p.sbuf("cend", [128, NQG], F32)
    rcend = Res("cend")
    p.op("dve", lambda v: v.tensor_copy(out=cend[:], in_=P[b2][:, 0:NQG]), reads=[rP[b2]], writes=[rcend])
    ball = p.sbuf("ball", [128, NQG, NKB], F32)
    rball = Res("ball")
    for g in range(NQG):
        p.op("dve", lambda v, g=g: v.tensor_scalar(out=ball[:, g, :], in0=negc[:, :], scalar1=cend[:, g:g + 1], scalar2=None,
                                                   op0=ALU.add), reads=[rnegc, rcend], writes=[rball])

    cx = AttnCtx(p, P, rP, identb, rident, trib, rtri, yT_d, s_out)
    SCALE = HD ** -0.5

    def qk_fox(J, g, c0):
        return [(KAT[:, J * 128:(J + 1) * 128], QAT[:, g * 512 + c0:(g + 1) * 512], [rKA[J // 4], rQA[g]])]

    def bias_fox(g, J):
        return (ball[:, g, J:J + 1], [rball])

    dense_causal_sweep(cx, qk_fox, bias_fox, VA, rVA, SCALE, 0)

    if stop == "fox":
        p.emit({"sp": cx.finals})
        p.close()
        return nc
    acc = p.sbuf("acc", [128, 4, 129], F32)
    racc = [Res(f"acc{s}") for s in range(4)]
    msteps = [(g, n, J) for g in range(NQG) for n in range(2 * g + 2) for J in (2 * n, 2 * n + 1)]
    started = {}

    def moba_a(g, n, J):
        r = J - 4 * g
        c0 = max(r, 0) * 128
        sb = cx.ksb % 3
        cx.ksb += 1
        adds = []
        for s in range(max(r, 0), 4):
            if J == 4 * g + s:
                adds.append((s, d0b, rd0))
            elif J == 4 * g + s - 1:
                adds.append((s, d1b, rd1))
        p.op("pe", lambda t, J=J, g=g, c0=c0, sb=sb, na=len(adds): t.matmul(
            P[sb][:, c0:512], lhsT=KBT[:, J * 128:(J + 1) * 128], rhs=QBT[:, g * 512 + c0:(g + 1) * 512],
            start=True, stop=(na == 0)), reads=[rKB[J // 4], rQB[g]], writes=[rP[sb]])
        for ai, (s, dt_, rdt) in enumerate(adds):
            p.op("pe", lambda t, s=s, dt_=dt_, sb=sb, ai=ai, na=len(adds): t.matmul(
                P[sb][:, s * 128:(s + 1) * 128], lhsT=identb[:], rhs=dt_[:], start=False, stop=(ai == na - 1)),
                reads=[rident, rdt], writes=[rP[sb]])
        pi = cx.kpt % 4
        cx.kpt += 1
        p.op("act", lambda a, pi=pi, sb=sb, c0=c0: a.activation(out=cx.pt[pi][:, c0:512], in_=P[sb][:, c0:512],
                                                                  func=AF.Exp, scale=SCALE),
             reads=[rP[sb]], writes=[cx.rpt[pi]])
        return pi

    def moba_b(g, n, J, pi):
        r = J - 4 * g
        yb = [3 + 2 * (n % 2), 4 + 2 * (n % 2)]
        for s in range(max(r, 0), 4):
            bi = YBNK[s]
            bank = yb[bi]
            st = not started.get((g, n, bi), False)
            started[(g, n, bi)] = True
            p.op("pe", lambda t, pi=pi, s=s, bank=bank, J=J, st=st: t.matmul(
                P[bank][:, YOFF[s]:YOFF[s] + 129], lhsT=cx.pt[pi][:, s * 128:(s + 1) * 128], rhs=VB[:, J, 0:129],
                start=st, stop=True, skip_group_check=True),
                reads=[cx.rpt[pi], rVB[J // 4]], writes=[rP[bank]])
        if J != 2 * n + 1:
            return
        smin_n = max(2 * n - 4 * g, 0)
        for s in range(smin_n, 4):
            qb = 4 * g + s
            own = qb // 2
            bank = yb[YBNK[s]]
            src = P[bank][:, YOFF[s]:YOFF[s] + 129]
            if n == own:
                if n == 0:
                    p.op("dve", lambda v, s=s, src=src: v.tensor_copy(out=acc[:, s, :], in_=src),
                         reads=[rP[bank]], writes=[racc[s]])
                else:
                    p.op("dve", lambda v, s=s, src=src: v.tensor_tensor(out=acc[:, s, :], in0=acc[:, s, :], in1=src, op=ALU.add),
                         reads=[rP[bank], racc[s]], writes=[racc[s]])
                cx.finish_block(acc[:, s, 0:128], acc[:, s, 128:129], [racc[s]], s, 128, g, s == 3)
            elif n == 0:
                p.op("dve", lambda v, s=s, src=src, qb=qb: v.tensor_scalar(out=acc[:, s, :], in0=src, scalar1=sel[:, qb, 0:1],
                                                                          scalar2=None, op0=ALU.mult),
                     reads=[rP[bank], rsel[qb]], writes=[racc[s]])
            else:
                p.op("dve", lambda v, s=s, src=src, qb=qb, n=n: v.scalar_tensor_tensor(
                    out=acc[:, s, :], in0=src, scalar=sel[:, qb, n:n + 1], in1=acc[:, s, :], op0=ALU.mult, op1=ALU.add),
                    reads=[rP[bank], rsel[qb], racc[s]], writes=[racc[s]])

    mp = [moba_a(*msteps[0]), moba_a(*msteps[1])]
    for i, st_ in enumerate(msteps):
        if i + 2 < len(msteps):
            mp.append(moba_a(*msteps[i + 2]))
        moba_b(*st_, mp[i])
        cx.tick()
    cx.tick(flush=True)

    p.emit({"sp": cx.finals})
    p.close()
    return nc


def _t5_bucket(rel):
    n = np.maximum(rel, 0)
    nf = np.maximum(n, 1).astype(np.float32)
    large = 16 + (np.log(nf / np.float32(16)) / np.float32(math.log(128 / 16)) * np.float32(16)).astype(np.int32)
    large = np.minimum(large, 31)
    return np.where(n < 16, n, large)


def consts_AE():
    j = np.arange(128)
    cst = np.zeros((128, 576), np.float32)
    cst[:, 0:128] = (j[:, None] <= j[None, :])
    b = np.arange(64)
    cst[0:64, 128:192] = (b[:, None] < b[None, :])
    cst[127, 192:320] = 1.0
    cst[:, 320:448] = np.eye(128)
    cst[:, 448:576] = np.where(j[:, None] <= j[None, :], 0.0, NEG)
    return cst


def gb_AE(rel_bias, b_f, head):
    j = np.arange(128)
    rel0 = j[None, :] - j[:, None]
    gbm = np.zeros((128, 258), np.float32)
    col = rel_bias[:, head]
    gbm[:, 0:128] = col[_t5_bucket(rel0)]
    gbm[:, 128:256] = col[_t5_bucket(128 + rel0)]
    gbm[:, 256] = col[31]
    gbm[:, 257] = b_f[head]
    return gbm


def build_AO():
    nc = bass.Bass("TRN2", target_bir_lowering=False)
    dt = nc.dram_tensor
    lat_d = dt("lat", [NCORES, MLA_IN, TOK], BF16, kind="ExternalInput").ap()
    wuq_d = dt("wuq", [512, 384], F32, kind="ExternalInput").ap()
    wukv_d = dt("wukv", [512, 512], F32, kind="ExternalInput").ap()
    cs_d = dt("cs", [64, 2 * S], F32, kind="ExternalInput").ap()
    cst_d = dt("cst", [128, 576], F32, kind="ExternalInput").ap()
    yT_d = dt("yT", [256, S], BF16, kind="ExternalOutput").ap()

    p = Prog(nc)
    cst = p.sbuf("cst", [128, 576], F32)
    rcst = Res("cst")
    p.dma("sp", lambda q: q.dma_start(out=cst[:], in_=cst_d[:, :]), None, writes=[rcst])
    identb = p.sbuf("identb", [128, 128], BF16)
    trib = p.sbuf("trib", [128, 128], BF16)
    rident, rtri = Res("identb"), Res("trib")
    p.op("dve", lambda v: v.tensor_copy(out=identb[:], in_=cst[:, 320:448]), reads=[rcst], writes=[rident])
    p.op("dve", lambda v: v.tensor_copy(out=trib[:], in_=cst[:, 448:576]), reads=[rcst], writes=[rtri])
    Wq = p.sbuf("Wq", [128, 4, 384], BF16)
    Wkv = p.sbuf("Wkv", [128, 4, 512], BF16)
    Wrot = p.sbuf("Wrot", [128, 2, 4, 64], BF16)
    rWq, rWkv, rWrot = Res("Wq"), Res("Wkv"), Res("Wrot")
    s_w = p.new_dma_sem("s_w", exclusive=True)
    p.dma("pool", lambda q: q.dma_start(out=Wq[:], in_=wuq_d.rearrange("(kc p) n -> p kc n", p=128)), s_w, writes=[rWq])
    s_w2 = p.new_dma_sem("s_w2", exclusive=True)
    p.dma("pool", lambda q: q.dma_start(out=Wkv[:], in_=wukv_d.rearrange("(kc p) n -> p kc n", p=128)), s_w2, writes=[rWkv])
    for h in range(2):
        r0 = h * 192 + 128
        p.op("dve", lambda v, h=h, r0=r0: v.tensor_scalar(out=Wrot[:, h, :, 0:32], in0=Wq[:, :, r0 + 32:r0 + 64], scalar1=-1.0,
                                                          scalar2=None, op0=ALU.mult), reads=[rWq], writes=[rWrot])
        p.op("dve", lambda v, h=h, r0=r0: v.tensor_copy(out=Wrot[:, h, :, 32:64], in_=Wq[:, :, r0:r0 + 32]), reads=[rWq], writes=[rWrot])
    QN = p.sbuf("QN", [128, S], BF16)
    KN = p.sbuf("KN", [128, S], BF16)
    QR = p.sbuf("QR", [64, S], BF16)
    KR = p.sbuf("KR", [64, S], BF16)
    V = p.sbuf("V", [128, NKB, 130], BF16)
    rQN = [Res(f"QN{i}") for i in range(NQG)]
    rKN = [Res(f"KN{i}") for i in range(NQG)]
    rQR = [Res(f"QR{i}") for i in range(NQG)]
    rKR = [Res(f"KR{i}") for i in range(NQG)]
    rV = [Res(f"V{i}") for i in range(NQG)]
    latc = [p.sbuf(f"latc{i}", [128, 8, 512], BF16) for i in range(2)]
    rlatc = [Res(f"latc{i}") for i in range(2)]
    csc = [p.sbuf(f"csc{i}", [64, 2, 512], F32) for i in range(2)]
    rcsc = [Res(f"csc{i}") for i in range(2)]
    t0 = p.sbuf("t0", [64, 512], F32)
    t1 = p.sbuf("t1", [64, 512], F32)
    rt0, rt1 = Res("t0"), Res("t1")
    P = [p.psum(f"P{i}", [128, 512], F32) for i in range(7)]
    rP = [Res(f"P{i}") for i in range(7)]
    p.op("pool", lambda g_: g_.memset(V[:, :, 128:130], 1.0), writes=rV)
    cx = AttnCtx(p, P, rP, identb, rident, trib, rtri, yT_d, None)
    SCALE = 192.0 ** -0.5
    rr = [0]

    def nb():
        b = rr[0] % 3
        rr[0] += 1
        return b

    kq = 0
    for h in range(2):
        for tc in range(NQG):
            xi = kq % 2
            kq += 1
            r_, c_ = tc // 2, (tc % 2) * 512
            tsl = slice(tc * 512, (tc + 1) * 512)
            p.dma("sp", lambda q, xi=xi, r_=r_, c_=c_: q.dma_start(
                out=latc[xi][:], in_=lat_d[r_, 0:1024, c_:c_ + 512].rearrange("(c p) t -> p c t", p=128)), None,
                writes=[rlatc[xi]])
            p.dma("sp", lambda q, xi=xi, tsl=tsl: q.dma_start(out=csc[xi][:, 0, :], in_=cs_d[:, tsl]), None, writes=[rcsc[xi]])
            p.dma("sp", lambda q, xi=xi, tc=tc: q.dma_start(out=csc[xi][:, 1, :], in_=cs_d[:, S + tc * 512:S + (tc + 1) * 512]), None,
                  writes=[rcsc[xi]])
            if h == 0:
                p.dma("sp", lambda q, r_=r_, c_=c_, tsl=tsl: q.dma_start(out=KR[:, tsl], in_=lat_d[r_, 1024:1088, c_:c_ + 512]), None,
                      writes=[rKR[tc]])
            for (W, rW, c0, kof, dst, rdst) in ((Wq, rWq, h * 192, 0, QN, rQN), (Wkv, rWkv, h * 256, 4, KN, rKN)):
                b = nb()
                for k in range(4):
                    p.op("pe", lambda t, k=k, b=b, W=W, c0=c0, kof=kof, xi=xi: t.matmul(
                        P[b][:], lhsT=W[:, k, c0:c0 + 128], rhs=latc[xi][:, kof + k, :], start=(k == 0), stop=(k == 3)),
                        reads=[rW, rlatc[xi]], writes=[rP[b]])
                p.op("act", lambda a, b=b, dst=dst, tsl=tsl: a.activation(out=dst[:, tsl], in_=P[b][:], func=AF.Copy),
                     reads=[rP[b]], writes=[rdst[tc]])
            b0, b1 = nb(), nb()
            for k in range(4):
                p.op("pe", lambda t, k=k, b0=b0, h=h, xi=xi: t.matmul(P[b0][0:64, :], lhsT=Wq[:, k, h * 192 + 128:h * 192 + 192],
                                                                       rhs=latc[xi][:, k, :], start=(k == 0), stop=(k == 3)),
                     reads=[rWq, rlatc[xi]], writes=[rP[b0]])
            for k in range(4):
                p.op("pe", lambda t, k=k, b1=b1, h=h, xi=xi: t.matmul(P[b1][0:64, :], lhsT=Wrot[:, h, k, :],
                                                                       rhs=latc[xi][:, k, :], start=(k == 0), stop=(k == 3)),
                     reads=[rWrot, rlatc[xi]], writes=[rP[b1]])
            p.op("dve", lambda v, b0=b0, xi=xi: v.tensor_tensor(out=t0[:], in0=P[b0][0:64, :], in1=csc[xi][:, 0, :], op=ALU.mult),
                 reads=[rP[b0], rcsc[xi]], writes=[rt0])
            p.op("dve", lambda v, b1=b1, xi=xi: v.tensor_tensor(out=t1[:], in0=P[b1][0:64, :], in1=csc[xi][:, 1, :], op=ALU.mult),
                 reads=[rP[b1], rcsc[xi]], writes=[rt1])
            p.op("dve", lambda v, tsl=tsl: v.tensor_tensor(out=QR[:, tsl], in0=t0[:], in1=t1[:], op=ALU.add),
                 reads=[rt0, rt1], writes=[rQR[tc]])
            for tb in range(4):
                b = nb()
                blk = tc * 4 + tb
                for k in range(4):
                    p.op("pe", lambda t, k=k, b=b, tb=tb, h=h, xi=xi: t.matmul(
                        P[b][:, 0:128], lhsT=latc[xi][:, 4 + k, tb * 128:(tb + 1) * 128], rhs=Wkv[:, k, h * 256 + 128:h * 256 + 256],
                        start=(k == 0), stop=(k == 3)), reads=[rWkv, rlatc[xi]], writes=[rP[b]])
                p.op("dve", lambda v, b=b, blk=blk: v.tensor_copy(out=V[:, blk, 0:128], in_=P[b][:, 0:128]),
                     reads=[rP[b]], writes=[rV[tc]])

        def qk_mla(J, g, c0):
            return [(KN[:, J * 128:(J + 1) * 128], QN[:, g * 512 + c0:(g + 1) * 512], [rKN[J // 4], rQN[g]]),
                    (KR[:, J * 128:(J + 1) * 128], QR[:, g * 512 + c0:(g + 1) * 512], [rKR[J // 4], rQR[g]])]

        dense_causal_sweep(cx, qk_mla, None, V, rV, SCALE, h * 128)

    p.emit({"sp": cx.finals})
    p.close()
    return nc


_NC_CACHE = {}


def _prog(name, fn):
    if name not in _NC_CACHE:
        _NC_CACHE[name] = fn()
    return _NC_CACHE[name]


def _lay128(v):
    return np.ascontiguousarray(v.reshape(-1, 128).T)


def _run(nc, in_maps):
    res = run_bass_kernel_spmd(nc, in_maps, core_ids=list(range(NCORES)))
    return res.results


def kernel(x, ab_w_in, ab_forget_bias, ab_w_out, rel_bias, mla_w_in, mla_q_norm, mla_kv_norm, mla_w_uq,
           mla_w_ukv, mla_w_out, ffn_w_gate, ffn_w_up, ffn_w_down, ln_g, ln_b):
    f32 = np.float32
    x = np.asarray(x, f32)
    xT = np.ascontiguousarray(x[0].T)
    xres = [np.ascontiguousarray(xT[:, c * TOK:(c + 1) * TOK]) for c in range(NCORES)]
    cstAE = consts_AE()
    inv = (np.float32(10000.0) ** (-np.arange(0, 64, 2, dtype=f32) / np.float32(64))).astype(f32)
    ang = np.arange(S, dtype=f32)[:, None] * inv[None, :]
    cos, sin = np.cos(ang).astype(f32), np.sin(ang).astype(f32)
    csT = np.ascontiguousarray(np.concatenate([np.concatenate([cos, cos], 1).T, np.concatenate([sin, sin], 1).T], axis=1))

    gathered = None
    for layer in range(DEPTH):
        j = layer // 2
        if layer % 2 == 0:
            if layer == 0:
                nc = _prog("AE32", lambda: build_AE(x_f32=True))
                gathered = np.ascontiguousarray(xT.reshape(D, NCORES, TOK).transpose(1, 0, 2))
            else:
                nc = _prog("AE16", lambda: build_AE(x_f32=False))
            w = np.asarray(ab_w_in[j], f32)
            maps = []
            for c in range(NCORES):
                cols = [w[:, c * 128:(c + 1) * 128], w[:, 1024 + c * 128:1024 + (c + 1) * 128],
                        w[:, 3080 + c * 128:3080 + (c + 1) * 128], w[:, 4104 + c * 128:4104 + (c + 1) * 128],
                        w[:, 2048 + c * 128:2048 + (c + 1) * 128], w[:, 3072 + c:3072 + c + 1],
                        w[:, 5128 + c * 128:5128 + (c + 1) * 128], np.zeros((D, 1), f32)]
                maps.append({"xT": gathered, "wh": np.ascontiguousarray(np.concatenate(cols, axis=1)), "cst": cstAE,
                             "gb": gb_AE(np.asarray(rel_bias, f32), np.asarray(ab_forget_bias[j], f32), c)})
            outs = _run(nc, maps)
            yT_full = np.concatenate([o["yT"][0:128] for o in outs] + [o["yT"][128:256] for o in outs], axis=0)
            w_out = np.asarray(ab_w_out[j], f32)
        else:
            nc = _prog("AO", build_AO)
            wq = np.asarray(mla_w_uq[j], f32)
            wkv = np.asarray(mla_w_ukv[j], f32)
            maps = []
            for c in range(NCORES):
                maps.append({"lat": gathered, "wuq": np.ascontiguousarray(wq[:, c * 384:(c + 1) * 384]),
                             "wukv": np.ascontiguousarray(wkv[:, c * 512:(c + 1) * 512]), "cs": csT, "cst": cstAE})
            outs = _run(nc, maps)
            yT_full = np.concatenate([o["yT"] for o in outs], axis=0)
            w_out = np.asarray(mla_w_out[j], f32)
        mode = "final" if layer == DEPTH - 1 else ("mla" if (layer + 1) % 2 == 1 else "even")
        nc = _prog("T" + mode, lambda: build_T(mode))
        lnp = np.ascontiguousarray(np.concatenate([_lay128(np.asarray(ln_g[layer, 0], f32)), _lay128(np.asarray(ln_b[layer, 0], f32)),
                                                   _lay128(np.asarray(ln_g[layer, 1], f32)), _lay128(np.asarray(ln_b[layer, 1], f32))], axis=1))
        maps = []
        for c in range(NCORES):
            m = {"yT": np.ascontiguousarray(yT_full[:, c * TOK:(c + 1) * TOK]), "xres": xres[c], "wout": w_out, "lnp": lnp,
                 "wg": np.asarray(ffn_w_gate[layer], f32), "wu": np.asarray(ffn_w_up[layer], f32),
                 "wd": np.asarray(ffn_w_down[layer], f32)}
            if mode == "mla":
                jn = (layer + 1) // 2
                m["win"] = np.asarray(mla_w_in[jn], f32)
                m["ng"] = np.ascontiguousarray(np.concatenate([_lay128(np.asarray(mla_q_norm[jn], f32)),
                                                               _lay128(np.asarray(mla_kv_norm[jn], f32))], axis=1))
                m["cs"] = np.ascontiguousarray(np.concatenate([csT[:, c * TOK:(c + 1) * TOK], csT[:, S + c * TOK:S + (c + 1) * TOK]], axis=1))
            maps.append(m)
        outs = _run(nc, maps)
        xres = [o["xout"] for o in outs]
        if mode == "mla":
            gathered = np.ascontiguousarray(np.stack([o["lat"] for o in outs], axis=0))
        elif mode == "even":
            gathered = np.ascontiguousarray(np.stack([o["xTb"] for o in outs], axis=0))
    out = np.concatenate(xres, axis=1)
    return np.ascontiguousarray(out.T)[None].astype(np.float32)
```

```python
import math
import numpy as np
import ml_dtypes
import concourse.bass as bass
import concourse.mybir as mybir
from concourse.bass_utils import run_bass_kernel_spmd

F32 = mybir.dt.float32
BF16 = mybir.dt.bfloat16
AF = mybir.ActivationFunctionType
ALU = mybir.AluOpType
AX = mybir.AxisListType

NCORES = 8
D = 2048
S = 8192
DEPTH = 4
TOK = S // NCORES
DFF = 5632
NFC = DFF // 128
KC = D // 128
ALPHA = (2 * DEPTH) ** 0.25
HD = 128
MLA_IN = 1088
NEG = -30000.0

EPOCH = 20000
SAME_ENGINE_SYNC = True
COMPUTE = ("pe", "act", "dve", "pool")


class Res:
    __slots__ = ("w", "rs", "name")

    def __init__(self, name=""):
        self.w = None
        self.rs = {}
        self.name = name


class Prog:
    def __init__(self, nc, same_engine_sync=None):
        if same_engine_sync is None:
            same_engine_sync = SAME_ENGINE_SYNC
        self.nc = nc
        self.ops = {e: [] for e in ("pe", "act", "dve", "pool", "sp")}
        self.same_engine_sync = same_engine_sync
        self.dma_sems = []
        self._ctx = []

    def enter(self, cm):
        v = cm.__enter__()
        self._ctx.append(cm)
        return v

    def sbuf(self, name, shape, dtype):
        return self.enter(self.nc.sbuf_tensor("sb_" + name, list(shape), dtype))

    def psum(self, name, shape, dtype=F32):
        return self.enter(self.nc.psum_tensor("ps_" + name, list(shape), dtype))

    def new_dma_sem(self, name, exclusive=False):
        if not exclusive:
            return None
        h = self.enter(self.nc.semaphore(name))
        self.dma_sems.append([h, 0])
        return len(self.dma_sems) - 1

    NRING = 24

    def _ring_sem(self, queue):
        if not hasattr(self, "_rings"):
            self._rings = {}
        if queue not in self._rings:
            n = self.NRING if queue == "sp" else 8
            idx = []
            for i in range(n):
                h = self.enter(self.nc.semaphore(f"ring_{queue}{i}"))
                self.dma_sems.append([h, 0])
                idx.append(len(self.dma_sems) - 1)
            self._rings[queue] = [idx, [None] * n, 0]
        rg = self._rings[queue]
        i = rg[2] % len(rg[0])
        rg[2] += 1
        return rg, i

    def _deps(self, eng, reads, writes):
        deps = set()
        for r in reads:
            if r.w is not None:
                deps.add(r.w)
        for w in writes:
            if w.w is not None:
                deps.add(w.w)
            deps.update(w.rs.values())
        out = []
        for d in deps:
            if d[0] == "E":
                if d[1] == eng and (eng == "pe" or not self.same_engine_sync):
                    continue
                self.ops[d[1]][d[2]][2] = True
            out.append(d)
        return out

    def op(self, eng, fn, reads=(), writes=()):
        deps = self._deps(eng, reads, writes)
        idx = len(self.ops[eng])
        ev = ("E", eng, idx)
        self.ops[eng].append([fn, deps, False, None])
        for r in reads:
            r.rs[eng] = ev
        for w in writes:
            w.w = ev
            w.rs = {}
        return ev

    def dma(self, queue, fn, sem, reads=(), writes=()):
        deps = self._deps(queue, reads, writes)
        if sem is None:
            rg, ri = self._ring_sem(queue)
            sem = rg[0][ri]
            if rg[1][ri] is not None:
                deps.append(rg[1][ri])
            rg[1][ri] = ("S", sem, self.dma_sems[sem][1] + 16)
        self.dma_sems[sem][1] += 16
        ev = ("S", sem, self.dma_sems[sem][1])
        self.ops[queue].append([fn, deps, False, sem])
        for r in reads:
            r.rs[("S", sem)] = ev
        for w in writes:
            w.w = ev
            w.rs = {}
        return ev

    def emit(self, final_events):
        nc = self.nc
        sigval = {}
        esems = {}
        for e in COMPUTE + ("sp",):
            c = 0
            vals = []
            for o in self.ops[e]:
                if o[2] and o[3] is None:
                    c += 1
                vals.append(c)
            sigval[e] = vals
            n_ep = max((c + EPOCH - 1) // EPOCH, 1)
            esems[e] = [self.enter(nc.semaphore(f"s_{e}_{k}")) for k in range(n_ep)]

        def resolve(d):
            if d[0] == "S":
                return ("S", d[1]), self.dma_sems[d[1]][0], d[2]
            v = sigval[d[1]][d[2]]
            ep = (v - 1) // EPOCH
            return ("E", d[1], ep), esems[d[1]][ep], v - ep * EPOCH

        engmap = {"pe": "tensor", "act": "scalar", "dve": "vector", "pool": "gpsimd", "sp": "sync"}
        prog = self

        def body_for(e):
            def body(engine):
                waited = {}
                for i, (fn, deps, sig, dsem) in enumerate(prog.ops[e]):
                    for d in deps:
                        key, sem, val = resolve(d)
                        if waited.get(key, 0) < val:
                            engine.wait_ge(sem, val)
                            waited[key] = val
                    inst = fn(engine)
                    if dsem is not None:
                        inst.then_inc(prog.dma_sems[dsem][0], 16)
                    elif sig:
                        v = sigval[e][i]
                        ep = (v - 1) // EPOCH
                        inst.then_inc(esems[e][ep], 1)
                for d in final_events.get(e, ()):
                    key, sem, val = resolve(d)
                    engine.wait_ge(sem, val)
            return body

        with nc.Block() as block:
            for e in COMPUTE + ("sp",):
                if not self.ops[e] and not final_events.get(e):
                    continue
                getattr(block, engmap[e])(body_for(e))

    def close(self):
        while self._ctx:
            self._ctx.pop().__exit__(None, None, None)


class Slots:
    def __init__(self, p, name, n, nbytes):
        self.p = p
        self.n = n
        self.t = [p.sbuf(f"{name}{i}", [128, nbytes // 2], BF16) for i in range(n)]
        self.r = [Res(f"{name}{i}") for i in range(n)]
        self.sem = [p.new_dma_sem(f"{name}s{i}", exclusive=True) for i in range(n)]
        self.k = 0

    def load(self, shape, src_ap, queue="pool"):
        i = self.k % self.n
        self.k += 1
        a, b = shape
        view = self.t[i][:, 0:a * b].rearrange("p (a b) -> p a b", a=a)
        self.p.dma(queue, lambda q, v=view, s=src_ap: q.dma_start(out=v, in_=s), self.sem[i],
                   writes=[self.r[i]])
        return view, self.r[i]


def build_T(mode_next, has_mixer=True):
    nc = bass.Bass("TRN2", target_bir_lowering=False)
    dt = nc.dram_tensor
    yT_d = dt("yT", [D, TOK], BF16, kind="ExternalInput").ap()
    xres_d = dt("xres", [D, TOK], F32, kind="ExternalInput").ap()
    wout_d = dt("wout", [D, D], F32, kind="ExternalInput").ap()
    lnp_d = dt("lnp", [128, 4 * KC], F32, kind="ExternalInput").ap()
    wg_d = dt("wg", [D, DFF], F32, kind="ExternalInput").ap()
    wu_d = dt("wu", [D, DFF], F32, kind="ExternalInput").ap()
    wd_d = dt("wd", [DFF, D], F32, kind="ExternalInput").ap()
    xout_d = dt("xout", [D, TOK], F32, kind="ExternalOutput").ap()
    if mode_next == "even":
        xTb_d = dt("xTb", [D, TOK], BF16, kind="ExternalOutput").ap()
    if mode_next == "mla":
        win_d = dt("win", [D, MLA_IN], F32, kind="ExternalInput").ap()
        ng_d = dt("ng", [128, 8], F32, kind="ExternalInput").ap()
        cs_d = dt("cs", [64, 2 * TOK], F32, kind="ExternalInput").ap()
        lat_d = dt("lat", [MLA_IN, TOK], BF16, kind="ExternalOutput").ap()

    p = Prog(nc)
    z = p.sbuf("z", [128, KC, TOK], F32)
    xb = p.sbuf("xb", [128, KC, TOK], BF16)
    hg = [p.sbuf("hg0", [128, 4, TOK], BF16)] * 2 if mode_next != "mla" else [p.sbuf(f"hg{i}", [128, 4, TOK], BF16) for i in range(2)]
    ws = Slots(p, "ws", 5 if mode_next != "mla" else 4, 16384)
    lnp = p.sbuf("lnp", [128, 4 * KC], F32)
    ones = p.sbuf("ones", [128, 128], F32)
    mean = p.sbuf("mean", [128, TOK], F32)
    rstd = p.sbuf("rstd", [128, TOK], F32)
    tmpf = [p.sbuf(f"tmpf{i}", [128, 512], F32) for i in range(4)]
    P = [p.psum(f"P{i}", [128, 512], F32) for i in range(8)]
    rP = [Res(f"P{i}") for i in range(8)]
    rz = [[Res(f"z{m}_{h}") for h in range(2)] for m in range(KC)]
    rxb = [[Res(f"xb{m}_{h}") for h in range(2)] for m in range(KC)]
    rhg = [[[Res(f"hg{i}_{f}_{h}") for h in range(2)] for f in range(4)] for i in range(2)]
    if mode_next != "mla":
        rhg[1] = rhg[0]
    rtmp = [Res(f"tmp{i}") for i in range(4)]
    rlnp, rones, rmean, rrstd = Res("lnp"), Res("ones"), [Res("mean0"), Res("mean1")], [Res("rstd0"), Res("rstd1")]
    s_in = p.new_dma_sem("s_in")
    s_in2 = p.new_dma_sem("s_in2")
    s_out = p.new_dma_sem("s_out")
    H = [slice(0, 512), slice(512, 1024)]

    p.dma("sp", lambda q: q.dma_start(out=lnp[:], in_=lnp_d[:, :]), s_in2, writes=[rlnp])
    p.op("dve", lambda v: v.memset(ones[:], 1.0 / D), writes=[rones])
    yv = yT_d.rearrange("(kc p) t -> p kc t", p=128)
    xv = xres_d.rearrange("(kc p) t -> p kc t", p=128)
    for m in range(KC):
        if has_mixer:
            p.dma("sp", lambda q, m=m: q.dma_start(out=xb[:, m, :], in_=yv[:, m, :]), s_in,
                  writes=[rxb[m][0], rxb[m][1]])
        p.dma("sp", lambda q, m=m: q.dma_start(out=z[:, m, :], in_=xv[:, m, :]), s_in2,
              writes=[rz[m][0], rz[m][1]])

    bank_rr = [0]

    def proj_accum(nk, lhs_fn, rhs_fn, rhs_res, w_res, bank):
        for k in range(nk):
            p.op("pe", lambda t, k=k, bank=bank: t.matmul(P[bank][:], lhsT=lhs_fn(k), rhs=rhs_fn(k),
                                                           start=(k == 0), stop=(k == nk - 1)),
                 reads=[w_res, rhs_res(k)], writes=[rP[bank]])

    if has_mixer:
        wv = wout_d.rearrange("(kc p) n -> p kc n", p=128)
        for m4 in range(4):
            slab, rs = ws.load((KC, 512), wv[:, :, m4 * 512:(m4 + 1) * 512])
            for mi in range(4):
                m = m4 * 4 + mi
                for h in range(2):
                    bank = bank_rr[0] % 4
                    bank_rr[0] += 1
                    proj_accum(KC, lambda k, slab=slab, mi=mi: slab[:, k, mi * 128:(mi + 1) * 128],
                               lambda k, h=h: xb[:, k, H[h]], lambda k, h=h: rxb[k][h], rs, bank)
                    p.op("dve", lambda v, m=m, h=h, bank=bank: v.scalar_tensor_tensor(
                        out=z[:, m, H[h]], in0=z[:, m, H[h]], scalar=ALPHA, in1=P[bank][:],
                        op0=ALU.mult, op1=ALU.add), reads=[rP[bank], rz[m][h]], writes=[rz[m][h]])

    def layer_norm(gi, bi):
        for h in range(2):
            for k in range(KC):
                p.op("pe", lambda t, k=k, h=h: t.matmul(P[4 + h][:], lhsT=ones[:], rhs=z[:, k, H[h]],
                                                         start=(k == 0), stop=(k == KC - 1)),
                     reads=[rones, rz[k][h]], writes=[rP[4 + h]])
            for k in range(KC):
                ti = k % 2
                p.op("act", lambda a, k=k, h=h, ti=ti: a.activation(out=tmpf[ti][:], in_=z[:, k, H[h]], func=AF.Square),
                     reads=[rz[k][h]], writes=[rtmp[ti]])
                p.op("pe", lambda t, k=k, h=h, ti=ti: t.matmul(P[6 + h][:], lhsT=ones[:], rhs=tmpf[ti][:],
                                                                start=(k == 0), stop=(k == KC - 1)),
                     reads=[rones, rtmp[ti]], writes=[rP[6 + h]])
            p.op("dve", lambda v, h=h: v.tensor_copy(out=mean[:, H[h]], in_=P[4 + h][:]),
                 reads=[rP[4 + h]], writes=[rmean[h]])
            p.op("dve", lambda v, h=h: v.tensor_tensor(out=tmpf[2][:], in0=mean[:, H[h]], in1=mean[:, H[h]], op=ALU.mult),
                 reads=[rmean[h]], writes=[rtmp[2]])
            p.op("dve", lambda v, h=h: v.tensor_tensor(out=tmpf[2][:], in0=P[6 + h][:], in1=tmpf[2][:], op=ALU.subtract),
                 reads=[rP[6 + h], rtmp[2]], writes=[rtmp[2]])
            p.op("dve", lambda v, h=h: v.tensor_scalar(out=tmpf[2][:], in0=tmpf[2][:], scalar1=1e-5, scalar2=None, op0=ALU.add),
                 reads=[rtmp[2]], writes=[rtmp[2]])
            p.op("act", lambda a, h=h: a.activation(out=tmpf[3][:], in_=tmpf[2][:], func=AF.Sqrt),
                 reads=[rtmp[2]], writes=[rtmp[3]])
            p.op("dve", lambda v, h=h: v.reciprocal(out=rstd[:, H[h]], in_=tmpf[3][:]),
                 reads=[rtmp[3]], writes=[rrstd[h]])
        for h in range(2):
            for k in range(KC):
                p.op("dve", lambda g, k=k, h=h: g.tensor_tensor(out=z[:, k, H[h]], in0=z[:, k, H[h]], in1=mean[:, H[h]], op=ALU.subtract),
                     reads=[rz[k][h], rmean[h]], writes=[rz[k][h]])
                p.op("dve", lambda v, k=k, h=h: v.tensor_tensor(out=z[:, k, H[h]], in0=z[:, k, H[h]], in1=rstd[:, H[h]], op=ALU.mult),
                     reads=[rz[k][h], rrstd[h]], writes=[rz[k][h]])
                p.op("act", lambda a, k=k, h=h: a.activation(out=z[:, k, H[h]], in_=z[:, k, H[h]], func=AF.Identity,
                                                              scale=lnp[:, gi * KC + k:gi * KC + k + 1],
                                                              bias=lnp[:, bi * KC + k:bi * KC + k + 1]),
                     reads=[rz[k][h], rlnp], writes=[rz[k][h]])
                p.op("act", lambda v, k=k, h=h: v.activation(out=xb[:, k, H[h]], in_=z[:, k, H[h]], func=AF.Copy),
                     reads=[rz[k][h]], writes=[rxb[k][h]])

    layer_norm(0, 1)

    gv = wg_d.rearrange("(kc p) n -> p kc n", p=128)
    uv = wu_d.rearrange("(kc p) n -> p kc n", p=128)
    dv = wd_d.rearrange("(fc p) n -> p fc n", p=128)
    NG = NFC // 4
    for g in range(NG):
        gslab, rg = ws.load((KC, 512), gv[:, :, g * 512:(g + 1) * 512])
        uslab, ru = ws.load((KC, 512), uv[:, :, g * 512:(g + 1) * 512])
        hb = hg[g % 2]
        rh = rhg[g % 2]
        for fi in range(4):
            for h in range(2):
                bg = bank_rr[0] % 4
                bu = (bank_rr[0] + 1) % 4
                bank_rr[0] += 2
                proj_accum(KC, lambda k, fi=fi, gslab=gslab: gslab[:, k, fi * 128:(fi + 1) * 128],
                           lambda k, h=h: xb[:, k, H[h]], lambda k, h=h: rxb[k][h], rg, bg)
                proj_accum(KC, lambda k, fi=fi, uslab=uslab: uslab[:, k, fi * 128:(fi + 1) * 128],
                           lambda k, h=h: xb[:, k, H[h]], lambda k, h=h: rxb[k][h], ru, bu)
                ti = (fi * 2 + h) % 2
                p.op("act", lambda a, bg=bg, ti=ti: a.activation(out=tmpf[ti][:], in_=P[bg][:], func=AF.Silu),
                     reads=[rP[bg]], writes=[rtmp[ti]])
                p.op("dve", lambda v, bu=bu, ti=ti, fi=fi, h=h, hb=hb: v.tensor_tensor(
                    out=hb[:, fi, H[h]], in0=P[bu][:], in1=tmpf[ti][:], op=ALU.mult),
                    reads=[rP[bu], rtmp[ti]], writes=[rh[fi][h]])
        for half in range(2):
            dslab, rd = ws.load((4, 1024), dv[:, g * 4:(g + 1) * 4, half * 1024:(half + 1) * 1024])
            for mi in range(8):
                m = half * 8 + mi
                for h in range(2):
                    bank = 4 + (bank_rr[0] % 4)
                    bank_rr[0] += 1
                    proj_accum(4, lambda k, mi=mi, dslab=dslab: dslab[:, k, mi * 128:(mi + 1) * 128],
                               lambda k, h=h, hb=hb: hb[:, k, H[h]], lambda k, h=h, rh=rh: rh[k][h], rd, bank)
                    if g == 0:
                        p.op("dve", lambda v, m=m, h=h, bank=bank: v.scalar_tensor_tensor(
                            out=z[:, m, H[h]], in0=z[:, m, H[h]], scalar=ALPHA, in1=P[bank][:],
                            op0=ALU.mult, op1=ALU.add), reads=[rP[bank], rz[m][h]], writes=[rz[m][h]])
                    else:
                        p.op("dve", lambda v, m=m, h=h, bank=bank: v.tensor_tensor(
                            out=z[:, m, H[h]], in0=z[:, m, H[h]], in1=P[bank][:], op=ALU.add),
                            reads=[rP[bank], rz[m][h]], writes=[rz[m][h]])

    layer_norm(2, 3)

    finals = []
    xo = xout_d.rearrange("(kc p) t -> p kc t", p=128)
    for m in range(KC):
        finals.append(p.dma("sp", lambda q, m=m: q.dma_start(out=xo[:, m, :], in_=z[:, m, :]), s_out,
                            reads=[rz[m][0], rz[m][1]]))
    if mode_next == "even":
        xt = xTb_d.rearrange("(kc p) t -> p kc t", p=128)
        for m in range(KC):
            finals.append(p.dma("sp", lambda q, m=m: q.dma_start(out=xt[:, m, :], in_=xb[:, m, :]), s_out,
                                reads=[rxb[m][0], rxb[m][1]]))
    if mode_next == "mla":
        finals += mla_latents(p, nc, ws, xb, rxb, P, rP, bank_rr, tmpf, rtmp, H,
                              win_d, ng_d, cs_d, lat_d, s_in2, s_out,
                              hg, rhg, mean, rstd, rmean, rrstd)
    p.emit({"sp": [finals[-1]] if False else finals})
    p.close()
    return nc


def mla_latents(p, nc, ws, xb, rxb, P, rP, bank_rr, tmpf, rtmp, H,
                win_d, ng_d, cs_d, lat_d, s_in, s_out, hg, rhg, mean, rstd, rmean, rrstd):
    finals = []
    ng = p.sbuf("ng", [128, 8], F32)
    rng_ = Res("ng")
    p.dma("sp", lambda q: q.dma_start(out=ng[:], in_=ng_d[:, :]), s_in, writes=[rng_])
    all_mean = [rmean[0], rmean[1]]
    all_rstd = [rrstd[0], rrstd[1]]
    p.dma("sp", lambda q: q.dma_start(out=mean[0:64, :], in_=cs_d[:, 0:TOK]), s_in, writes=all_mean)
    p.dma("sp", lambda q: q.dma_start(out=rstd[0:64, :], in_=cs_d[:, TOK:2 * TOK]), s_in, writes=all_rstd)
    lat = hg[0].bitcast(F32)
    latb = hg[1]
    rl0 = [r for f in rhg[0] for r in f]
    rl1 = [r for f in rhg[1] for r in f]
    ones512 = p.sbuf("ones512", [128, 128], F32)
    r512 = Res("ones512")
    p.op("dve", lambda v: v.memset(ones512[:], 1.0 / 512), writes=[r512])
    wv = win_d.rearrange("(kc p) n -> p kc n", p=128)
    for li in range(2):
        slab, rs = ws.load((KC, 512), wv[:, :, li * 512:(li + 1) * 512])
        for h in range(2):
            for c in range(4):
                bank = bank_rr[0] % 4
                bank_rr[0] += 1
                for k in range(KC):
                    p.op("pe", lambda t, k=k, c=c, h=h, bank=bank, slab=slab: t.matmul(
                        P[bank][:], lhsT=slab[:, k, c * 128:(c + 1) * 128], rhs=xb[:, k, H[h]],
                        start=(k == 0), stop=(k == KC - 1)), reads=[rs, rxb[k][h]], writes=[rP[bank]])
                p.op("act", lambda a, c=c, bank=bank: a.activation(out=lat[:, c, :], in_=P[bank][:], func=AF.Copy),
                     reads=[rP[bank]], writes=rl0)
            for c in range(4):
                ti = c % 2
                p.op("act", lambda a, c=c, ti=ti: a.activation(out=tmpf[ti][:], in_=lat[:, c, :], func=AF.Square),
                     reads=rl0, writes=[rtmp[ti]])
                p.op("pe", lambda t, c=c, h=h, ti=ti: t.matmul(P[6 + h][:], lhsT=ones512[:], rhs=tmpf[ti][:],
                                                                start=(c == 0), stop=(c == 3)),
                     reads=[r512, rtmp[ti]], writes=[rP[6 + h]])
            p.op("dve", lambda v, h=h: v.tensor_scalar(out=tmpf[2][:], in0=P[6 + h][:], scalar1=1e-6, scalar2=None, op0=ALU.add),
                 reads=[rP[6 + h]], writes=[rtmp[2]])
            p.op("act", lambda a: a.activation(out=tmpf[3][:], in_=tmpf[2][:], func=AF.Sqrt),
                 reads=[rtmp[2]], writes=[rtmp[3]])
            p.op("dve", lambda v: v.reciprocal(out=tmpf[2][:], in_=tmpf[3][:]), reads=[rtmp[3]], writes=[rtmp[2]])
            for c in range(4):
                p.op("dve", lambda v, c=c: v.tensor_tensor(out=lat[:, c, :], in0=lat[:, c, :], in1=tmpf[2][:], op=ALU.mult),
                     reads=rl0 + [rtmp[2]], writes=rl0)
                p.op("act", lambda a, c=c, li=li: a.activation(out=latb[:, c, 0:512], in_=lat[:, c, :], func=AF.Identity,
                                                               scale=ng[:, li * 4 + c:li * 4 + c + 1]),
                     reads=rl0 + [rng_], writes=rl1)
            lv = lat_d[li * 512:(li + 1) * 512, H[h]].rearrange("(c p) t -> p c t", p=128)
            finals.append(p.dma("sp", lambda q, lv=lv: q.dma_start(out=lv, in_=latb[:, :, 0:512]), s_out, reads=rl1))
    slab, rs = ws.load((KC, 64), wv[:, :, 1024:1088])
    wrot = p.sbuf("wrot", [128, KC, 64], BF16)
    rwrot = Res("wrot")
    p.op("dve", lambda v: v.tensor_scalar(out=wrot[:, :, 0:32], in0=slab[:, :, 32:64], scalar1=-1.0, scalar2=None, op0=ALU.mult),
         reads=[rs], writes=[rwrot])
    p.op("dve", lambda v: v.tensor_copy(out=wrot[:, :, 32:64], in_=slab[:, :, 0:32]), reads=[rs], writes=[rwrot])
    kr = p.sbuf("kr", [64, TOK], BF16)
    rkr = [Res("kr0"), Res("kr1")]
    for h in range(2):
        b0 = bank_rr[0] % 4
        b1 = (bank_rr[0] + 1) % 4
        bank_rr[0] += 2
        for k in range(KC):
            p.op("pe", lambda t, k=k, h=h, b0=b0: t.matmul(P[b0][0:64, :], lhsT=slab[:, k, :], rhs=xb[:, k, H[h]],
                                                            start=(k == 0), stop=(k == KC - 1)),
                 reads=[rs, rxb[k][h]], writes=[rP[b0]])
        for k in range(KC):
            p.op("pe", lambda t, k=k, h=h, b1=b1: t.matmul(P[b1][0:64, :], lhsT=wrot[:, k, :], rhs=xb[:, k, H[h]],
                                                            start=(k == 0), stop=(k == KC - 1)),
                 reads=[rwrot, rxb[k][h]], writes=[rP[b1]])
        p.op("dve", lambda v, h=h, b0=b0: v.tensor_tensor(out=tmpf[0][0:64, :], in0=P[b0][0:64, :], in1=mean[0:64, H[h]], op=ALU.mult),
             reads=[rP[b0]] + all_mean, writes=[rtmp[0]])
        p.op("dve", lambda v, h=h, b1=b1: v.tensor_tensor(out=tmpf[1][0:64, :], in0=P[b1][0:64, :],
                                                          in1=rstd[0:64, H[h]], op=ALU.mult),
             reads=[rP[b1]] + all_rstd, writes=[rtmp[1]])
        p.op("dve", lambda v, h=h: v.tensor_tensor(out=kr[:, H[h]], in0=tmpf[0][0:64, :], in1=tmpf[1][0:64, :], op=ALU.add),
             reads=[rtmp[0], rtmp[1]], writes=[rkr[h]])
    finals.append(p.dma("sp", lambda q: q.dma_start(out=lat_d[1024:1088, :], in_=kr[:]), s_out, reads=rkr))
    return finals


NQG = S // 512
NKB = S // 128
YOFF = [0, 129, 258, 0]
YBNK = [0, 0, 0, 1]


class AttnCtx:
    def __init__(self, p, P, rP, identb, rident, trib, rtri, yT_d, s_out):
        self.p, self.P, self.rP = p, P, rP
        self.identb, self.rident, self.trib, self.rtri = identb, rident, trib, rtri
        self.pt = [p.sbuf(f"pt{i}", [128, 512], BF16) for i in range(4)]
        self.rpt = [Res(f"pt{i}") for i in range(4)]
        self.yn = [p.sbuf(f"yn{i}", [128, 128], BF16) for i in range(4)]
        self.ryn = [Res(f"yn{i}") for i in range(4)]
        self.pending = []
        self.ytb = [p.sbuf(f"ytb{i}", [128, 512], BF16) for i in range(2)]
        self.rytb = [Res(f"ytb{i}") for i in range(2)]
        self.rinv = [p.sbuf(f"rinv{i}", [128, 1], F32) for i in range(4)]
        self.rrinv = [Res(f"rinv{i}") for i in range(4)]
        self.tp = p.enter(p.nc.psum_tensor("ps_tp", [128, 1024], BF16))
        self.rtp = Res("tp")
        self.yT_d, self.s_out = yT_d, s_out
        self.finals = []
        self.kpt = 0
        self.ksb = 0
        self.kyn = 0
        self.kg = 0

    def finish_block(self, src_ap, sum_ap, src_res, s, row0, g, last):
        p = self.p
        i = self.kyn % 4
        self.kyn += 1
        p.op("dve", lambda v, i=i: v.reciprocal(out=self.rinv[i][:], in_=sum_ap), reads=src_res, writes=[self.rrinv[i]])
        p.op("dve", lambda v, i=i: v.tensor_scalar(out=self.yn[i][:], in0=src_ap, scalar1=self.rinv[i][:, 0:1], scalar2=None,
                                                   op0=ALU.mult),
             reads=list(src_res) + [self.rrinv[i]], writes=[self.ryn[i]])
        self.pending.append([2, i, s, row0, g, last])

    def tick(self, flush=False):
        keep = []
        for item in self.pending:
            item[0] -= 1
            if item[0] > 0 and not flush:
                keep.append(item)
                continue
            _, i, s, row0, g, last = item
            p = self.p
            gb = self.kg % 2
            p.op("pe", lambda t, i=i: t.transpose(out=self.tp[:, 0:128], in_=self.yn[i][:], identity=self.identb[:]),
                 reads=[self.ryn[i], self.rident], writes=[self.rtp])
            p.op("dve", lambda v, s=s, gb=gb: v.tensor_copy(out=self.ytb[gb][:, s * 128:(s + 1) * 128], in_=self.tp[:, 0:128]),
                 reads=[self.rtp], writes=[self.rytb[gb]])
            if last:
                self.finals.append(p.dma("sp", lambda q, gb=gb, row0=row0, g=g: q.dma_start(
                    out=self.yT_d[row0:row0 + 128, g * 512:(g + 1) * 512], in_=self.ytb[gb][:]), self.s_out,
                    reads=[self.rytb[gb]]))
                self.kg += 1
        self.pending = keep


def dense_causal_sweep(cx, qk_fn, bias_fn, V, rV, scale, row0):
    p, P, rP = cx.p, cx.P, cx.rP
    steps = [(g, J) for g in range(NQG) for J in range(4 * g + 4)]

    def part_a(g, J):
        r = J - 4 * g
        c0 = max(r, 0) * 128
        sb = cx.ksb % 3
        cx.ksb += 1
        terms = qk_fn(J, g, c0)
        for ti, (lh, rh, rd) in enumerate(terms):
            p.op("pe", lambda t, lh=lh, rh=rh, ti=ti, sb=sb, c0=c0, nt=len(terms), r=r: t.matmul(
                P[sb][:, c0:512], lhsT=lh, rhs=rh, start=(ti == 0), stop=(ti == nt - 1 and r < 0)),
                reads=rd, writes=[rP[sb]])
        if r >= 0:
            p.op("pe", lambda t, sb=sb, c0=c0: t.matmul(P[sb][:, c0:c0 + 128], lhsT=cx.identb[:], rhs=cx.trib[:],
                                                        start=False, stop=True),
                 reads=[cx.rident, cx.rtri], writes=[rP[sb]])
        pi = cx.kpt % 4
        cx.kpt += 1
        b = bias_fn(g, J) if bias_fn else None
        if b is None:
            p.op("act", lambda a, pi=pi, sb=sb, c0=c0: a.activation(out=cx.pt[pi][:, c0:512], in_=P[sb][:, c0:512],
                                                                      func=AF.Exp, scale=scale),
                 reads=[rP[sb]], writes=[cx.rpt[pi]])
        else:
            bap, brd = b
            p.op("act", lambda a, pi=pi, sb=sb, c0=c0, bap=bap: a.activation(
                out=cx.pt[pi][:, c0:512], in_=P[sb][:, c0:512], func=AF.Exp, scale=scale, bias=bap),
                reads=[rP[sb]] + brd, writes=[cx.rpt[pi]])
        return pi

    def part_b(g, J, pi):
        r = J - 4 * g
        for s in range(max(r, 0), 4):
            bank = 3 + s
            p.op("pe", lambda t, pi=pi, s=s, bank=bank, J=J, g=g: t.matmul(
                P[bank][:, 0:129], lhsT=cx.pt[pi][:, s * 128:(s + 1) * 128], rhs=V[:, J, 0:129],
                start=(J == 0), stop=(J == 4 * g + s)),
                reads=[cx.rpt[pi], rV[J // 4]], writes=[rP[bank]])
            if J == 4 * g + s:
                cx.finish_block(P[bank][:, 0:128], P[bank][:, 128:129], [rP[bank]], s, row0, g, s == 3)

    pis = [part_a(*steps[0]), part_a(*steps[1])]
    for i, (g, J) in enumerate(steps):
        if i + 2 < len(steps):
            pis.append(part_a(*steps[i + 2]))
        part_b(g, J, pis[i])
        cx.tick()
    cx.tick(flush=True)


def build_AE(x_f32=False, stop=None, flags="qkvgABFrlhmx"):
    nc = bass.Bass("TRN2", target_bir_lowering=False)
    dt = nc.dram_tensor
    xT_d = dt("xT", [NCORES, D, TOK], F32 if x_f32 else BF16, kind="ExternalInput").ap()
    wh_d = dt("wh", [D, 770], F32, kind="ExternalInput").ap()
    cst_d = dt("cst", [128, 576], F32, kind="ExternalInput").ap()
    gb_d = dt("gb", [128, 258], F32, kind="ExternalInput").ap()
    yT_d = dt("yT", [256, S], BF16, kind="ExternalOutput").ap()

    p = Prog(nc)
    s_in = p.new_dma_sem("s_in")
    s_w = p.new_dma_sem("s_w")
    s_out = p.new_dma_sem("s_out")
    s_x = [p.new_dma_sem(f"s_x{i}") for i in range(2)]
    s_xe = [p.new_dma_sem(f"s_xe{i}", exclusive=True) for i in range(2)]
    cst = p.sbuf("cst", [128, 576], F32)
    rcst = Res("cst")
    p.dma("sp", lambda q: q.dma_start(out=cst[:], in_=cst_d[:, :]), s_in, writes=[rcst])
    gb = p.sbuf("gb", [128, 258], F32)
    rgb = Res("gb")
    p.dma("sp", lambda q: q.dma_start(out=gb[:], in_=gb_d[:, :]), s_in, writes=[rgb])
    U, LST, E127, IDF, TRIF = cst[:, 0:128], cst[0:64, 128:192], cst[:, 192:320], cst[:, 320:448], cst[:, 448:576]
    identb = p.sbuf("identb", [128, 128], BF16)
    trib = p.sbuf("trib", [128, 128], BF16)
    d0b = p.sbuf("d0b", [128, 128], BF16)
    d1b = p.sbuf("d1b", [128, 128], BF16)
    rident, rtri, rd0, rd1 = Res("identb"), Res("trib"), Res("d0b"), Res("d1b")
    tmpc = p.sbuf("tmpc", [128, 128], F32)
    rtmpc = Res("tmpc")
    p.op("dve", lambda v: v.tensor_copy(out=identb[:], in_=IDF), reads=[rcst], writes=[rident])
    p.op("dve", lambda v: v.tensor_copy(out=trib[:], in_=TRIF), reads=[rcst], writes=[rtri])
    SQ = math.sqrt(HD)
    p.op("dve", lambda v: v.tensor_scalar(out=tmpc[:], in0=gb[:, 0:128], scalar1=gb[:, 256:257], scalar2=SQ,
                                          op0=ALU.subtract, op1=ALU.mult), reads=[rgb], writes=[rtmpc])
    p.op("dve", lambda v: v.tensor_tensor(out=d0b[:], in0=tmpc[:], in1=TRIF, op=ALU.add), reads=[rtmpc, rcst], writes=[rd0])
    p.op("dve", lambda v: v.tensor_scalar(out=d1b[:], in0=gb[:, 128:256], scalar1=gb[:, 256:257], scalar2=SQ,
                                          op0=ALU.subtract, op1=ALU.mult), reads=[rgb], writes=[rd1])
    Wh = p.sbuf("Wh", [128, KC, 770], BF16)
    rWh = Res("Wh")
    whv = wh_d.rearrange("(kc p) n -> p kc n", p=128)
    for k4 in range(4):
        p.dma("pool", lambda q, k4=k4: q.dma_start(out=Wh[:, k4 * 4:(k4 + 1) * 4, :], in_=whv[:, k4 * 4:(k4 + 1) * 4, :]),
              s_w, writes=[rWh])
    QAT = p.sbuf("QAT", [128, S], BF16)
    KAT = p.sbuf("KAT", [128, S], BF16)
    QBT = p.sbuf("QBT", [128, S], BF16)
    KBT = p.sbuf("KBT", [128, S], BF16)
    VA = p.sbuf("VA", [128, NKB, 130], BF16)
    VB = p.sbuf("VB", [128, NKB, 130], BF16)
    rQA = [Res(f"QA{i}") for i in range(NQG)]
    rKA = [Res(f"KA{i}") for i in range(NQG)]
    rQB = [Res(f"QB{i}") for i in range(NQG)]
    rKB = [Res(f"KB{i}") for i in range(NQG)]
    rVA = [Res(f"VA{i}") for i in range(NQG)]
    rVB = [Res(f"VB{i}") for i in range(NQG)]
    Fl = p.sbuf("Fl", [128, NKB], F32)
    rF = Res("F")
    kmT = p.sbuf("kmT", [128, 32], F32)
    rkm = Res("kmT")
    sel = p.sbuf("sel", [128, NKB, 32], F32)
    rsel = [Res(f"sel{i}") for i in range(NKB)]
    gm = p.sbuf("gm", [128, 32], F32)
    rgm = Res("gm")
    m8 = p.sbuf("m8", [128, 8], F32)
    rm8 = Res("m8")
    qlo = [p.sbuf(f"qlo{i}", [128, 512], BF16) for i in range(2)]
    kmh = p.sbuf("kmh", [128, 32], BF16)
    kml = p.sbuf("kml", [128, 32], BF16)
    rkmh = Res("kmh")
    rqf = [Res(f"qf{i}") for i in range(2)]
    xc = [p.sbuf(f"xc{i}", [128, KC, 512], BF16) for i in range(2)]
    rxc = [Res(f"xc{i}") for i in range(2)]
    P = [p.psum(f"P{i}", [128, 512], F32) for i in range(7)]
    rP = [Res(f"P{i}") for i in range(7)]
    p.op("dve", lambda v: v.memset(kmT[:], 0.0), writes=[rkm])
    p.op("dve", lambda v: v.memset(gm[:], -1e30), writes=[rgm])
    for i, (Vt, rVt) in enumerate(((VA, rVA), (VB, rVB))):
        p.op("pool", lambda g_, Vt=Vt: g_.memset(Vt[:, :, 128:130], 1.0), writes=rVt)

    rr = [0]

    def nb():
        b = rr[0] % 7
        rr[0] += 1
        return b

    for tc in range(NQG):
        xi = tc % 2
        src = xT_d[tc // 2, :, (tc % 2) * 512:(tc % 2 + 1) * 512].rearrange("(kc p) t -> p kc t", p=128)
        if x_f32:
            for k4 in range(4):
                ev_ = p.dma("pool", lambda q, xi=xi, src=src, k4=k4: q.dma_start(out=xc[xi][:, k4 * 4:(k4 + 1) * 4, :],
                                                                            in_=src[:, k4 * 4:(k4 + 1) * 4, :]),
                            s_xe[xi], writes=[rxc[xi]])
        else:
            p.dma("sp", lambda q, xi=xi, src=src: q.dma_start(out=xc[xi][:], in_=src), s_x[xi], writes=[rxc[xi]])
        tsl = slice(tc * 512, (tc + 1) * 512)
        for (c0, dst, rdst, extra) in ((0, QAT, rQA, None), (128, KAT, rKA, None), (384, KBT, rKB, "km"), (256, QBT, rQB, "qf")):
            b = nb()
            for k in range(KC):
                p.op("pe", lambda t, k=k, b=b, c0=c0, xi=xi: t.matmul(P[b][:], lhsT=Wh[:, k, c0:c0 + 128], rhs=xc[xi][:, k, :],
                                                                      start=(k == 0), stop=(k == KC - 1)),
                     reads=[rWh, rxc[xi]], writes=[rP[b]])
            if extra is None:
                p.op("act", lambda a, b=b, dst=dst, tsl=tsl: a.activation(out=dst[:, tsl], in_=P[b][:], func=AF.Copy),
                     reads=[rP[b]], writes=[rdst[tc]])
            else:
                p.op("dve", lambda v, b=b, dst=dst, tsl=tsl: v.tensor_copy(out=dst[:, tsl], in_=P[b][:]),
                     reads=[rP[b]], writes=[rdst[tc]])
            if extra == "km" and "r" in flags:
                p.op("dve", lambda v, b=b, tc=tc: v.tensor_reduce(out=kmT[:, 2 * tc:2 * tc + 2],
                                                                  in_=P[b][:].rearrange("p (n k) -> p n k", n=2),
                                                                  axis=AX.X, op=ALU.add),
                     reads=[rP[b]], writes=[rkm])
            if extra == "qf" and "l" in flags:
                p.op("dve", lambda v, b=b, xi=xi, tsl=tsl: v.tensor_tensor(out=qlo[xi][:], in0=P[b][:], in1=QBT[:, tsl], op=ALU.subtract),
                     reads=[rP[b], rQB[tc]], writes=[rqf[xi]])
        for tb in range(4 if "v" in flags else 0):
            b = nb()
            blk = tc * 4 + tb
            for k in range(KC):
                p.op("pe", lambda t, k=k, b=b, tb=tb, xi=xi: t.matmul(P[b][:, 0:257], lhsT=xc[xi][:, k, tb * 128:(tb + 1) * 128],
                                                                      rhs=Wh[:, k, 512:769], start=(k == 0), stop=(k == KC - 1)),
                     reads=[rWh, rxc[xi]], writes=[rP[b]])
            if "A" in flags:
                p.op("dve", lambda v, b=b, blk=blk: v.tensor_copy(out=VA[:, blk, 0:128], in_=P[b][:, 0:128]),
                     reads=[rP[b]], writes=[rVA[tc]])
            if "B" in flags:
                p.op("dve", lambda v, b=b, blk=blk: v.tensor_copy(out=VB[:, blk, 0:128], in_=P[b][:, 129:257]),
                     reads=[rP[b]], writes=[rVB[tc]])
            if "F" in flags:
                p.op("dve", lambda v, b=b, blk=blk: v.tensor_copy(out=Fl[:, blk:blk + 1], in_=P[b][:, 128:129]),
                     reads=[rP[b]], writes=[rF])
        for s in range(4):
            qb = tc * 4 + s
            own = qb // 2
            if own == 0 or "g" not in flags:
                continue
            b = nb()
            if s in (0, 2) and "h" in flags:
                p.op("dve", lambda v: v.tensor_copy(out=kmh[:], in_=kmT[:]), reads=[rkm], writes=[rkmh])
                p.op("dve", lambda v: v.tensor_tensor(out=kml[:], in0=kmT[:], in1=kmh[:], op=ALU.subtract),
                     reads=[rkm, rkmh], writes=[rkmh])
            qsl = slice(tc * 512 + s * 128, tc * 512 + (s + 1) * 128)
            if "m" not in flags:
                continue
            p.op("pe", lambda t, b=b, qsl=qsl: t.matmul(P[b][:, 0:32], lhsT=QBT[:, qsl], rhs=kmh[:, :], start=True, stop=False),
                 reads=[rQB[tc], rkmh], writes=[rP[b]])
            p.op("pe", lambda t, b=b, qsl=qsl: t.matmul(P[b][:, 0:32], lhsT=QBT[:, qsl], rhs=kml[:, :], start=False, stop=False),
                 reads=[rQB[tc], rkmh], writes=[rP[b]])
            p.op("pe", lambda t, b=b, s=s, xi=xi: t.matmul(P[b][:, 0:32], lhsT=qlo[xi][:, s * 128:(s + 1) * 128], rhs=kmh[:, :],
                                                            start=False, stop=True),
                 reads=[rqf[xi], rkmh], writes=[rP[b]])
            if "x" not in flags:
                continue
            p.op("dve", lambda v, b=b, own=own: v.tensor_copy(out=gm[:, 0:own], in_=P[b][:, 0:own]), reads=[rP[b]], writes=[rgm])
            p.op("dve", lambda v: v.max(out=m8[:], in_=gm[:]), reads=[rgm], writes=[rm8])
            p.op("dve", lambda v, qb=qb, own=own: v.tensor_scalar(out=sel[:, qb, 0:own], in0=gm[:, 0:own], scalar1=m8[:, 2:3],
                                                                  scalar2=None, op0=ALU.is_ge),
                 reads=[rgm, rm8], writes=[rsel[qb]])

    if stop == "proj":
        ev = p.dma("sp", lambda q: q.dma_start(out=yT_d[0:128, :], in_=QAT[:, :]), None, reads=rQA + rKA + rQB + rKB + rVA + rVB + rsel + [rF])
        p.emit({"sp": [ev]})
        p.close()
        return nc
    nbf = p.sbuf("nbf", [128, 1], F32)
    rnbf = Res("nbf")
    p.op("dve", lambda v: v.tensor_scalar(out=nbf[:], in0=gb[:, 257:258], scalar1=-1.0, scalar2=None, op0=ALU.mult),
         reads=[rgb], writes=[rnbf])
    ef = p.sbuf("ef", [128, NKB], F32)
    lf = p.sbuf("lf", [128, NKB], F32)
    ref_, rlf = Res("ef"), Res("lf")
    p.op("act", lambda a: a.activation(out=ef[:], in_=Fl[:], func=AF.Exp, scale=-1.0, bias=nbf[:, 0:1]),
         reads=[rF, rnbf], writes=[ref_])
    p.op("act", lambda a: a.activation(out=lf[:], in_=ef[:], func=AF.Ln, bias=1.0, scale=1.0), reads=[ref_], writes=[rlf])
    p.op("dve", lambda v: v.tensor_scalar(out=lf[:], in0=lf[:], scalar1=-1.0, scalar2=None, op0=ALU.mult),
         reads=[rlf], writes=[rlf])
    onesf = p.sbuf("onesf", [128, 128], F32)
    ronesf = Res("onesf")
    p.op("dve", lambda v: v.memset(onesf[:], 1.0), writes=[ronesf])
    b0 = nb()
    p.op("pe", lambda t: t.matmul(P[b0][0:64, 0:1], lhsT=lf[:, :], rhs=onesf[:, 0:1], start=True, stop=True),
         reads=[rlf, ronesf], writes=[rP[b0]])
    totbc = p.sbuf("totbc", [64, 128], F32)
    rtot = Res("totbc")
    p.op("dve", lambda v: v.tensor_scalar(out=totbc[:], in0=onesf[0:64, :], scalar1=P[b0][0:64, 0:1], scalar2=None, op0=ALU.mult),
         reads=[rP[b0], ronesf], writes=[rtot])
    b1 = nb()
    p.op("pe", lambda t: t.matmul(P[b1][:, 0:NKB], lhsT=U, rhs=lf[:, :], start=True, stop=False),
         reads=[rcst, rlf], writes=[rP[b1]])
    p.op("pe", lambda t: t.matmul(P[b1][:, 0:NKB], lhsT=totbc[:, :], rhs=LST, start=False, stop=True),
         reads=[rcst, rtot], writes=[rP[b1]])
    csb = p.sbuf("csb", [128, NKB], F32)
    negc = p.sbuf("negc", [128, NKB], F32)
    rcsb, rnegc = Res("csb"), Res("negc")
    p.op("dve", lambda v: v.tensor_copy(out=csb[:], in_=P[b1][:, 0:NKB]), reads=[rP[b1]], writes=[rcsb])
    p.op("dve", lambda v: v.tensor_scalar(out=negc[:], in0=P[b1][:, 0:NKB], scalar1=-1.0, scalar2=None, op0=ALU.mult),
         reads=[rP[b1]], writes=[rnegc])
    cendc = p.sbuf("cendc", [128, NQG], F32)
    rcendc = Res("cendc")
    p.op("dve", lambda v: v.tensor_copy(out=cendc[:], in_=csb[:].rearrange("p (g f) -> p g f", f=4)[:, :, 3]),
         reads=[rcsb], writes=[rcendc])
    b2 = nb()
    p.op("pe", lambda t: t.matmul(P[b2][:, 0:NQG], lhsT=E127, rhs=cendc[:, :], start=True, stop=True),
         reads=[rcst, rcendc], writes=[rP[b2]])
    cend = p.sbuf("cend", [128, NQG], F32)
    rcend = Res("cend")
    p.op("dve", lambda v: v.tensor_copy(out=cend[:], in_=P[b2][:, 0:NQG]), reads=[rP[b2]], writes=[rcend])
    ball = p.sbuf("ball", [128, NQG, NKB], F32)
    rball = Res("ball")
    for g in range(NQG):
        p.op("dve", lambda v, g=g: v.tensor_scalar(out=ball[:, g, :], in0=negc[:, :], scalar1=cend[:, g:g + 1], scalar2=None,
                                                   op0=ALU.add), reads=[rnegc, rcend], writes=[rball])

    cx = AttnCtx(p, P, rP, identb, rident, trib, rtri, yT_d, s_out)
    SCALE = HD ** -0.5

    def qk_fox(J, g, c0):
        return [(KAT[:, J * 128:(J + 1) * 128], QAT[:, g * 512 + c0:(g + 1) * 512], [rKA[J // 4], rQA[g]])]

    def bias_fox(g, J):
        return (ball[:, g, J:J + 1], [rball])

    dense_causal_sweep(cx, qk_fox, bias_fox, VA, rVA, SCALE, 0)

    if stop == "fox":
        p.emit({"sp": cx.finals})
        p.close()
        return nc
    acc = p.sbuf("acc", [128, 4, 129], F32)
    racc = [Res(f"acc{s}") for s in range(4)]
    msteps = [(g, n, J) for g in range(NQG) for n in range(2 * g + 2) for J in (2 * n, 2 * n + 1)]
    started = {}

    def moba_a(g, n, J):
        r = J - 4 * g
        c0 = max(r, 0) * 128
        sb = cx.ksb % 3
        cx.ksb += 1
        adds = []
        for s in range(max(r, 0), 4):
            if J == 4 * g + s:
                adds.append((s, d0b, rd0))
            elif J == 4 * g + s - 1:
                adds.append((s, d1b, rd1))
        p.op("pe", lambda t, J=J, g=g, c0=c0, sb=sb, na=len(adds): t.matmul(
            P[sb][:, c0:512], lhsT=KBT[:, J * 128:(J + 1) * 128], rhs=QBT[:, g * 512 + c0:(g + 1) * 512],
            start=True, stop=(na == 0)), reads=[rKB[J // 4], rQB[g]], writes=[rP[sb]])
        for ai, (s, dt_, rdt) in enumerate(adds):
            p.op("pe", lambda t, s=s, dt_=dt_, sb=sb, ai=ai, na=len(adds): t.matmul(
                P[sb][:, s * 128:(s + 1) * 128], lhsT=identb[:], rhs=dt_[:], start=False, stop=(ai == na - 1)),
                reads=[rident, rdt], writes=[rP[sb]])
        pi = cx.kpt % 4
        cx.kpt += 1
        p.op("act", lambda a, pi=pi, sb=sb, c0=c0: a.activation(out=cx.pt[pi][:, c0:512], in_=P[sb][:, c0:512],
                                                                  func=AF.Exp, scale=SCALE),
             reads=[rP[sb]], writes=[cx.rpt[pi]])
        return pi

    def moba_b(g, n, J, pi):
        r = J - 4 * g
        yb = [3 + 2 * (n % 2), 4 + 2 * (n % 2)]
        for s in range(max(r, 0), 4):
            bi = YBNK[s]
            bank = yb[bi]
            st = not started.get((g, n, bi), False)
            started[(g, n, bi)] = True
            p.op("pe", lambda t, pi=pi, s=s, bank=bank, J=J, st=st: t.matmul(
                P[bank][:, YOFF[s]:YOFF[s] + 129], lhsT=cx.pt[pi][:, s * 128:(s + 1) * 128], rhs=VB[:, J, 0:129],
                start=st, stop=True, skip_group_check=True),
                reads=[cx.rpt[pi], rVB[J // 4]], writes=[rP[bank]])
        if J != 2 * n + 1:
            return
        smin_n = max(2 * n - 4 * g, 0)
        for s in range(smin_n, 4):
            qb = 4 * g + s
            own = qb // 2
            bank = yb[YBNK[s]]
            src = P[bank][:, YOFF[s]:YOFF[s] + 129]
            if n == own:
                if n == 0:
                    p.op("dve", lambda v, s=s, src=src: v.tensor_copy(out=acc[:, s, :], in_=src),
                         reads=[rP[bank]], writes=[racc[s]])
                else:
                    p.op("dve", lambda v, s=s, src=src: v.tensor_tensor(out=acc[:, s, :], in0=acc[:, s, :], in1=src, op=ALU.add),
                         reads=[rP[bank], racc[s]], writes=[racc[s]])
                cx.finish_block(acc[:, s, 0:128], acc[:, s, 128:129], [racc[s]], s, 128, g, s == 3)
            elif n == 0:
                p.op("dve", lambda v, s=s, src=src, qb=qb: v.tensor_scalar(out=acc[:, s, :], in0=src, scalar1=sel[:, qb, 0:1],
                                                                          scalar2=None, op0=ALU.mult),
                     reads=[rP[bank], rsel[qb]], writes=[racc[s]])
            else:
                p.op("dve", lambda v, s=s, src=src, qb=qb, n=n: v.scalar_tensor_tensor(
                    out=acc[:, s, :], in0=src, scalar=sel[:, qb, n:n + 1], in1=acc[:, s, :], op0=ALU.mult, op1=ALU.add),
                    reads=[rP[bank], rsel[qb], racc[s]], writes=[racc[s]])

    mp = [moba_a(*msteps[0]), moba_a(*msteps[1])]
    for i, st_ in enumerate(msteps):
        if i + 2 < len(msteps):
            mp.append(moba_a(*msteps[i + 2]))
        moba_b(*st_, mp[i])
        cx.tick()
    cx.tick(flush=True)

    p.emit({"sp": cx.finals})
    p.close()
    return nc


def _t5_bucket(rel):
    n = np.maximum(rel, 0)
    nf = np.maximum(n, 1).astype(np.float32)
    large = 16 + (np.log(nf / np.float32(16)) / np.float32(math.log(128 / 16)) * np.float32(16)).astype(np.int32)
    large = np.minimum(large, 31)
    return np.where(n < 16, n, large)


def consts_AE():
    j = np.arange(128)
    cst = np.zeros((128, 576), np.float32)
    cst[:, 0:128] = (j[:, None] <= j[None, :])
    b = np.arange(64)
    cst[0:64, 128:192] = (b[:, None] < b[None, :])
    cst[127, 192:320] = 1.0
    cst[:, 320:448] = np.eye(128)
    cst[:, 448:576] = np.where(j[:, None] <= j[None, :], 0.0, NEG)
    return cst


def gb_AE(rel_bias, b_f, head):
    j = np.arange(128)
    rel0 = j[None, :] - j[:, None]
    gbm = np.zeros((128, 258), np.float32)
    col = rel_bias[:, head]
    gbm[:, 0:128] = col[_t5_bucket(rel0)]
    gbm[:, 128:256] = col[_t5_bucket(128 + rel0)]
    gbm[:, 256] = col[31]
    gbm[:, 257] = b_f[head]
    return gbm


def build_AO():
    nc = bass.Bass("TRN2", target_bir_lowering=False)
    dt = nc.dram_tensor
    lat_d = dt("lat", [NCORES, MLA_IN, TOK], BF16, kind="ExternalInput").ap()
    wuq_d = dt("wuq", [512, 384], F32, kind="ExternalInput").ap()
    wukv_d = dt("wukv", [512, 512], F32, kind="ExternalInput").ap()
    cs_d = dt("cs", [64, 2 * S], F32, kind="ExternalInput").ap()
    cst_d = dt("cst", [128, 576], F32, kind="ExternalInput").ap()
    yT_d = dt("yT", [256, S], BF16, kind="ExternalOutput").ap()

    p = Prog(nc)
    cst = p.sbuf("cst", [128, 576], F32)
    rcst = Res("cst")
    p.dma("sp", lambda q: q.dma_start(out=cst[:], in_=cst_d[:, :]), None, writes=[rcst])
    identb = p.sbuf("identb", [128, 128], BF16)
    trib = p.sbuf("trib", [128, 128], BF16)
    rident, rtri = Res("identb"), Res("trib")
    p.op("dve", lambda v: v.tensor_copy(out=identb[:], in_=cst[:, 320:448]), reads=[rcst], writes=[rident])
    p.op("dve", lambda v: v.tensor_copy(out=trib[:], in_=cst[:, 448:576]), reads=[rcst], writes=[rtri])
    Wq = p.sbuf("Wq", [128, 4, 384], BF16)
    Wkv = p.sbuf("Wkv", [128, 4, 512], BF16)
    Wrot = p.sbuf("Wrot", [128, 2, 4, 64], BF16)
    rWq, rWkv, rWrot = Res("Wq"), Res("Wkv"), Res("Wrot")
    s_w = p.new_dma_sem("s_w", exclusive=True)
    p.dma("pool", lambda q: q.dma_start(out=Wq[:], in_=wuq_d.rearrange("(kc p) n -> p kc n", p=128)), s_w, writes=[rWq])
    s_w2 = p.new_dma_sem("s_w2", exclusive=True)
    p.dma("pool", lambda q: q.dma_start(out=Wkv[:], in_=wukv_d.rearrange("(kc p) n -> p kc n", p=128)), s_w2, writes=[rWkv])
    for h in range(2):
        r0 = h * 192 + 128
        p.op("dve", lambda v, h=h, r0=r0: v.tensor_scalar(out=Wrot[:, h, :, 0:32], in0=Wq[:, :, r0 + 32:r0 + 64], scalar1=-1.0,
                                                          scalar2=None, op0=ALU.mult), reads=[rWq], writes=[rWrot])
        p.op("dve", lambda v, h=h, r0=r0: v.tensor_copy(out=Wrot[:, h, :, 32:64], in_=Wq[:, :, r0:r0 + 32]), reads=[rWq], writes=[rWrot])
    QN = p.sbuf("QN", [128, S], BF16)
    KN = p.sbuf("KN", [128, S], BF16)
    QR = p.sbuf("QR", [64, S], BF16)
    KR = p.sbuf("KR", [64, S], BF16)
    V = p.sbuf("V", [128, NKB, 130], BF16)
    rQN = [Res(f"QN{i}") for i in range(NQG)]
    rKN = [Res(f"KN{i}") for i in range(NQG)]
    rQR = [Res(f"QR{i}") for i in range(NQG)]
    rKR = [Res(f"KR{i}") for i in range(NQG)]
    rV = [Res(f"V{i}") for i in range(NQG)]
    latc = [p.sbuf(f"latc{i}", [128, 8, 512], BF16) for i in range(2)]
    rlatc = [Res(f"latc{i}") for i in range(2)]
    csc = [p.sbuf(f"csc{i}", [64, 2, 512], F32) for i in range(2)]
    rcsc = [Res(f"csc{i}") for i in range(2)]
    t0 = p.sbuf("t0", [64, 512], F32)
    t1 = p.sbuf("t1", [64, 512], F32)
    rt0, rt1 = Res("t0"), Res("t1")
    P = [p.psum(f"P{i}", [128, 512], F32) for i in range(7)]
    rP = [Res(f"P{i}") for i in range(7)]
    p.op("pool", lambda g_: g_.memset(V[:, :, 128:130], 1.0), writes=rV)
    cx = AttnCtx(p, P, rP, identb, rident, trib, rtri, yT_d, None)
    SCALE = 192.0 ** -0.5
    rr = [0]

    def nb():
        b = rr[0] % 3
        rr[0] += 1
        return b

    kq = 0
    for h in range(2):
        for tc in range(NQG):
            xi = kq % 2
            kq += 1
            r_, c_ = tc // 2, (tc % 2) * 512
            tsl = slice(tc * 512, (tc + 1) * 512)
            p.dma("sp", lambda q, xi=xi, r_=r_, c_=c_: q.dma_start(
                out=latc[xi][:], in_=lat_d[r_, 0:1024, c_:c_ + 512].rearrange("(c p) t -> p c t", p=128)), None,
                writes=[rlatc[xi]])
            p.dma("sp", lambda q, xi=xi, tsl=tsl: q.dma_start(out=csc[xi][:, 0, :], in_=cs_d[:, tsl]), None, writes=[rcsc[xi]])
            p.dma("sp", lambda q, xi=xi, tc=tc: q.dma_start(out=csc[xi][:, 1, :], in_=cs_d[:, S + tc * 512:S + (tc + 1) * 512]), None,
                  writes=[rcsc[xi]])
            if h == 0:
                p.dma("sp", lambda q, r_=r_, c_=c_, tsl=tsl: q.dma_start(out=KR[:, tsl], in_=lat_d[r_, 1024:1088, c_:c_ + 512]), None,
                      writes=[rKR[tc]])
            for (W, rW, c0, kof, dst, rdst) in ((Wq, rWq, h * 192, 0, QN, rQN), (Wkv, rWkv, h * 256, 4, KN, rKN)):
                b = nb()
                for k in range(4):
                    p.op("pe", lambda t, k=k, b=b, W=W, c0=c0, kof=kof, xi=xi: t.matmul(
                        P[b][:], lhsT=W[:, k, c0:c0 + 128], rhs=latc[xi][:, kof + k, :], start=(k == 0), stop=(k == 3)),
                        reads=[rW, rlatc[xi]], writes=[rP[b]])
                p.op("act", lambda a, b=b, dst=dst, tsl=tsl: a.activation(out=dst[:, tsl], in_=P[b][:], func=AF.Copy),
                     reads=[rP[b]], writes=[rdst[tc]])
            b0, b1 = nb(), nb()
            for k in range(4):
                p.op("pe", lambda t, k=k, b0=b0, h=h, xi=xi: t.matmul(P[b0][0:64, :], lhsT=Wq[:, k, h * 192 + 128:h * 192 + 192],
                                                                       rhs=latc[xi][:, k, :], start=(k == 0), stop=(k == 3)),
                     reads=[rWq, rlatc[xi]], writes=[rP[b0]])
            for k in range(4):
                p.op("pe", lambda t, k=k, b1=b1, h=h, xi=xi: t.matmul(P[b1][0:64, :], lhsT=Wrot[:, h, k, :],
                                                                       rhs=latc[xi][:, k, :], start=(k == 0), stop=(k == 3)),
                     reads=[rWrot, rlatc[xi]], writes=[rP[b1]])
            p.op("dve", lambda v, b0=b0, xi=xi: v.tensor_tensor(out=t0[:], in0=P[b0][0:64, :], in1=csc[xi][:, 0, :], op=ALU.mult),
                 reads=[rP[b0], rcsc[xi]], writes=[rt0])
            p.op("dve", lambda v, b1=b1, xi=xi: v.tensor_tensor(out=t1[:], in0=P[b1][0:64, :], in1=csc[xi][:, 1, :], op=ALU.mult),
                 reads=[rP[b1], rcsc[xi]], writes=[rt1])
            p.op("dve", lambda v, tsl=tsl: v.tensor_tensor(out=QR[:, tsl], in0=t0[:], in1=t1[:], op=ALU.add),
                 reads=[rt0, rt1], writes=[rQR[tc]])
            for tb in range(4):
                b = nb()
                blk = tc * 4 + tb
                for k in range(4):
                    p.op("pe", lambda t, k=k, b=b, tb=tb, h=h, xi=xi: t.matmul(
                        P[b][:, 0:128], lhsT=latc[xi][:, 4 + k, tb * 128:(tb + 1) * 128], rhs=Wkv[:, k, h * 256 + 128:h * 256 + 256],
                        start=(k == 0), stop=(k == 3)), reads=[rWkv, rlatc[xi]], writes=[rP[b]])
                p.op("dve", lambda v, b=b, blk=blk: v.tensor_copy(out=V[:, blk, 0:128], in_=P[b][:, 0:128]),
                     reads=[rP[b]], writes=[rV[tc]])

        def qk_mla(J, g, c0):
            return [(KN[:, J * 128:(J + 1) * 128], QN[:, g * 512 + c0:(g + 1) * 512], [rKN[J // 4], rQN[g]]),
                    (KR[:, J * 128:(J + 1) * 128], QR[:, g * 512 + c0:(g + 1) * 512], [rKR[J // 4], rQR[g]])]

        dense_causal_sweep(cx, qk_mla, None, V, rV, SCALE, h * 128)

    p.emit({"sp": cx.finals})
    p.close()
    return nc


_NC_CACHE = {}


def _prog(name, fn):
    if name not in _NC_CACHE:
        _NC_CACHE[name] = fn()
    return _NC_CACHE[name]


def _lay128(v):
    return np.ascontiguousarray(v.reshape(-1, 128).T)


def _run(nc, in_maps):
    res = run_bass_kernel_spmd(nc, in_maps, core_ids=list(range(NCORES)))
    return res.results


def kernel(x, ab_w_in, ab_forget_bias, ab_w_out, rel_bias, mla_w_in, mla_q_norm, mla_kv_norm, mla_w_uq,
           mla_w_ukv, mla_w_out, ffn_w_gate, ffn_w_up, ffn_w_down, ln_g, ln_b):
    f32 = np.float32
    x = np.asarray(x, f32)
    xT = np.ascontiguousarray(x[0].T)
    xres = [np.ascontiguousarray(xT[:, c * TOK:(c + 1) * TOK]) for c in range(NCORES)]
    cstAE = consts_AE()
    inv = (np.float32(10000.0) ** (-np.arange(0, 64, 2, dtype=f32) / np.float32(64))).astype(f32)
    ang = np.arange(S, dtype=f32)[:, None] * inv[None, :]
    cos, sin = np.cos(ang).astype(f32), np.sin(ang).astype(f32)
    csT = np.ascontiguousarray(np.concatenate([np.concatenate([cos, cos], 1).T, np.concatenate([sin, sin], 1).T], axis=1))

    gathered = None
    for layer in range(DEPTH):
        j = layer // 2
        if layer % 2 == 0:
            if layer == 0:
                nc = _prog("AE32", lambda: build_AE(x_f32=True))
                gathered = np.ascontiguousarray(xT.reshape(D, NCORES, TOK).transpose(1, 0, 2))
            else:
                nc = _prog("AE16", lambda: build_AE(x_f32=False))
            w = np.asarray(ab_w_in[j], f32)
            maps = []
            for c in range(NCORES):
                cols = [w[:, c * 128:(c + 1) * 128], w[:, 1024 + c * 128:1024 + (c + 1) * 128],
                        w[:, 3080 + c * 128:3080 + (c + 1) * 128], w[:, 4104 + c * 128:4104 + (c + 1) * 128],
                        w[:, 2048 + c * 128:2048 + (c + 1) * 128], w[:, 3072 + c:3072 + c + 1],
                        w[:, 5128 + c * 128:5128 + (c + 1) * 128], np.zeros((D, 1), f32)]
                maps.append({"xT": gathered, "wh": np.ascontiguousarray(np.concatenate(cols, axis=1)), "cst": cstAE,
                             "gb": gb_AE(np.asarray(rel_bias, f32), np.asarray(ab_forget_bias[j], f32), c)})
            outs = _run(nc, maps)
            yT_full = np.concatenate([o["yT"][0:128] for o in outs] + [o["yT"][128:256] for o in outs], axis=0)
            w_out = np.asarray(ab_w_out[j], f32)
        else:
            nc = _prog("AO", build_AO)
            wq = np.asarray(mla_w_uq[j], f32)
            wkv = np.asarray(mla_w_ukv[j], f32)
            maps = []
            for c in range(NCORES):
                maps.append({"lat": gathered, "wuq": np.ascontiguousarray(wq[:, c * 384:(c + 1) * 384]),
                             "wukv": np.ascontiguousarray(wkv[:, c * 512:(c + 1) * 512]), "cs": csT, "cst": cstAE})
            outs = _run(nc, maps)
            yT_full = np.concatenate([o["yT"] for o in outs], axis=0)
            w_out = np.asarray(mla_w_out[j], f32)
        mode = "final" if layer == DEPTH - 1 else ("mla" if (layer + 1) % 2 == 1 else "even")
        nc = _prog("T" + mode, lambda: build_T(mode))
        lnp = np.ascontiguousarray(np.concatenate([_lay128(np.asarray(ln_g[layer, 0], f32)), _lay128(np.asarray(ln_b[layer, 0], f32)),
                                                   _lay128(np.asarray(ln_g[layer, 1], f32)), _lay128(np.asarray(ln_b[layer, 1], f32))], axis=1))
        maps = []
        for c in range(NCORES):
            m = {"yT": np.ascontiguousarray(yT_full[:, c * TOK:(c + 1) * TOK]), "xres": xres[c], "wout": w_out, "lnp": lnp,
                 "wg": np.asarray(ffn_w_gate[layer], f32), "wu": np.asarray(ffn_w_up[layer], f32),
                 "wd": np.asarray(ffn_w_down[layer], f32)}
            if mode == "mla":
                jn = (layer + 1) // 2
                m["win"] = np.asarray(mla_w_in[jn], f32)
                m["ng"] = np.ascontiguousarray(np.concatenate([_lay128(np.asarray(mla_q_norm[jn], f32)),
                                                               _lay128(np.asarray(mla_kv_norm[jn], f32))], axis=1))
                m["cs"] = np.ascontiguousarray(np.concatenate([csT[:, c * TOK:(c + 1) * TOK], csT[:, S + c * TOK:S + (c + 1) * TOK]], axis=1))
            maps.append(m)
        outs = _run(nc, maps)
        xres = [o["xout"] for o in outs]
        if mode == "mla":
            gathered = np.ascontiguousarray(np.stack([o["lat"] for o in outs], axis=0))
        elif mode == "even":
            gathered = np.ascontiguousarray(np.stack([o["xTb"] for o in outs], axis=0))
    out = np.concatenate(xres, axis=1)
    return np.ascontiguousarray(out.T)[None].astype(np.float32)
```
